# Optimizing a Trainium2 kernel written in Bass

```python
import math
import jax, jax.numpy as jnp
from jax import lax
import numpy as np

D_MODEL = 1024
BATCH = 8
SEQ = 8192
DEPTH = 2

CHUNK = 64
N_MIXERS = 2
EPS = 1e-6

RET_HEADS = 4
RET_QK_DIM = D_MODEL // RET_HEADS
RET_V_DIM = 2 * D_MODEL // RET_HEADS
RET_QK_WIDTH = RET_HEADS * RET_QK_DIM
RET_WIDTH = RET_HEADS * RET_V_DIM
ROPE_BASE = 10000.0

ATT_HEADS = 16
ATT_HEAD_DIM = 2 * D_MODEL // ATT_HEADS
ATT_WIDTH = ATT_HEADS * ATT_HEAD_DIM
LEFT_CHUNKS = 8
BAND = (LEFT_CHUNKS + 1) * CHUNK
MAX_REL = 2 * CHUNK

N_RET_LAYERS = (DEPTH + 1) // 2
N_ATT_LAYERS = DEPTH // 2

kernel_name = "hybrid_retention_chunked_attention_adaln"


def rms_norm_f32(t, gain):
    t32 = t.astype(jnp.float32)
    return t32 * lax.rsqrt(jnp.mean(t32 * t32, axis=-1, keepdims=True) + EPS) * gain.astype(jnp.float32)


def apply_rope(t, cos, sin):
    t1, t2 = jnp.split(t, 2, axis=-1)
    return jnp.concatenate([t1 * cos - t2 * sin, t1 * sin + t2 * cos], axis=-1)


def retention_mixer(h, positions, w_in, gn_g, w_out):
    B, S, _ = h.shape
    nc = S // CHUNK
    proj = h @ w_in
    q, k, v, g = jnp.split(proj, [RET_QK_WIDTH, 2 * RET_QK_WIDTH, 2 * RET_QK_WIDTH + RET_WIDTH], axis=-1)
    q = q.reshape(B, S, RET_HEADS, RET_QK_DIM).astype(jnp.float32)
    k = k.reshape(B, S, RET_HEADS, RET_QK_DIM).astype(jnp.float32)
    v = v.reshape(B, S, RET_HEADS, RET_V_DIM).astype(jnp.float32)

    inv_freq = 1.0 / (ROPE_BASE ** (jnp.arange(0, RET_QK_DIM, 2, dtype=jnp.float32) / RET_QK_DIM))
    ang = positions.astype(jnp.float32)[..., None] * inv_freq
    cos, sin = jnp.cos(ang)[:, :, None, :], jnp.sin(ang)[:, :, None, :]
    q = apply_rope(q, cos, sin)
    k = apply_rope(k, cos, sin) * (RET_QK_DIM ** -0.5)

    def to_chunks(t):
        return t.reshape(B, nc, CHUNK, RET_HEADS, t.shape[-1]).transpose(0, 3, 1, 2, 4)

    qc, kc, vc = to_chunks(q), to_chunks(k), to_chunks(v)

    log_gamma = jnp.log(1.0 - 2.0 ** (-5.0 - jnp.arange(RET_HEADS, dtype=jnp.float32)))
    idx = jnp.arange(CHUNK, dtype=jnp.float32)
    intra_decay = jnp.exp(log_gamma[:, None, None] * jnp.abs(idx[:, None] - idx[None, :]))
    q_decay = jnp.exp(log_gamma[:, None] * (idx + 1.0))
    k_decay = jnp.exp(log_gamma[:, None] * (CHUNK - 1.0 - idx))
    chunk_decay = jnp.exp(log_gamma * CHUNK)

    scores = jnp.einsum('bhncd,bhnkd->bhnck', qc, kc) * intra_decay[:, None]
    o_intra = jnp.einsum('bhnck,bhnke->bhnce', scores, vc)

    def step(state, xs):
        q_j, k_j, v_j = xs
        o_j = jnp.einsum('bhcd,bhde->bhce', q_j * q_decay[:, :, None], state)
        state = chunk_decay[:, None, None] * state + jnp.einsum(
            'bhcd,bhce->bhde', k_j * k_decay[:, :, None], v_j)
        return state, o_j

    xs = (qc.transpose(2, 0, 1, 3, 4), kc.transpose(2, 0, 1, 3, 4), vc.transpose(2, 0, 1, 3, 4))
    state0 = jnp.zeros((B, RET_HEADS, RET_QK_DIM, RET_V_DIM), jnp.float32)
    _, o_inter = lax.scan(step, state0, xs)

    o = o_intra + o_inter.transpose(1, 2, 0, 3, 4)
    o = o.transpose(0, 2, 3, 1, 4).reshape(B, S, RET_HEADS, RET_V_DIM)
    o = rms_norm_f32(o, gn_g.reshape(RET_HEADS, RET_V_DIM))
    o = o.reshape(B, S, RET_WIDTH) * jax.nn.silu(g.astype(jnp.float32))
    return o.astype(h.dtype) @ w_out


def chunked_attention_mixer(h, w_in, q_g, k_g, rel_table, w_out):
    B, S, _ = h.shape
    nc = S // CHUNK
    proj = h @ w_in
    q, k, v, g = jnp.split(proj, 4, axis=-1)
    q = rms_norm_f32(q.reshape(B, S, ATT_HEADS, ATT_HEAD_DIM), q_g)
    k = rms_norm_f32(k.reshape(B, S, ATT_HEADS, ATT_HEAD_DIM), k_g)
    v = v.reshape(B, S, ATT_HEADS, ATT_HEAD_DIM).astype(jnp.float32)

    pad = LEFT_CHUNKS * CHUNK
    k_pad = jnp.pad(k, ((0, 0), (pad, 0), (0, 0), (0, 0)))
    v_pad = jnp.pad(v, ((0, 0), (pad, 0), (0, 0), (0, 0)))

    qi = jnp.arange(CHUNK)
    kb = jnp.arange(BAND)
    rel = qi[:, None] + pad - kb[None, :]
    bias = rel_table.astype(jnp.float32)[:, jnp.clip(rel, -MAX_REL, MAX_REL) + MAX_REL]
    scale = ATT_HEAD_DIM ** -0.5

    q_chunks = q.reshape(B, nc, CHUNK, ATT_HEADS, ATT_HEAD_DIM).transpose(1, 0, 2, 3, 4)

    def one_chunk(args):
        j, q_j = args
        k_j = lax.dynamic_slice_in_dim(k_pad, j * CHUNK, BAND, axis=1)
        v_j = lax.dynamic_slice_in_dim(v_pad, j * CHUNK, BAND, axis=1)
        s = jnp.einsum('bqhd,bkhd->bhqk', q_j, k_j) * scale + bias
        valid = kb >= (LEFT_CHUNKS - j) * CHUNK
        s = jnp.where(valid, s, jnp.float32(-1e30))
        p = jax.nn.softmax(s, axis=-1)
        return jnp.einsum('bhqk,bkhd->bqhd', p, v_j)

    o = lax.map(one_chunk, (jnp.arange(nc), q_chunks))
    o = o.transpose(1, 0, 2, 3, 4).reshape(B, S, ATT_WIDTH)
    o = o * jax.nn.silu(g.astype(jnp.float32))
    return o.astype(h.dtype) @ w_out


def setup_inputs(seed: int = 0) -> dict:
    key = jax.random.key(seed)
    ks = jax.random.split(key, 16)
    D = D_MODEL
    x = jax.random.normal(ks[0], (BATCH, SEQ, D), jnp.float32)
    c = jax.random.normal(ks[1], (BATCH, D), jnp.float32)
    offsets = jax.random.randint(ks[2], (BATCH, 1), 0, 4096, dtype=jnp.int32)
    positions = offsets + jnp.arange(SEQ, dtype=jnp.int32)[None, :]
    norm_g = 1.0 + 0.1 * jax.random.normal(ks[3], (DEPTH, D), jnp.float32)
    ada_w = jax.random.normal(ks[4], (DEPTH, D, 3 * D), jnp.float32) * D ** -0.5
    ada_b = 0.02 * jax.random.normal(ks[5], (DEPTH, 3 * D), jnp.float32)
    ret_w_in = jax.random.normal(ks[6], (N_RET_LAYERS, D, 2 * RET_QK_WIDTH + 2 * RET_WIDTH), jnp.float32) * D ** -0.5
    ret_gn_g = 1.0 + 0.1 * jax.random.normal(ks[7], (N_RET_LAYERS, RET_WIDTH), jnp.float32)
    ret_w_out = jax.random.normal(ks[8], (N_RET_LAYERS, RET_WIDTH, D), jnp.float32) * RET_WIDTH ** -0.5
    att_w_in = jax.random.normal(ks[9], (N_ATT_LAYERS, D, 4 * ATT_WIDTH), jnp.float32) * D ** -0.5
    att_q_g = 1.0 + 0.1 * jax.random.normal(ks[10], (N_ATT_LAYERS, ATT_HEAD_DIM), jnp.float32)
    att_k_g = 1.0 + 0.1 * jax.random.normal(ks[11], (N_ATT_LAYERS, ATT_HEAD_DIM), jnp.float32)
    att_rel_bias = 0.5 * jax.random.normal(ks[12], (N_ATT_LAYERS, ATT_HEADS, 2 * MAX_REL + 1), jnp.float32)
    att_w_out = jax.random.normal(ks[13], (N_ATT_LAYERS, ATT_WIDTH, D), jnp.float32) * ATT_WIDTH ** -0.5
    return {"x": x, "c": c, "positions": positions, "norm_g": norm_g, "ada_w": ada_w,
            "ada_b": ada_b, "ret_w_in": ret_w_in, "ret_gn_g": ret_gn_g, "ret_w_out": ret_w_out,
            "att_w_in": att_w_in, "att_q_g": att_q_g, "att_k_g": att_k_g,
            "att_rel_bias": att_rel_bias, "att_w_out": att_w_out}


def reference(x, c, positions, norm_g, ada_w, ada_b, ret_w_in, ret_gn_g, ret_w_out,
              att_w_in, att_q_g, att_k_g, att_rel_bias, att_w_out):
    cond = jax.nn.silu(c.astype(jnp.float32))
    for i in range(DEPTH):
        mod = cond @ ada_w[i].astype(jnp.float32) + ada_b[i].astype(jnp.float32)
        shift, scale, gate = jnp.split(mod[:, None, :], 3, axis=-1)
        h = (rms_norm_f32(x, norm_g[i]) * (1.0 + scale) + shift).astype(x.dtype)
        j = i // N_MIXERS
        if i % N_MIXERS == 0:
            out = retention_mixer(h, positions, ret_w_in[j], ret_gn_g[j], ret_w_out[j])
        else:
            out = chunked_attention_mixer(h, att_w_in[j], att_q_g[j], att_k_g[j],
                                          att_rel_bias[j], att_w_out[j])
        x = x + (gate * out.astype(jnp.float32)).astype(x.dtype)
    return x
```

```python
import math
import os
from contextlib import ExitStack, contextmanager

import numpy as np
import ml_dtypes
import concourse.bass as bass
import concourse.mybir as mybir
from concourse.bass_utils import run_bass_kernel_spmd

F32 = mybir.dt.float32
BF16 = mybir.dt.bfloat16
I32 = mybir.dt.int32
AF = mybir.ActivationFunctionType
ALU = mybir.AluOpType

ENGINES = ("pe", "act", "dve", "pool", "sp")
NDSEM = 6


class TBuf:
    __slots__ = ("name", "t", "w", "r", "wd")

    def __init__(self, name, t=None):
        self.name = name
        self.t = t
        self.w = None
        self.r = {}
        self.wd = {}


class _Op:
    __slots__ = ("eng", "fn", "idx", "deps", "kind", "dsem")


class Prog:
    def __init__(self, nc):
        self.nc = nc
        self.ops = {e: [] for e in ENGINES}
        self.ndma = {e: 0 for e in ENGINES}
        self.finals = []
        self.stack = None
        self.phase = 0
        self.base = {e: 0 for e in ENGINES}

    @contextmanager
    def ctx(self):
        with ExitStack() as st:
            self.stack = st
            self.csem = {e: st.enter_context(self.nc.semaphore("cs_" + e)) for e in ENGINES}
            self.dsem = {
                e: [st.enter_context(self.nc.semaphore("ds_%s%d" % (e, i))) for i in range(NDSEM)]
                for e in ("sp", "pool", "act")
            }
            yield self
            self.emit()

    @contextmanager
    def scope(self):
        self.emit()
        outer = self.stack
        with ExitStack() as st:
            self.stack = st
            yield self
            self.emit()
        self.stack = outer

    def sb(self, name, shape, dt):
        self.uid = getattr(self, "uid", 0) + 1
        t = self.stack.enter_context(self.nc.sbuf_tensor("sb%d_%s" % (self.uid, name), list(shape), dt))
        return TBuf(name, t)

    def ps(self, name, shape, dt):
        t = self.stack.enter_context(self.nc.psum_tensor("ps_" + name, list(shape), dt))
        return TBuf(name, t)

    def view(self, buf, name):
        return TBuf(name, buf.t)

    def _collect(self, eng, reads, writes):
        deps = []
        for b in reads:
            if b.w is not None:
                deps.append(b.w)
            deps.extend(b.wd.values())
        for b in writes:
            if b.w is not None:
                deps.append(b.w)
            deps.extend(b.wd.values())
            deps.extend(b.r.values())
        ph = self.phase
        deps = [d for d in deps if not (d[0] == "c" and d[3] != ph)]
        if eng == "pe":
            deps = [d for d in deps if not (d[0] == "c" and d[1] == "pe")]
        return deps

    def _commit(self, reads, writes, tk):
        key = tk[:2] if tk[0] == "c" else tk[:3]
        for b in reads:
            b.r[key] = tk
        for b in writes:
            b.w = tk
            b.r = {}
            if tk[0] == "d":
                b.wd[key] = tk
            else:
                b.wd = {}

    def op(self, eng, fn, reads=(), writes=(), kind="c"):
        o = _Op()
        o.eng, o.fn, o.idx, o.kind = eng, fn, len(self.ops[eng]), kind
        o.deps = self._collect(eng, reads, writes)
        tk = ("c", eng, o.idx, self.phase)
        self._commit(reads, writes, tk)
        self.ops[eng].append(o)
        return tk

    def dma(self, q, out_ap, in_ap, reads=(), writes=(), final=False, **kw):
        j = self.ndma[q]
        self.ndma[q] += 1
        si, val = j % NDSEM, 16 * (j // NDSEM + 1)
        o = _Op()
        o.eng, o.idx, o.kind, o.dsem = q, len(self.ops[q]), "d", si
        o.fn = lambda e: e.dma_start(out=out_ap, in_=in_ap, **kw)
        o.deps = self._collect(q, reads, writes)
        if val > 16:
            o.deps.append(("d", q, si, val - 16))
        tk = ("d", q, si, val)
        self._commit(reads, writes, tk)
        self.ops[q].append(o)
        if final:
            self.finals.append(tk)
        return tk

    def emit(self):
        nc = self.nc
        if not any(self.ops[e] for e in ENGINES):
            return
        ph = self.phase
        deps = list(self.finals)
        self.finals = []
        for e in ENGINES:
            if e != "sp" and self.ops[e]:
                deps.append(("c", e, len(self.ops[e]) - 1, ph))
        for q in self.dsem:
            for si in range(NDSEM):
                n = (self.ndma[q] - si + NDSEM - 1) // NDSEM
                if n > 0:
                    deps.append(("d", q, si, 16 * n))
        o = _Op()
        o.eng, o.idx, o.kind, o.fn, o.deps = "sp", len(self.ops["sp"]), "w", None, deps
        self.ops["sp"].append(o)
        o = _Op()
        o.eng, o.idx, o.kind, o.fn, o.deps = "sp", len(self.ops["sp"]), "s", None, []
        self.ops["sp"].append(o)
        rel = ("c", "sp", o.idx, ph)
        for e in ENGINES:
            if e != "sp":
                o = _Op()
                o.eng, o.idx, o.kind, o.fn, o.deps = e, len(self.ops[e]), "w", None, [rel]
                self.ops[e].append(o)

        sig = {e: set() for e in ENGINES}
        for e in ENGINES:
            for o in self.ops[e]:
                for d in o.deps:
                    if d[0] == "c":
                        sig[d[1]].add(d[2])
        rank = {e: {idx: self.base[e] + r + 1 for r, idx in enumerate(sorted(sig[e]))} for e in ENGINES}
        ops = self.ops

        def run(e, eng):
            waited = {}
            for o in ops[e]:
                need = {}
                for d in o.deps:
                    if d[0] == "c":
                        k, sem, val = ("c", d[1]), self.csem[d[1]], rank[d[1]][d[2]]
                    else:
                        k, sem, val = ("d", d[1], d[2]), self.dsem[d[1]][d[2]], d[3]
                    if need.get(k, (None, 0))[1] < val:
                        need[k] = (sem, val)
                for k, (sem, val) in need.items():
                    if waited.get(k, 0) >= val:
                        continue
                    eng.wait_ge(sem, val)
                    waited[k] = val
                if os.environ.get("PDBG"):
                    print("PDBG", self.phase, e, o.idx, o.kind, sorted((str(k), v) for k, (_, v) in need.items()),
                          "SIG=%d" % rank[e][o.idx] if o.idx in sig[e] else "")
                if o.kind == "w":
                    continue
                if o.kind == "s":
                    eng.sem_inc(self.csem[e], 1)
                    continue
                ins = o.fn(eng)
                if o.kind == "d":
                    ins.then_inc(self.dsem[e][o.dsem], 16)
                elif o.idx in sig[e]:
                    ins.then_inc(self.csem[e], 1)

        with nc.Block() as block:
            @block.tensor
            def _(eng):
                run("pe", eng)

            @block.scalar
            def _(eng):
                run("act", eng)

            @block.vector
            def _(eng):
                run("dve", eng)

            @block.gpsimd
            def _(eng):
                run("pool", eng)

            @block.sync
            def _(eng):
                run("sp", eng)

        for e in ENGINES:
            self.base[e] += len(sig[e])
        self.ops = {e: [] for e in ENGINES}
        self.phase += 1


def bcast_rows(ap, n):
    dims = [list(d) for d in ap.ap]
    if len(dims) >= 2 and dims[0][1] == 1:
        dims = dims[1:]
    return bass.AP(ap.tensor, ap.offset, [[0, n]] + dims)


TWO_PI = 2.0 * math.pi
CW1 = 6.28125
CW2 = float(np.float32(TWO_PI - CW1))
CW3 = float(TWO_PI - CW1 - float(np.float32(TWO_PI - CW1)))
PI_LO = 3.1415925


def emit_sincos(P, ang, sc, ki, kf, red, redc, eng="dve"):
    Fd = ang.t.shape[1]
    P.op(eng, lambda e: e.tensor_scalar(ki.t[:], ang.t[:], 1.0 / TWO_PI, None, op0=ALU.mult),
         reads=[ang], writes=[ki])
    P.op(eng, lambda e: e.tensor_copy(kf.t[:], ki.t[:]), reads=[ki], writes=[kf])
    P.op(eng, lambda e: e.scalar_tensor_tensor(red.t[:], kf.t[:], -CW1, ang.t[:], op0=ALU.mult, op1=ALU.add),
         reads=[kf, ang], writes=[red])
    P.op(eng, lambda e: e.scalar_tensor_tensor(red.t[:], kf.t[:], -CW2, red.t[:], op0=ALU.mult, op1=ALU.add),
         reads=[kf, red], writes=[red])
    P.op(eng, lambda e: e.scalar_tensor_tensor(red.t[:], kf.t[:], -CW3, red.t[:], op0=ALU.mult, op1=ALU.add),
         reads=[kf, red], writes=[red])
    P.op(eng, lambda e: e.tensor_scalar(kf.t[:], red.t[:], math.pi, -TWO_PI, op0=ALU.is_gt, op1=ALU.mult),
         reads=[red], writes=[kf])
    P.op(eng, lambda e: e.tensor_tensor(red.t[:], red.t[:], kf.t[:], op=ALU.add), reads=[red, kf], writes=[red])
    P.op(eng, lambda e: e.tensor_scalar(kf.t[:], red.t[:], math.pi / 2, -TWO_PI, op0=ALU.is_gt, op1=ALU.mult),
         reads=[red], writes=[kf])
    P.op(eng, lambda e: e.scalar_tensor_tensor(redc.t[:], red.t[:], math.pi / 2, kf.t[:], op0=ALU.add, op1=ALU.add),
         reads=[red, kf], writes=[redc])
    for src in (red, redc):
        P.op(eng, lambda e, s=src: e.tensor_scalar(s.t[:], s.t[:], PI_LO, -PI_LO, op0=ALU.min, op1=ALU.max),
             reads=[src], writes=[src])
    P.op("act", lambda e: e.activation(sc.t[:, 0:Fd], red.t[:], AF.Sin), reads=[red], writes=[sc])
    P.op("act", lambda e: e.activation(sc.t[:, Fd:2 * Fd], redc.t[:], AF.Sin), reads=[redc], writes=[sc])


def emit_sincos_v(P, buf, tv, sc, eng="dve", sin=True):
    ang, ki, kf, red, redc = [v.t for v in tv]
    kii = ki.bitcast(I32)
    ops = [
        lambda e: e.tensor_scalar(kii, ang, 1.0 / TWO_PI, None, op0=ALU.mult),
        lambda e: e.tensor_copy(kf, kii),
        lambda e: e.scalar_tensor_tensor(red, kf, -CW1, ang, op0=ALU.mult, op1=ALU.add),
        lambda e: e.scalar_tensor_tensor(red, kf, -(CW2 + CW3), red, op0=ALU.mult, op1=ALU.add),
        lambda e: e.tensor_scalar(kf, red, math.pi, -TWO_PI, op0=ALU.is_gt, op1=ALU.mult),
        lambda e: e.tensor_tensor(red, red, kf, op=ALU.add),
        lambda e: e.tensor_scalar(kf, red, math.pi / 2, -TWO_PI, op0=ALU.is_gt, op1=ALU.mult),
        lambda e: e.scalar_tensor_tensor(redc, red, math.pi / 2, kf, op0=ALU.add, op1=ALU.add),
        lambda e: e.tensor_scalar(red, red, PI_LO, -PI_LO, op0=ALU.min, op1=ALU.max),
        lambda e: e.tensor_scalar(redc, redc, PI_LO, -PI_LO, op0=ALU.min, op1=ALU.max),
    ]
    for f in ops:
        P.op(eng, f, reads=[buf], writes=[buf])
    if not sin:
        return
    P.op("act", lambda e: e.activation(sc.t[:, 0:128], red, AF.Sin), reads=[buf], writes=[sc])
    P.op("act", lambda e: e.activation(sc.t[:, 128:256], redc, AF.Sin), reads=[buf], writes=[sc])


DM = 1024
EPS = 1e-6
RH = 4
AH = 16
HG = 8
NGRP = AH // HG
MASKNEG = -30000.0


def bc(ap, axis, n):
    dims = [list(d) for d in ap.ap]
    dims.insert(axis, [0, n])
    return bass.AP(ap.tensor, ap.offset, dims)


def host_consts():
    c = {}
    c["identb"] = np.eye(128, dtype=np.float32).astype(ml_dtypes.bfloat16)
    c["identf"] = np.eye(128, dtype=np.float32)
    c["invf"] = (1.0 / (10000.0 ** (np.arange(0, 256, 2, dtype=np.float32) / np.float32(256)))).astype(np.float32).reshape(1, 128)
    idx = np.arange(128, dtype=np.float64)
    maskT = np.zeros((128, RH, 128), np.float64)
    qdec = np.zeros((RH, 128), np.float64)
    kdec = np.zeros((RH, 128), np.float64)
    for h in range(RH):
        g = 1.0 - 2.0 ** (-5.0 - h)
        lg = math.log(g)
        k = idx[:, None]
        cc = idx[None, :]
        same = (k // 64) == (cc // 64)
        cross = (k < 64) & (cc >= 64)
        dm = np.where(same, np.exp(lg * np.abs(cc - k)), np.where(cross, np.exp(lg * (cc - k)), 0.0))
        maskT[:, h, :] = dm * np.exp(lg * (k - cc - 128.0))
        qdec[h] = np.exp(lg * (idx + 1.0))
        kdec[h] = np.exp(lg * (127.0 - idx)) / 16.0
    c["maskT"] = maskT.astype(np.float32).reshape(128, RH * 128)
    c["qdec_row"] = qdec.astype(np.float32).reshape(1, RH * 128)
    c["kdec_row"] = kdec.astype(np.float32).reshape(1, RH * 128)
    c["kdec_col"] = kdec.T.astype(np.float32).copy()
    c["gam128"] = [float((1.0 - 2.0 ** (-5.0 - h)) ** 128) for h in range(RH)]
    return c


def attn_bias_tiles(rel_table):
    H = rel_table.shape[0]
    k = np.arange(128)[:, None]
    q = np.arange(128)[None, :]
    out = np.empty((H, 2, 128, 128), np.float32)
    for bi, koff in enumerate((-128, 0)):
        rel = q - (k + koff)
        idx = np.clip(rel, -128, 128) + 128
        t = rel_table[:, idx]
        if koff == 0:
            invalid = (k // 64) > (q // 64)
            t = np.where(invalid[None], np.float32(MASKNEG), t)
        out[:, bi] = t
    return out


def build_program(NT, do_l0=True, do_l1=True):
    S = NT * 128
    nc = bass.Bass("TRN2", target_bir_lowering=False)
    hc = host_consts()

    def din(name, shape, dtype=F32):
        return nc.dram_tensor(name, list(shape), dtype, kind="ExternalInput").ap()

    x_d = din("x", [S, DM])
    cT_d = din("cT", [128, 8])
    pos_d = din("pos", [NT, 128], I32)
    ng_d = din("norm_g", [2, DM])
    adaw_d = din("ada_w", [2, DM, 3 * DM])
    adab_d = din("ada_b", [2, 3 * DM])
    rwin_d = din("ret_w_in", [DM, 6144])
    rgn_d = din("ret_gnT", [128, 16])
    rwout_d = din("ret_w_out", [2048, DM])
    awin_d = din("att_w_in", [DM, 8192])
    aqk_d = din("att_qk_g", [128, 2])
    abias_d = din("att_bias_tiles", [AH, 2, 128, 128])
    acfar_d = din("att_cfar", [1, AH])
    awout_d = din("att_w_out", [2048, DM])
    identb_d = din("identb", [128, 128], BF16)
    identf_d = din("identf", [128, 128])
    invf_d = din("invf", [1, 128])
    maskT_d = din("maskT", [128, RH * 128])
    qdecr_d = din("qdec_row", [1, 512])
    kdecr_d = din("kdec_row", [1, 512])
    kdecc_d = din("kdec_col", [128, RH])
    if do_l0 and do_l1:
        x1_d = nc.dram_tensor("x1s", [S, DM], F32, kind="Internal").ap()
    elif do_l0:
        x1_d = None
    else:
        x1_d = x_d
    out_d = nc.dram_tensor("out", [S, DM], F32, kind="ExternalOutput").ap()

    P = Prog(nc)
    with P.ctx():
        pb = [P.ps("pb%d" % i, [128, 512], F32) for i in range(8)]
        identb = P.sb("identb", [128, 128], BF16)
        identf = P.sb("identf", [128, 128], F32)
        AB = P.sb("AB", [128, 16], F32)
        Grow = P.sb("Grow", [128, DM], F32)
        posT = P.sb("posT", [128, NT], F32)
        mh = P.sb("mh", [128, 16], F32)
        P.dma("sp", identb.t[:], identb_d[:, :], writes=[identb])
        P.dma("sp", identf.t[:], identf_d[:, :], writes=[identf])
        P.op("pool", lambda e: e.memset(mh.t[:], -0.5), writes=[mh])

        def prologue(li):
            with P.scope():
                cT = P.sb("cT", [128, 8], F32)
                cond = P.sb("cond", [128, 8], F32)
                condbc = P.sb("condbc", [128, 8, 128], F32)
                adaw = [P.sb("adaw%d" % i, [128, 3 * DM], F32) for i in range(2)]
                modrow = P.sb("modrow", [128, 3 * DM], F32)
                adab = P.sb("adab", [128, 3 * DM], F32)
                ngrow = P.sb("ngrow", [128, DM], F32)
                arow = P.sb("arow", [128, DM], F32)
                P.dma("sp", cT.t[:], cT_d[:, :], writes=[cT])
                if li == 0 or not do_l0:
                    posi = P.sb("posi", [NT, 128], I32)
                    posf = P.sb("posf", [NT, 128], F32)
                    P.dma("sp", posi.t[:], pos_d[:, :], writes=[posi])
                    P.op("dve", lambda e: e.tensor_copy(posf.t[:], posi.t[:]), reads=[posi], writes=[posf])
                    P.op("pe", lambda e: e.transpose(pb[7].t[:, 0:NT], posf.t[:, :], identf.t[0:NT, 0:NT]),
                         reads=[posf, identf], writes=[pb[7]])
                    P.op("dve", lambda e: e.tensor_copy(posT.t[:], pb[7].t[:, 0:NT]), reads=[pb[7]], writes=[posT])
                P.op("act", lambda e: e.activation(cond.t[:], cT.t[:], AF.Tanh, scale=0.5), reads=[cT], writes=[cond])
                P.op("dve", lambda e: e.tensor_scalar(cond.t[:], cond.t[:], 0.5, 0.5, op0=ALU.mult, op1=ALU.add),
                     reads=[cond], writes=[cond])
                P.op("dve", lambda e: e.tensor_tensor(cond.t[:], cond.t[:], cT.t[:], op=ALU.mult), reads=[cond, cT], writes=[cond])
                P.op("dve", lambda e: e.tensor_copy(condbc.t[:], bc(cond.t[:, :], 2, 128)), reads=[cond], writes=[condbc])
                P.dma("sp", adab.t[:], bcast_rows(adab_d[li:li + 1, :], 128), writes=[adab])
                P.dma("sp", ngrow.t[:], bcast_rows(ng_d[li:li + 1, :], 128), writes=[ngrow])
                for k in range(8):
                    aw = adaw[k % 2]
                    P.dma("sp", aw.t[:], adaw_d[li, k * 128:(k + 1) * 128, :], writes=[aw])
                    for n in range(6):
                        P.op("pe", lambda e, aw=aw, k=k, n=n: e.matmul(
                            pb[n].t[:], condbc.t[:, k, :], aw.t[:, n * 512:(n + 1) * 512],
                            start=(k == 0), stop=(k == 7)), reads=[aw, condbc], writes=[pb[n]])
                for n in range(6):
                    P.op("dve", lambda e, n=n: e.tensor_tensor(
                        modrow.t[:, n * 512:(n + 1) * 512], pb[n].t[:], adab.t[:, n * 512:(n + 1) * 512], op=ALU.add),
                        reads=[pb[n], adab], writes=[modrow])
                P.op("dve", lambda e: e.scalar_tensor_tensor(
                    arow.t[:], modrow.t[:, DM:2 * DM], 1.0, ngrow.t[:], op0=ALU.add, op1=ALU.mult),
                    reads=[modrow, ngrow], writes=[arow])
                P.op("act", lambda e: e.activation(Grow.t[:], modrow.t[:, 2 * DM:3 * DM], AF.Copy),
                     reads=[modrow], writes=[Grow])
                for j in range(16):
                    src = arow.t[:, j * 128:(j + 1) * 128] if j < 8 else modrow.t[:, (j - 8) * 128:(j - 7) * 128]
                    bank = pb[6 + (j % 2)]
                    P.op("pe", lambda e, src=src, bank=bank: e.transpose(bank.t[:, 0:128], src, identf.t[:]),
                         reads=[arow, modrow, identf], writes=[bank])
                    P.op("dve", lambda e, bank=bank, j=j: e.tensor_copy(AB.t[:, j:j + 1], bank.t[:, 0:1]),
                         reads=[bank], writes=[AB])

        if do_l0:
            prologue(0)
            for g in range(2):
                with P.scope():
                    emit_layer0(P, nc, NT, hc, g, pb, identb, AB, Grow, posT, mh,
                                dict(x=x_d, x1=(x1_d if x1_d is not None else out_d), rwin=rwin_d, rgn=rgn_d, rwout=rwout_d,
                                     invf=invf_d, maskT=maskT_d, qdecr=qdecr_d, kdecr=kdecr_d, kdecc=kdecc_d),
                                final=(not do_l1))
        if do_l1:
            prologue(1)
            for g in range(int(os.environ.get('NG', NGRP))):
                with P.scope():
                    emit_layer1(P, nc, NT, g, pb, identb, AB, Grow, mh,
                                dict(x1=x1_d, out=out_d, awin=awin_d, aqk=aqk_d, abias=abias_d, acfar=acfar_d,
                                     awout=awout_d))
    return nc


def emit_norm_stage(P, t, par, xt, src_ap, ss, xs, hT, AB, mh, pbT, identb, q="sp", part="all"):
    if part in ("all", "load"):
        P.dma(q, xt.t[:], src_ap, writes=[xt])
    if part == "load":
        return
    if part in ("all", "stat"):
        _norm_stat(P, xt, ss, xs, mh)
    if part == "stat":
        return
    _norm_tr(P, xs, hT, AB, pbT, identb)


def _norm_stat(P, xt, ss, xs, mh):
    P.op("act", lambda e: e.activation(xs.t[:], xt.t[:], AF.Square, accum_out=ss.t[:, 0:1]),
         reads=[xt], writes=[ss, xs])
    P.op("pool", lambda e: e.tensor_scalar(ss.t[:, 1:2], ss.t[:, 0:1], 1.0 / DM, EPS, op0=ALU.mult, op1=ALU.add),
         reads=[ss], writes=[ss])
    P.op("pool", lambda e: e.tensor_tensor(ss.t[:, 2:3], ss.t[:, 1:2], mh.t[:, 0:1], op=ALU.pow),
         reads=[ss, mh], writes=[ss])
    P.op("act", lambda e: e.activation(xs.t[:], xt.t[:], AF.Copy, scale=ss.t[:, 2:3]), reads=[xt, ss], writes=[xs])


def _norm_tr(P, xs, hT, AB, pbT, identb):
    pv = pbT.t[:, :].bitcast(BF16)
    for k in range(8):
        P.op("pe", lambda e, k=k: e.transpose(pv[:, k * 128:(k + 1) * 128], xs.t[:, k * 128:(k + 1) * 128], identb.t[:]),
             reads=[xs, identb], writes=[pbT])
    hv = hT.t[:, :, :]
    P.op("dve", lambda e: e.tensor_tensor(hv, pv.rearrange("p (k t) -> p k t", k=8), bc(AB.t[:, 0:8], 2, 128), op=ALU.mult),
         reads=[pbT, AB], writes=[hT])
    P.op("dve", lambda e: e.tensor_tensor(hv, hv, bc(AB.t[:, 8:16], 2, 128), op=ALU.add),
         reads=[hT, AB], writes=[hT])


def emit_layer0(P, nc, NT, hc, g, pb, identb, AB, Grow, posT, mh, D, final):
    HP = 2
    first = (g == 0)
    h0 = g * HP
    gam128 = hc["gam128"]
    Win = P.sb("Win0", [128, 8, 3072], BF16)
    Wout = P.sb("Wout0", [128, 8, DM], BF16)
    gnT = P.sb("gnT", [128, 16], F32)
    invr = P.sb("invr", [128, 128], F32)
    maskT = P.sb("maskT", [128, HP * 128], F32)
    qdecr = P.sb("qdecr", [128, HP * 128], F32)
    kdecr = P.sb("kdecr", [128, HP * 128], F32)
    kdecc = P.sb("kdecc", [128, RH], F32)
    S32 = [P.sb("S32_%d" % i, [128, 512], F32) for i in range(2 * HP)]
    Sbf = [P.sb("Sbf_%d" % i, [128, 512], BF16) for i in range(2 * HP)]
    NX = 4
    xt = [P.sb("xt%d" % i, [128, DM], F32) for i in range(NX)]
    ot = [P.sb("ot%d" % i, [128, DM], F32) for i in range(NX)] if not first else None
    ss = [P.sb("ss%d" % i, [128, 8], F32) for i in range(NX)]
    xs = P.sb("xs", [128, DM], BF16)
    hT2 = [P.sb("hT%d" % i, [128, 8, 128], BF16) for i in range(2)]
    sc2 = [P.sb("sc%d" % i, [128, 256], F32) for i in range(2)]
    stmp = P.sb("stmp", [128, 5, 128], F32)
    rtmp = P.sb("rtmp", [128, 4, 256], F32)
    qk2 = [P.sb("qk%d" % i, [128, 1024], BF16) for i in range(2)]
    v2 = [P.sb("v%d" % i, [128, 1024], BF16) for i in range(2)]
    sg2 = [P.sb("sg%d" % i, [128, 1024], BF16) for i in range(2)]
    kd = P.sb("kd", [128, 512], BF16)
    qdT = P.sb("qdT", [128, 4, 128], BF16)
    kdT = P.sb("kdT", [128, 4, 128], BF16)
    sT = P.sb("sT", [128, HP, 128], BF16)
    y = P.sb("y", [128, 1024], BF16)
    yT = P.sb("yT", [128, 8, 128], BF16)
    junko = P.sb("junko", [128, 512], BF16)
    osq = P.sb("osq", [128, 8], F32)

    class _V:
        def __init__(self, ap):
            self.t = ap
    tmpv = [_V(stmp.t[:, i, :]) for i in range(5)]

    P.dma("sp", gnT.t[:], D["rgn"][:, :], writes=[gnT])
    P.dma("sp", invr.t[:], bcast_rows(D["invf"], 128), writes=[invr])
    P.dma("sp", maskT.t[:], D["maskT"][:, h0 * 128:(h0 + HP) * 128], writes=[maskT])
    P.dma("sp", qdecr.t[:], bcast_rows(D["qdecr"][0:1, h0 * 128:(h0 + HP) * 128], 128), writes=[qdecr])
    P.dma("sp", kdecr.t[:], bcast_rows(D["kdecr"][0:1, h0 * 128:(h0 + HP) * 128], 128), writes=[kdecr])
    P.dma("sp", kdecc.t[:], D["kdecc"][:, :], writes=[kdecc])
    blocks_src = [(0, h0 * 256), (512, 1024 + h0 * 256), (1024, 2048 + h0 * 512), (1536, 2048 + h0 * 512 + 512),
                  (2048, 4096 + h0 * 512), (2560, 4096 + h0 * 512 + 512)]
    WinB = [TBuf("Win0_b%d" % n, Win.t) for n in range(6)]
    for n, (dst0, src0) in enumerate(blocks_src):
        for k in range(8):
            P.dma("pool", Win.t[:, k, dst0:dst0 + 512], D["rwin"][k * 128:(k + 1) * 128, src0:src0 + 512],
                  writes=[WinB[n]], max_dma_last_dim=4096)
    for ch in range(8):
        w = xt[ch % 2]
        cg = g * 8 + ch
        P.dma("sp", w.t[:], D["rwout"][cg * 128:(cg + 1) * 128, :], writes=[w])
        P.op("dve", lambda e, w=w, ch=ch, cg=cg: e.scalar_tensor_tensor(
            Wout.t[:, ch, :], w.t[:], gnT.t[:, cg:cg + 1], Grow.t[:], op0=ALU.mult, op1=ALU.mult),
            reads=[w, gnT, Grow], writes=[Wout])
    for i in range(2 * HP):
        P.op("pool", lambda e, i=i: e.memset(S32[i].t[:], 0.0), writes=[S32[i]])
        P.op("pool", lambda e, i=i: e.memset(Sbf[i].t[:], 0.0), writes=[Sbf[i]])

    pbP = [pb[0], pb[1]]
    pbT = [pb[2], pb[3]]
    pbS = pb[4]
    pbO = [pb[5], pb[6]]
    pbU = pb[7]
    x1buf = TBuf("x1dram")

    def s1_chunks(t):
        par = t % 2
        X = xt[t % NX]
        hT = hT2[par]
        sc = sc2[par]
        qk, v, sg = qk2[par], v2[par], sg2[par]
        sin_b = bc(sc.t[:, 0:128], 1, 2)
        cos_b = bc(sc.t[:, 128:256], 1, 2)

        def c0a():
            emit_norm_stage(P, t, par, X, D["x"][t * 128:(t + 1) * 128, :], ss[t % NX], xs, hT, AB, mh, pbT[0], identb, part="load")
            if not first:
                P.dma("sp", ot[t % NX].t[:], D["x1"][t * 128:(t + 1) * 128, :], reads=[x1buf], writes=[ot[t % NX]])

        def c0b():
            emit_norm_stage(P, t, par, X, None, ss[t % NX], xs, hT, AB, mh, pbT[0], identb, part="stat")

        def c0():
            emit_norm_stage(P, t, par, X, None, ss[t % NX], xs, hT, AB, mh, pbT[0], identb, part="tr")
            P.op("act", lambda e: e.activation(sc.t[:, 0:128], tmpv[3].t, AF.Sin), reads=[stmp], writes=[sc])
            P.op("act", lambda e: e.activation(sc.t[:, 128:256], tmpv[4].t, AF.Sin), reads=[stmp], writes=[sc])

        def csc():
            angv = tmpv[0]
            P.op("dve", lambda e: e.tensor_scalar(angv.t, invr.t[:], posT.t[:, t:t + 1], None, op0=ALU.mult),
                 reads=[invr, posT], writes=[stmp])
            emit_sincos_v(P, stmp, tmpv, sc, sin=False)

        def blk(n):
            bank = pbP[n % 2]
            for k in range(8):
                P.op("pe", lambda e, k=k: e.matmul(
                    bank.t[:], hT.t[:, k, :], Win.t[:, k, n * 512:(n + 1) * 512], start=(k == 0), stop=(k == 7)),
                    reads=[hT, WinB[n]], writes=[bank])
            if n < 2:
                pv4 = bank.t[:, :].rearrange("p (h t d) -> p h t d", h=2, t=2)
                t1, t2 = pv4[:, :, 0, :], pv4[:, :, 1, :]
                r = [rtmp.t[:, i, :].rearrange("p (h d) -> p h d", h=2) for i in range(4)]
                P.op("dve", lambda e: e.tensor_tensor(r[0], t1, cos_b, op=ALU.mult), reads=[bank, sc], writes=[rtmp])
                P.op("dve", lambda e: e.tensor_tensor(r[1], t2, sin_b, op=ALU.mult), reads=[bank, sc], writes=[rtmp])
                P.op("dve", lambda e: e.tensor_tensor(r[2], t1, sin_b, op=ALU.mult), reads=[bank, sc], writes=[rtmp])
                P.op("dve", lambda e: e.tensor_tensor(r[3], t2, cos_b, op=ALU.mult), reads=[bank, sc], writes=[rtmp])
                ov = qk.t[:, n * 512:(n + 1) * 512].rearrange("p (h t d) -> p h t d", h=2, t=2)
                P.op("pool", lambda e: e.tensor_tensor(ov[:, :, 0, :], r[0], r[1], op=ALU.subtract),
                     reads=[rtmp], writes=[qk])
                P.op("pool", lambda e: e.tensor_tensor(ov[:, :, 1, :], r[2], r[3], op=ALU.add),
                     reads=[rtmp], writes=[qk])
            elif n < 4:
                j = n - 2
                P.op("act", lambda e: e.activation(v.t[:, j * 512:(j + 1) * 512], bank.t[:], AF.Copy),
                     reads=[bank], writes=[v])
            else:
                j = n - 4
                P.op("act", lambda e: e.activation(sg.t[:, j * 512:(j + 1) * 512], bank.t[:], AF.Silu),
                     reads=[bank], writes=[sg])
        return [c0] + [(lambda n=n: blk(n)) for n in range(6)] + [c0a, c0b, csc]

    def s2_chunks(t):
        par = t % 2
        X = xt[t % NX]
        qk, v, sg = qk2[par], v2[par], sg2[par]

        def d0():
            for hh in range(HP):
                P.op("dve", lambda e, hh=hh: e.tensor_scalar(
                    kd.t[:, hh * 256:(hh + 1) * 256], qk.t[:, 512 + hh * 256:512 + (hh + 1) * 256],
                    kdecc.t[:, h0 + hh:h0 + hh + 1], None, op0=ALU.mult), reads=[qk, kdecc], writes=[kd])
            bank = pbT[1]
            pv = bank.t[:, :].bitcast(BF16)
            for j in range(8):
                P.op("pe", lambda e, j=j: e.transpose(
                    pv[:, j * 128:(j + 1) * 128], qk.t[:, j * 128:(j + 1) * 128], identb.t[:]),
                    reads=[qk, identb], writes=[bank])
            for which, (dst, dec) in enumerate(((qdT, qdecr), (kdT, kdecr))):
                decb = bc(dec.t[:, :].rearrange("p (h t) -> p h t", h=HP), 2, 2)
                src = pv[:, which * 512:(which + 1) * 512].rearrange("p (h c t) -> p h c t", h=HP, c=2)
                P.op("dve", lambda e, dst=dst, decb=decb, src=src: e.tensor_tensor(
                    dst.t[:, :, :].rearrange("p (h c) t -> p h c t", h=HP), src, decb, op=ALU.mult),
                    reads=[bank, dec], writes=[dst])

        def d1():
            for hh in range(HP):
                for ch in range(2):
                    P.op("pe", lambda e, hh=hh, ch=ch: e.matmul(
                        pbS.t[:, hh * 128:(hh + 1) * 128], kdT.t[:, 2 * hh + ch, :], qdT.t[:, 2 * hh + ch, :],
                        start=(ch == 0), stop=(ch == 1)), reads=[kdT, qdT], writes=[pbS])
            P.op("dve", lambda e: e.tensor_tensor(sT.t[:, :, :].rearrange("p a b -> p (a b)"), pbS.t[:, 0:HP * 128], maskT.t[:], op=ALU.mult),
                 reads=[pbS, maskT], writes=[sT])

        def head(hh):
            ob = pbO[hh % 2]
            vh = v.t[:, hh * 512:(hh + 1) * 512]
            P.op("pe", lambda e: e.matmul(ob.t[:], sT.t[:, hh, :], vh, start=True, stop=False),
                 reads=[sT, v], writes=[ob])
            for ch in range(2):
                i = 2 * hh + ch
                P.op("pe", lambda e, i=i, ch=ch: e.matmul(ob.t[:], qdT.t[:, i, :], Sbf[i].t[:], start=False, stop=(ch == 1)),
                     reads=[qdT, Sbf[i]], writes=[ob])
            P.op("act", lambda e: e.activation(junko.t[:], ob.t[:], AF.Square, accum_out=osq.t[:, hh:hh + 1]),
                 reads=[ob], writes=[osq])
            P.op("pool", lambda e: e.tensor_scalar(osq.t[:, 4 + hh:5 + hh], osq.t[:, hh:hh + 1], 1.0 / 512, EPS, op0=ALU.mult, op1=ALU.add),
                 reads=[osq], writes=[osq])
            P.op("pool", lambda e: e.tensor_tensor(osq.t[:, 4 + hh:5 + hh], osq.t[:, 4 + hh:5 + hh], mh.t[:, 0:1], op=ALU.pow),
                 reads=[osq, mh], writes=[osq])
            for ch in range(2):
                i = 2 * hh + ch
                ub = pbU if ch == 0 else pbT[0]
                P.op("pe", lambda e, ch=ch, ub=ub: e.matmul(
                    ub.t[:], kd.t[:, hh * 256 + ch * 128: hh * 256 + (ch + 1) * 128], vh, start=True, stop=True),
                    reads=[kd, v], writes=[ub])
                P.op("dve", lambda e, i=i, ub=ub: e.scalar_tensor_tensor(
                    S32[i].t[:], S32[i].t[:], gam128[h0 + hh], ub.t[:], op0=ALU.mult, op1=ALU.add),
                    reads=[S32[i], ub], writes=[S32[i]])
                P.op("act", lambda e, i=i: e.activation(Sbf[i].t[:], S32[i].t[:], AF.Copy), reads=[S32[i]], writes=[Sbf[i]])

        def ygate(hh):
            ob = pbO[hh % 2]
            P.op("dve", lambda e: e.scalar_tensor_tensor(
                y.t[:, hh * 512:(hh + 1) * 512], ob.t[:], osq.t[:, 4 + hh:5 + hh], sg.t[:, hh * 512:(hh + 1) * 512],
                op0=ALU.mult, op1=ALU.mult), reads=[ob, osq, sg], writes=[y])

        def ytr():
            bank = pbT[1]
            pv = bank.t[:, :].bitcast(BF16)
            for j in range(8):
                P.op("pe", lambda e, j=j: e.transpose(
                    pv[:, j * 128:(j + 1) * 128], y.t[:, j * 128:(j + 1) * 128], identb.t[:]),
                    reads=[y, identb], writes=[bank])
            P.op("act", lambda e: e.activation(yT.t[:, :, :].rearrange("p a b -> p (a b)"), pv, AF.Copy), reads=[bank], writes=[yT])

        def oproj(n):
            acc = X if first else ot[t % NX]
            bank = pbP[n]
            for ch in range(8):
                P.op("pe", lambda e, ch=ch: e.matmul(
                    bank.t[:], yT.t[:, ch, :], Wout.t[:, ch, n * 512:(n + 1) * 512], start=(ch == 0), stop=(ch == 7)),
                    reads=[yT, Wout], writes=[bank])
            P.op("dve", lambda e: e.tensor_tensor(
                acc.t[:, n * 512:(n + 1) * 512], bank.t[:], acc.t[:, n * 512:(n + 1) * 512], op=ALU.add),
                reads=[bank, acc], writes=[acc])
            if n == 1:
                P.dma("sp", D["x1"][t * 128:(t + 1) * 128, :], acc.t[:], reads=[acc], writes=[x1buf],
                      final=(final and g == 1))

        return [d0, d1, lambda: head(0), lambda: head(1), lambda: ygate(0), lambda: ygate(1),
                ytr, lambda: oproj(0), lambda: oproj(1)]

    order = [("d", 0), ("c", 1), ("p", 6), ("d", 1), ("c", 2), ("p", 7), ("b", 0), ("d", 2), ("p", 8), ("a", 0), ("c", 3), ("d", 3),
             ("d", 4), ("c", 4), ("s", 0), ("d", 5), ("c", 5), ("e", 0), ("c", 6)]
    s1 = {}
    for tt in range(min(3, NT)):
        s1[tt] = s1_chunks(tt)
        s1[tt][7]()
    for tt in range(min(2, NT)):
        s1[tt][8]()
        s1[tt][9]()
        s1[tt][0]()
    for f in s1[0][1:7]:
        f()
    prev = None
    for t in range(NT):
        dch = s2_chunks(t)
        cch = s1.get(t + 1)
        if t + 3 < NT:
            s1[t + 3] = s1_chunks(t + 3)
        for kind, i in order:
            if kind == "d":
                dch[i]()
            elif kind == "p":
                if prev is not None:
                    prev[i]()
            elif kind == "c":
                if cch is not None:
                    cch[i]()
            elif kind == "a":
                if t + 3 < NT:
                    s1[t + 3][7]()
            elif kind == "b":
                if t + 2 < NT:
                    s1[t + 2][8]()
            elif kind == "s":
                if t + 2 < NT:
                    s1[t + 2][9]()
            elif t + 2 < NT:
                s1[t + 2][0]()
        s1.pop(t, None)
        prev = dch
    for i in (6, 7, 8):
        prev[i]()


def emit_layer1(P, nc, NT, g, pb, identb, AB, Grow, mh, D):
    first = (g == 0)
    last = (g == NGRP - 1)
    W = HG * 128
    NR = 6
    Win = P.sb("Win1", [128, 8, 4 * W], BF16)
    Wout = P.sb("Wout1", [128, HG, DM], BF16)
    qkg = P.sb("qkg", [128, 2], F32)
    GG = P.sb("GG", [128, 1], F32)
    Bn = P.sb("Bn", [128, HG, 2, 128], F32)
    cfar = P.sb("cfar", [128, HG], F32)
    kTc = [P.sb("kTc%d" % i, [128, HG, 128], BF16) for i in range(NR)]
    vxc = [P.sb("vxc%d" % i, [128, HG, 130], BF16) for i in range(NR)]
    pT = [P.sb("pT%d" % i, [128, 5, 128], BF16) for i in range(2)]
    NX = 4
    xt = [P.sb("xt%d" % i, [128, DM], F32) for i in range(NX)]
    ot = [P.sb("ot%d" % i, [128, DM], F32) for i in range(NX)] if not first else None
    ss = [P.sb("ss%d" % i, [128, 8], F32) for i in range(NX)]
    xs2 = [P.sb("xs%d" % i, [128, DM], BF16) for i in range(2)]
    hT2 = [P.sb("hT%d" % i, [128, 8, 128], BF16) for i in range(2)]
    qkr = P.sb("qkr", [128, 2 * W], BF16)
    qkn = P.sb("qkn", [128, 2 * W], BF16)
    sq = [P.sb("sq%d" % i, [128, 2 * HG], F32) for i in range(2)]
    rq = [P.sb("rq%d" % i, [128, 2 * HG], F32) for i in range(2)]
    th = P.sb("th", [128, 512], F32)
    u = [P.sb("u%d" % i, [128, W], BF16) for i in range(2)]
    qT = [P.sb("qT%d" % i, [128, HG, 128], BF16) for i in range(2)]
    stmp = [P.sb("stmp%d" % i, [128, 256], F32) for i in range(2)]
    rinv = P.sb("rinv", [128, HG], F32)
    y = P.sb("y", [128, W], BF16)
    yT = P.sb("yT", [128, HG, 128], BF16)
    sqv = [P.sb("sqv%d" % i, [128, 512], F32) for i in range(2)]
    qkrb = [TBuf("qkr%d" % i, qkr.t) for i in range(4)]

    WinB = [TBuf("Win1_b%d" % n, Win.t) for n in range(8)]
    for n in range(8):
        j, hb = n // 2, n % 2
        src0 = j * 2048 + g * W + hb * 512
        for k in range(8):
            P.dma("pool", Win.t[:, k, n * 512:(n + 1) * 512], D["awin"][k * 128:(k + 1) * 128, src0:src0 + 512],
                  writes=[WinB[n]], max_dma_last_dim=4096)
    for h in range(HG):
        w = xt[h % 2]
        r0 = (g * HG + h) * 128
        P.dma("sp", w.t[:], D["awout"][r0:r0 + 128, :], writes=[w])
        P.op("dve", lambda e, w=w, h=h: e.scalar_tensor_tensor(
            Wout.t[:, h, :], w.t[:], 0.5, Grow.t[:], op0=ALU.mult, op1=ALU.mult), reads=[w, Grow], writes=[Wout])
    P.dma("sp", qkg.t[:], D["aqk"][:, :], writes=[qkg])
    P.op("dve", lambda e: e.scalar_tensor_tensor(GG.t[:], qkg.t[:, 0:1], 128.0 ** -0.5, qkg.t[:, 1:2], op0=ALU.mult, op1=ALU.mult),
         reads=[qkg], writes=[GG])
    P.dma("sp", Bn.t[:], D["abias"][g * HG:(g + 1) * HG].rearrange("h b p c -> p h b c"), writes=[Bn])
    P.dma("sp", cfar.t[:], bcast_rows(D["acfar"][0:1, g * HG:(g + 1) * HG], 128), writes=[cfar])
    for i in range(NR):
        P.op("pool", lambda e, i=i: e.memset(vxc[i].t[:, :, 128:130], 1.0), writes=[vxc[i]])
    for i in range(2):
        P.op("pool", lambda e, i=i: e.memset(pT[i].t[:], 0.0), writes=[pT[i]])

    pbP = [pb[0], pb[1]]
    pbT = pb[2]
    pbF = [pb[3], pb[4]]
    pbN = [pb[5], pb[6]]
    pbV = pb[7]
    pbVs = [TBuf("pbV%d" % i, pb[7].t) for i in range(3)]
    outbuf = TBuf("outdram")
    src_d = D["x1"]
    pv = pbT.t[:, :].bitcast(BF16)

    def s1_chunks(t):
        par = t % 2
        X = xt[t % NX]
        hT = hT2[par]
        slot = t % NR
        SQ, RQ, U, QT = sq[par], rq[par], u[par], qT[par]

        def c0a():
            emit_norm_stage(P, t, par, X, src_d[t * 128:(t + 1) * 128, :], ss[t % NX], xs2[par], hT, AB, mh, pbT, identb, part="load")
            if not first:
                P.dma("sp", ot[t % NX].t[:], D["out"][t * 128:(t + 1) * 128, :], reads=[outbuf], writes=[ot[t % NX]])

        def c0b():
            emit_norm_stage(P, t, par, X, None, ss[t % NX], xs2[par], hT, AB, mh, pbT, identb, part="stat")

        def c0():
            emit_norm_stage(P, t, par, X, None, ss[t % NX], xs2[par], hT, AB, mh, pbT, identb, part="tr")

        def blk(n):
            bank = pbP[n % 2]
            for k in range(8):
                P.op("pe", lambda e, k=k: e.matmul(
                    bank.t[:], hT.t[:, k, :], Win.t[:, k, n * 512:(n + 1) * 512], start=(k == 0), stop=(k == 7)),
                    reads=[hT, WinB[n]], writes=[bank])
            if 1 <= n <= 4:
                m = n - 1
                svm = sqv[m % 2]
                P.op("dve", lambda e: e.tensor_reduce(SQ.t[:, m * 4:(m + 1) * 4], svm.t[:, :].rearrange("p (h d) -> p h d", h=4),
                                                      op=ALU.add, axis=mybir.AxisListType.X),
                     reads=[svm], writes=[SQ])
            if n < 4:
                qb = qkrb[n]
                P.op("dve", lambda e: e.tensor_copy(qkr.t[:, n * 512:(n + 1) * 512], bank.t[:]),
                     reads=[bank], writes=[qb])
                sv = sqv[n % 2]
                P.op("pool", lambda e: e.tensor_tensor(sv.t[:], qkr.t[:, n * 512:(n + 1) * 512], qkr.t[:, n * 512:(n + 1) * 512], op=ALU.mult),
                     reads=[qb], writes=[sv])
            elif n < 6:
                j = n - 4
                P.op("act", lambda e: e.activation(
                    vxc[slot].t[:, j * 4:(j + 1) * 4, 0:128], bank.t[:, :].rearrange("p (h d) -> p h d", h=4), AF.Copy),
                    reads=[bank], writes=[vxc[slot]])
            else:
                j = n - 6
                P.op("act", lambda e: e.activation(th.t[:], bank.t[:], AF.Tanh, scale=0.5), reads=[bank], writes=[th])
                P.op("dve", lambda e: e.scalar_tensor_tensor(
                    U.t[:, j * 512:(j + 1) * 512], th.t[:], 1.0, bank.t[:], op0=ALU.add, op1=ALU.mult),
                    reads=[th, bank], writes=[U])

        def c9a():
            P.op("pool", lambda e: e.tensor_scalar(RQ.t[:], SQ.t[:], 1.0 / 128, EPS, op0=ALU.mult, op1=ALU.add), reads=[SQ], writes=[RQ])
            P.op("pool", lambda e: e.tensor_tensor(RQ.t[:], RQ.t[:], mh.t[:, 0:16], op=ALU.pow), reads=[RQ, mh], writes=[RQ])

        def c9d():
            for hf in range(2):
                P.op("dve", lambda e, hf=hf: e.tensor_tensor(
                    qkn.t[:, hf * W:(hf + 1) * W].rearrange("p (h d) -> p h d", h=HG),
                    qkr.t[:, hf * W:(hf + 1) * W].rearrange("p (h d) -> p h d", h=HG),
                    bc(RQ.t[:, hf * HG:(hf + 1) * HG], 2, 128), op=ALU.mult), reads=[qkrb[2 * hf], qkrb[2 * hf + 1], RQ], writes=[qkn])

        def c9b():
            for j in range(HG):
                P.op("pe", lambda e, j=j: e.transpose(pv[:, j * 128:(j + 1) * 128], qkn.t[:, j * 128:(j + 1) * 128], identb.t[:]),
                     reads=[qkn, identb], writes=[pbT])
            P.op("dve", lambda e: e.tensor_scalar(QT.t[:, :, :].rearrange("p a b -> p (a b)"), pv, GG.t[:, 0:1], None, op0=ALU.mult),
                 reads=[pbT, GG], writes=[QT])

        def c10k():
            for j in range(HG):
                P.op("pe", lambda e, j=j: e.transpose(pv[:, j * 128:(j + 1) * 128], qkn.t[:, W + j * 128: W + (j + 1) * 128], identb.t[:]),
                     reads=[qkn, identb], writes=[pbT])
            P.op("dve", lambda e: e.tensor_copy(kTc[slot].t[:, :, :].rearrange("p a b -> p (a b)"), pv),
                 reads=[pbT], writes=[kTc[slot]])

        return [c0] + [(lambda n=n: blk(n)) for n in range(8)] + [c9a, c9b, c0a, c0b, c9d, c10k]

    def s2_chunks(t):
        par = t % 2
        X = xt[t % NX]
        U, QT = u[par], qT[par]
        blocks = [i for i in range(5) if t - 4 + i >= 0]
        far = [i for i in blocks if i < 3]

        def scores(h):
            hp = h % 2
            fb, nb = pbF[hp], pbN[hp]
            for i in far:
                sl = (t - 4 + i) % NR
                P.op("pe", lambda e, i=i, sl=sl: e.matmul(
                    fb.t[:, i * 128:(i + 1) * 128], kTc[sl].t[:, h, :], QT.t[:, h, :], start=True, stop=True),
                    reads=[kTc[sl], QT], writes=[fb])
            for i in blocks:
                if i < 3:
                    continue
                sl = (t - 4 + i) % NR
                P.op("pe", lambda e, i=i, sl=sl: e.matmul(
                    nb.t[:, (i - 3) * 128:(i - 2) * 128], kTc[sl].t[:, h, :], QT.t[:, h, :], start=True, stop=True),
                    reads=[kTc[sl], QT], writes=[nb])

        def expo(h):
            hp = h % 2
            fb, nb, pt = pbF[hp], pbN[hp], pT[hp]
            if far:
                f0 = far[0]
                cf_ = f0 * 128
                P.op("act", lambda e: e.activation(
                    pt.t[:, f0:3, :].rearrange("p a b -> p (a b)"), fb.t[:, cf_:384], AF.Exp, bias=cfar.t[:, h:h + 1]),
                    reads=[fb, cfar], writes=[pt])
                if f0 == 0:
                    P.op("pool", lambda e: e.memset(pt.t[0:64, 0, 64:128], 0.0), writes=[pt])
            nn = [i for i in blocks if i >= 3]
            n0 = nn[0] - 3
            st = stmp[hp]
            P.op("dve", lambda e: e.tensor_tensor(
                st.t[:, n0 * 128:256], nb.t[:, n0 * 128:256], Bn.t[:, h, n0:2, :].rearrange("p a b -> p (a b)"), op=ALU.add),
                reads=[nb, Bn], writes=[st])
            P.op("act", lambda e: e.activation(
                pt.t[:, 3 + n0:5, :].rearrange("p a b -> p (a b)"), st.t[:, n0 * 128:256], AF.Exp), reads=[st], writes=[pt])
        def pvh(h):
            hp = h % 2
            pt = pT[hp]
            vs = pbVs[h % 3]
            c0_ = (h % 3) * 130
            for bi, i in enumerate(blocks):
                sl = (t - 4 + i) % NR
                P.op("pe", lambda e, i=i, sl=sl, bi=bi: e.matmul(
                    pbV.t[:, c0_:c0_ + 129], pt.t[:, i, :], vxc[sl].t[:, h, 0:129], start=(bi == 0), stop=(bi == len(blocks) - 1)),
                    reads=[pt, vxc[sl]], writes=[vs])
            P.op("dve", lambda e: e.reciprocal(rinv.t[:, h:h + 1], pbV.t[:, c0_ + 128:c0_ + 129]), reads=[vs], writes=[rinv])
            P.op("dve", lambda e: e.scalar_tensor_tensor(
                y.t[:, h * 128:(h + 1) * 128], pbV.t[:, c0_:c0_ + 128], rinv.t[:, h:h + 1], U.t[:, h * 128:(h + 1) * 128],
                op0=ALU.mult, op1=ALU.mult), reads=[vs, rinv, U], writes=[y])

        def ytr():
            for j in range(HG):
                P.op("pe", lambda e, j=j: e.transpose(pv[:, j * 128:(j + 1) * 128], y.t[:, j * 128:(j + 1) * 128], identb.t[:]),
                     reads=[y, identb], writes=[pbT])
            P.op("act", lambda e: e.activation(yT.t[:, :, :].rearrange("p a b -> p (a b)"), pv, AF.Copy), reads=[pbT], writes=[yT])

        def oproj(n):
            acc = X if first else ot[t % NX]
            bank = pbP[n]
            for ch in range(HG):
                P.op("pe", lambda e, ch=ch: e.matmul(
                    bank.t[:], yT.t[:, ch, :], Wout.t[:, ch, n * 512:(n + 1) * 512], start=(ch == 0), stop=(ch == HG - 1)),
                    reads=[yT, Wout], writes=[bank])
            P.op("dve", lambda e: e.tensor_tensor(
                acc.t[:, n * 512:(n + 1) * 512], bank.t[:], acc.t[:, n * 512:(n + 1) * 512], op=ALU.add),
                reads=[bank, acc], writes=[acc])
            if n == 1:
                P.dma("sp", D["out"][t * 128:(t + 1) * 128, :], acc.t[:], reads=[acc], writes=[outbuf], final=last)

        return [(lambda h=h: scores(h)) for h in range(HG)] + [(lambda h=h: expo(h)) for h in range(HG)] + \
               [(lambda h=h: pvh(h)) for h in range(HG)] + [ytr, lambda: oproj(0), lambda: oproj(1)]

    order = [("d", 0), ("c", 1), ("p", 24), ("d", 1), ("p", 25), ("b", 0),
             ("d", 8), ("c", 2), ("d", 2), ("d", 16), ("p", 26), ("a", 0),
             ("d", 9), ("c", 3), ("d", 3), ("d", 17),
             ("d", 10), ("c", 4), ("d", 4), ("d", 18),
             ("d", 11), ("c", 5), ("c", 9), ("d", 5), ("d", 19),
             ("d", 12), ("c", 6), ("d", 6), ("d", 20),
             ("d", 13), ("c", 7), ("c", 13), ("d", 7), ("d", 21), ("e", 0),
             ("d", 14), ("c", 8), ("d", 22), ("c", 10),
             ("d", 15), ("c", 14), ("d", 23)]
    s1 = {}
    for tt in range(min(3, NT)):
        s1[tt] = s1_chunks(tt)
        s1[tt][11]()
    for tt in range(min(2, NT)):
        s1[tt][12]()
        s1[tt][0]()
    for i in (1, 2, 3, 4, 5, 6, 7, 8, 9, 13, 10, 14):
        s1[0][i]()
    prev = None
    for t in range(NT):
        dch = s2_chunks(t)
        cch = s1.get(t + 1)
        if t + 3 < NT:
            s1[t + 3] = s1_chunks(t + 3)
        for kind, i in order:
            if kind == "d":
                dch[i]()
            elif kind == "p":
                if prev is not None:
                    prev[i]()
            elif kind == "c":
                if cch is not None:
                    cch[i]()
            elif kind == "a":
                if t + 3 < NT:
                    s1[t + 3][11]()
            elif kind == "b":
                if t + 2 < NT:
                    s1[t + 2][12]()
            elif t + 2 < NT:
                s1[t + 2][0]()
        s1.pop(t, None)
        prev = dch
    for i in (24, 25, 26):
        prev[i]()


def _lay(inputs, b, NT):
    S = NT * 128
    hc = host_consts()
    f = lambda a: np.ascontiguousarray(a, dtype=np.float32)
    d = {
        "x": f(inputs["x"][b, :S]),
        "cT": f(inputs["c"][b].reshape(8, 128).T),
        "pos": np.ascontiguousarray(inputs["positions"][b, :S].reshape(NT, 128).astype(np.int32)),
        "norm_g": f(inputs["norm_g"]),
        "ada_w": f(inputs["ada_w"]),
        "ada_b": f(inputs["ada_b"]),
        "ret_w_in": f(inputs["ret_w_in"][0]),
        "ret_gnT": f(inputs["ret_gn_g"][0].reshape(16, 128).T),
        "ret_w_out": f(inputs["ret_w_out"][0]),
        "att_w_in": f(inputs["att_w_in"][0]),
        "att_qk_g": f(np.stack([inputs["att_q_g"][0], inputs["att_k_g"][0]], axis=1)),
        "att_bias_tiles": attn_bias_tiles(np.asarray(inputs["att_rel_bias"][0], np.float32)),
        "att_cfar": f(inputs["att_rel_bias"][0][:, 256].reshape(1, AH)),
        "att_w_out": f(inputs["att_w_out"][0]),
    }
    for k in ("identb", "identf", "invf", "maskT", "qdec_row", "kdec_row", "kdec_col"):
        d[k] = hc[k]
    return d


def run(inputs, NT=64, ncores=8, do_l0=True, do_l1=True):
    inputs = {k: np.asarray(v) for k, v in inputs.items()}
    nc = build_program(NT, do_l0, do_l1)
    in_maps = [_lay(inputs, b, NT) for b in range(ncores)]
    res = run_bass_kernel_spmd(nc, in_maps, core_ids=list(range(ncores)))
    return np.stack([np.asarray(r["out"]) for r in res.results], axis=0)


def kernel(**inputs):
    return run(inputs, NT=64, ncores=8)
```

```python
import math
import os
from contextlib import ExitStack, contextmanager

import numpy as np
import ml_dtypes
import concourse.bass as bass
import concourse.mybir as mybir
from concourse.bass_utils import run_bass_kernel_spmd

F32 = mybir.dt.float32
BF16 = mybir.dt.bfloat16
I32 = mybir.dt.int32
AF = mybir.ActivationFunctionType
ALU = mybir.AluOpType

ENGINES = ("pe", "act", "dve", "pool", "sp")
NDSEM = 6


class TBuf:
    __slots__ = ("name", "t", "w", "r", "wd")

    def __init__(self, name, t=None):
        self.name = name
        self.t = t
        self.w = None
        self.r = {}
        self.wd = {}


class _Op:
    __slots__ = ("eng", "fn", "idx", "deps", "kind", "dsem")


class Prog:
    def __init__(self, nc):
        self.nc = nc
        self.ops = {e: [] for e in ENGINES}
        self.ndma = {e: 0 for e in ENGINES}
        self.finals = []
        self.stack = None
        self.phase = 0
        self.base = {e: 0 for e in ENGINES}

    @contextmanager
    def ctx(self):
        with ExitStack() as st:
            self.stack = st
            self.csem = {e: st.enter_context(self.nc.semaphore("cs_" + e)) for e in ENGINES}
            self.dsem = {
                e: [st.enter_context(self.nc.semaphore("ds_%s%d" % (e, i))) for i in range(NDSEM)]
                for e in ("sp", "pool", "act")
            }
            yield self
            self.emit()

    @contextmanager
    def scope(self):
        self.emit()
        outer = self.stack
        with ExitStack() as st:
            self.stack = st
            yield self
            self.emit()
        self.stack = outer

    def sb(self, name, shape, dt):
        self.uid = getattr(self, "uid", 0) + 1
        t = self.stack.enter_context(self.nc.sbuf_tensor("sb%d_%s" % (self.uid, name), list(shape), dt))
        return TBuf(name, t)

    def ps(self, name, shape, dt):
        t = self.stack.enter_context(self.nc.psum_tensor("ps_" + name, list(shape), dt))
        return TBuf(name, t)

    def view(self, buf, name):
        return TBuf(name, buf.t)

    def _collect(self, eng, reads, writes):
        deps = []
        for b in reads:
            if b.w is not None:
                deps.append(b.w)
            deps.extend(b.wd.values())
        for b in writes:
            if b.w is not None:
                deps.append(b.w)
            deps.extend(b.wd.values())
            deps.extend(b.r.values())
        ph = self.phase
        deps = [d for d in deps if not (d[0] == "c" and d[3] != ph)]
        if eng == "pe":
            deps = [d for d in deps if not (d[0] == "c" and d[1] == "pe")]
        return deps

    def _commit(self, reads, writes, tk):
        key = tk[:2] if tk[0] == "c" else tk[:3]
        for b in reads:
            b.r[key] = tk
        for b in writes:
            b.w = tk
            b.r = {}
            if tk[0] == "d":
                b.wd[key] = tk
            else:
                b.wd = {}

    def op(self, eng, fn, reads=(), writes=(), kind="c"):
        o = _Op()
        o.eng, o.fn, o.idx, o.kind = eng, fn, len(self.ops[eng]), kind
        o.deps = self._collect(eng, reads, writes)
        tk = ("c", eng, o.idx, self.phase)
        self._commit(reads, writes, tk)
        self.ops[eng].append(o)
        return tk

    def dma(self, q, out_ap, in_ap, reads=(), writes=(), final=False, **kw):
        j = self.ndma[q]
        self.ndma[q] += 1
        si, val = j % NDSEM, 16 * (j // NDSEM + 1)
        o = _Op()
        o.eng, o.idx, o.kind, o.dsem = q, len(self.ops[q]), "d", si
        o.fn = lambda e: e.dma_start(out=out_ap, in_=in_ap, **kw)
        o.deps = self._collect(q, reads, writes)
        if val > 16:
            o.deps.append(("d", q, si, val - 16))
        tk = ("d", q, si, val)
        self._commit(reads, writes, tk)
        self.ops[q].append(o)
        if final:
            self.finals.append(tk)
        return tk

    def emit(self):
        nc = self.nc
        if not any(self.ops[e] for e in ENGINES):
            return
        ph = self.phase
        deps = list(self.finals)
        self.finals = []
        for e in ENGINES:
            if e != "sp" and self.ops[e]:
                deps.append(("c", e, len(self.ops[e]) - 1, ph))
        for q in self.dsem:
            for si in range(NDSEM):
                n = (self.ndma[q] - si + NDSEM - 1) // NDSEM
                if n > 0:
                    deps.append(("d", q, si, 16 * n))
        o = _Op()
        o.eng, o.idx, o.kind, o.fn, o.deps = "sp", len(self.ops["sp"]), "w", None, deps
        self.ops["sp"].append(o)
        o = _Op()
        o.eng, o.idx, o.kind, o.fn, o.deps = "sp", len(self.ops["sp"]), "s", None, []
        self.ops["sp"].append(o)
        rel = ("c", "sp", o.idx, ph)
        for e in ENGINES:
            if e != "sp":
                o = _Op()
                o.eng, o.idx, o.kind, o.fn, o.deps = e, len(self.ops[e]), "w", None, [rel]
                self.ops[e].append(o)

        sig = {e: set() for e in ENGINES}
        for e in ENGINES:
            for o in self.ops[e]:
                for d in o.deps:
                    if d[0] == "c":
                        sig[d[1]].add(d[2])
        rank = {e: {idx: self.base[e] + r + 1 for r, idx in enumerate(sorted(sig[e]))} for e in ENGINES}
        ops = self.ops

        def run(e, eng):
            waited = {}
            for o in ops[e]:
                need = {}
                for d in o.deps:
                    if d[0] == "c":
                        k, sem, val = ("c", d[1]), self.csem[d[1]], rank[d[1]][d[2]]
                    else:
                        k, sem, val = ("d", d[1], d[2]), self.dsem[d[1]][d[2]], d[3]
                    if need.get(k, (None, 0))[1] < val:
                        need[k] = (sem, val)
                for k, (sem, val) in need.items():
                    if waited.get(k, 0) >= val:
                        continue
                    eng.wait_ge(sem, val)
                    waited[k] = val
                if os.environ.get("PDBG"):
                    print("PDBG", self.phase, e, o.idx, o.kind, sorted((str(k), v) for k, (_, v) in need.items()),
                          "SIG=%d" % rank[e][o.idx] if o.idx in sig[e] else "")
                if o.kind == "w":
                    continue
                if o.kind == "s":
                    eng.sem_inc(self.csem[e], 1)
                    continue
                ins = o.fn(eng)
                if o.kind == "d":
                    ins.then_inc(self.dsem[e][o.dsem], 16)
                elif o.idx in sig[e]:
                    ins.then_inc(self.csem[e], 1)

        with nc.Block() as block:
            @block.tensor
            def _(eng):
                run("pe", eng)

            @block.scalar
            def _(eng):
                run("act", eng)

            @block.vector
            def _(eng):
                run("dve", eng)

            @block.gpsimd
            def _(eng):
                run("pool", eng)

            @block.sync
            def _(eng):
                run("sp", eng)

        for e in ENGINES:
            self.base[e] += len(sig[e])
        self.ops = {e: [] for e in ENGINES}
        self.phase += 1


def bcast_rows(ap, n):
    dims = [list(d) for d in ap.ap]
    if len(dims) >= 2 and dims[0][1] == 1:
        dims = dims[1:]
    return bass.AP(ap.tensor, ap.offset, [[0, n]] + dims)


TWO_PI = 2.0 * math.pi
CW1 = 6.28125
CW2 = float(np.float32(TWO_PI - CW1))
CW3 = float(TWO_PI - CW1 - float(np.float32(TWO_PI - CW1)))
PI_LO = 3.1415925


def emit_sincos(P, ang, sc, ki, kf, red, redc, eng="dve"):
    Fd = ang.t.shape[1]
    P.op(eng, lambda e: e.tensor_scalar(ki.t[:], ang.t[:], 1.0 / TWO_PI, None, op0=ALU.mult),
         reads=[ang], writes=[ki])
    P.op(eng, lambda e: e.tensor_copy(kf.t[:], ki.t[:]), reads=[ki], writes=[kf])
    P.op(eng, lambda e: e.scalar_tensor_tensor(red.t[:], kf.t[:], -CW1, ang.t[:], op0=ALU.mult, op1=ALU.add),
         reads=[kf, ang], writes=[red])
    P.op(eng, lambda e: e.scalar_tensor_tensor(red.t[:], kf.t[:], -CW2, red.t[:], op0=ALU.mult, op1=ALU.add),
         reads=[kf, red], writes=[red])
    P.op(eng, lambda e: e.scalar_tensor_tensor(red.t[:], kf.t[:], -CW3, red.t[:], op0=ALU.mult, op1=ALU.add),
         reads=[kf, red], writes=[red])
    P.op(eng, lambda e: e.tensor_scalar(kf.t[:], red.t[:], math.pi, -TWO_PI, op0=ALU.is_gt, op1=ALU.mult),
         reads=[red], writes=[kf])
    P.op(eng, lambda e: e.tensor_tensor(red.t[:], red.t[:], kf.t[:], op=ALU.add), reads=[red, kf], writes=[red])
    P.op(eng, lambda e: e.tensor_scalar(kf.t[:], red.t[:], math.pi / 2, -TWO_PI, op0=ALU.is_gt, op1=ALU.mult),
         reads=[red], writes=[kf])
    P.op(eng, lambda e: e.scalar_tensor_tensor(redc.t[:], red.t[:], math.pi / 2, kf.t[:], op0=ALU.add, op1=ALU.add),
         reads=[red, kf], writes=[redc])
    for src in (red, redc):
        P.op(eng, lambda e, s=src: e.tensor_scalar(s.t[:], s.t[:], PI_LO, -PI_LO, op0=ALU.min, op1=ALU.max),
             reads=[src], writes=[src])
    P.op("act", lambda e: e.activation(sc.t[:, 0:Fd], red.t[:], AF.Sin), reads=[red], writes=[sc])
    P.op("act", lambda e: e.activation(sc.t[:, Fd:2 * Fd], redc.t[:], AF.Sin), reads=[redc], writes=[sc])


def emit_sincos_v(P, buf, tv, sc, eng="dve", sin=True):
    ang, ki, kf, red, redc = [v.t for v in tv]
    kii = ki.bitcast(I32)
    ops = [
        lambda e: e.tensor_scalar(kii, ang, 1.0 / TWO_PI, None, op0=ALU.mult),
        lambda e: e.tensor_copy(kf, kii),
        lambda e: e.scalar_tensor_tensor(red, kf, -CW1, ang, op0=ALU.mult, op1=ALU.add),
        lambda e: e.scalar_tensor_tensor(red, kf, -(CW2 + CW3), red, op0=ALU.mult, op1=ALU.add),
        lambda e: e.tensor_scalar(kf, red, math.pi, -TWO_PI, op0=ALU.is_gt, op1=ALU.mult),
        lambda e: e.tensor_tensor(red, red, kf, op=ALU.add),
        lambda e: e.tensor_scalar(kf, red, math.pi / 2, -TWO_PI, op0=ALU.is_gt, op1=ALU.mult),
        lambda e: e.scalar_tensor_tensor(redc, red, math.pi / 2, kf, op0=ALU.add, op1=ALU.add),
        lambda e: e.tensor_scalar(red, red, PI_LO, -PI_LO, op0=ALU.min, op1=ALU.max),
        lambda e: e.tensor_scalar(redc, redc, PI_LO, -PI_LO, op0=ALU.min, op1=ALU.max),
    ]
    for f in ops:
        P.op(eng, f, reads=[buf], writes=[buf])
    if not sin:
        return
    P.op("act", lambda e: e.activation(sc.t[:, 0:128], red, AF.Sin), reads=[buf], writes=[sc])
    P.op("act", lambda e: e.activation(sc.t[:, 128:256], redc, AF.Sin), reads=[buf], writes=[sc])


DM = 1024
EPS = 1e-6
RH = 4
AH = 16
HG = 8
NGRP = AH // HG
MASKNEG = -30000.0


def bc(ap, axis, n):
    dims = [list(d) for d in ap.ap]
    dims.insert(axis, [0, n])
    return bass.AP(ap.tensor, ap.offset, dims)


def host_consts():
    c = {}
    c["identb"] = np.eye(128, dtype=np.float32).astype(ml_dtypes.bfloat16)
    c["identf"] = np.eye(128, dtype=np.float32)
    c["invf"] = (1.0 / (10000.0 ** (np.arange(0, 256, 2, dtype=np.float32) / np.float32(256)))).astype(np.float32).reshape(1, 128)
    idx = np.arange(128, dtype=np.float64)
    maskT = np.zeros((128, RH, 128), np.float64)
    qdec = np.zeros((RH, 128), np.float64)
    kdec = np.zeros((RH, 128), np.float64)
    for h in range(RH):
        g = 1.0 - 2.0 ** (-5.0 - h)
        lg = math.log(g)
        k = idx[:, None]
        cc = idx[None, :]
        same = (k // 64) == (cc // 64)
        cross = (k < 64) & (cc >= 64)
        dm = np.where(same, np.exp(lg * np.abs(cc - k)), np.where(cross, np.exp(lg * (cc - k)), 0.0))
        maskT[:, h, :] = dm * np.exp(lg * (k - cc - 128.0))
        qdec[h] = np.exp(lg * (idx + 1.0))
        kdec[h] = np.exp(lg * (127.0 - idx)) / 16.0
    c["maskT"] = maskT.astype(np.float32).reshape(128, RH * 128)
    c["qdec_row"] = qdec.astype(np.float32).reshape(1, RH * 128)
    c["kdec_row"] = kdec.astype(np.float32).reshape(1, RH * 128)
    c["kdec_col"] = kdec.T.astype(np.float32).copy()
    c["gam128"] = [float((1.0 - 2.0 ** (-5.0 - h)) ** 128) for h in range(RH)]
    return c


def attn_bias_tiles(rel_table):
    H = rel_table.shape[0]
    k = np.arange(128)[:, None]
    q = np.arange(128)[None, :]
    out = np.empty((H, 2, 128, 128), np.float32)
    for bi, koff in enumerate((-128, 0)):
        rel = q - (k + koff)
        idx = np.clip(rel, -128, 128) + 128
        t = rel_table[:, idx]
        if koff == 0:
            invalid = (k // 64) > (q // 64)
            t = np.where(invalid[None], np.float32(MASKNEG), t)
        out[:, bi] = t
    return out


def build_program(NT, do_l0=True, do_l1=True):
    S = NT * 128
    nc = bass.Bass("TRN2", target_bir_lowering=False)
    hc = host_consts()

    def din(name, shape, dtype=F32):
        return nc.dram_tensor(name, list(shape), dtype, kind="ExternalInput").ap()

    x_d = din("x", [S, DM])
    cT_d = din("cT", [128, 8])
    pos_d = din("pos", [NT, 128], I32)
    ng_d = din("norm_g", [2, DM])
    adaw_d = din("ada_w", [2, DM, 3 * DM])
    adab_d = din("ada_b", [2, 3 * DM])
    rwin_d = din("ret_w_in", [DM, 6144])
    rgn_d = din("ret_gnT", [128, 16])
    rwout_d = din("ret_w_out", [2048, DM])
    awin_d = din("att_w_in", [DM, 8192])
    aqk_d = din("att_qk_g", [128, 2])
    abias_d = din("att_bias_tiles", [AH, 2, 128, 128])
    acfar_d = din("att_cfar", [1, AH])
    awout_d = din("att_w_out", [2048, DM])
    identb_d = din("identb", [128, 128], BF16)
    identf_d = din("identf", [128, 128])
    invf_d = din("invf", [1, 128])
    maskT_d = din("maskT", [128, RH * 128])
    qdecr_d = din("qdec_row", [1, 512])
    kdecr_d = din("kdec_row", [1, 512])
    kdecc_d = din("kdec_col", [128, RH])
    if do_l0 and do_l1:
        x1_d = nc.dram_tensor("x1s", [S, DM], F32, kind="Internal").ap()
    elif do_l0:
        x1_d = None
    else:
        x1_d = x_d
    out_d = nc.dram_tensor("out", [S, DM], F32, kind="ExternalOutput").ap()

    P = Prog(nc)
    with P.ctx():
        pb = [P.ps("pb%d" % i, [128, 512], F32) for i in range(8)]
        identb = P.sb("identb", [128, 128], BF16)
        identf = P.sb("identf", [128, 128], F32)
        AB = P.sb("AB", [128, 16], F32)
        Grow = P.sb("Grow", [128, DM], F32)
        posT = P.sb("posT", [128, NT], F32)
        mh = P.sb("mh", [128, 16], F32)
        P.dma("sp", identb.t[:], identb_d[:, :], writes=[identb])
        P.dma("sp", identf.t[:], identf_d[:, :], writes=[identf])
        P.op("pool", lambda e: e.memset(mh.t[:], -0.5), writes=[mh])

        def prologue(li):
            with P.scope():
                cT = P.sb("cT", [128, 8], F32)
                cond = P.sb("cond", [128, 8], F32)
                condbc = P.sb("condbc", [128, 8, 128], F32)
                adaw = [P.sb("adaw%d" % i, [128, 3 * DM], F32) for i in range(2)]
                modrow = P.sb("modrow", [128, 3 * DM], F32)
                adab = P.sb("adab", [128, 3 * DM], F32)
                ngrow = P.sb("ngrow", [128, DM], F32)
                arow = P.sb("arow", [128, DM], F32)
                P.dma("sp", cT.t[:], cT_d[:, :], writes=[cT])
                if li == 0 or not do_l0:
                    posi = P.sb("posi", [NT, 128], I32)
                    posf = P.sb("posf", [NT, 128], F32)
                    P.dma("sp", posi.t[:], pos_d[:, :], writes=[posi])
                    P.op("dve", lambda e: e.tensor_copy(posf.t[:], posi.t[:]), reads=[posi], writes=[posf])
                    P.op("pe", lambda e: e.transpose(pb[7].t[:, 0:NT], posf.t[:, :], identf.t[0:NT, 0:NT]),
                         reads=[posf, identf], writes=[pb[7]])
                    P.op("dve", lambda e: e.tensor_copy(posT.t[:], pb[7].t[:, 0:NT]), reads=[pb[7]], writes=[posT])
                P.op("act", lambda e: e.activation(cond.t[:], cT.t[:], AF.Tanh, scale=0.5), reads=[cT], writes=[cond])
                P.op("dve", lambda e: e.tensor_scalar(cond.t[:], cond.t[:], 0.5, 0.5, op0=ALU.mult, op1=ALU.add),
                     reads=[cond], writes=[cond])
                P.op("dve", lambda e: e.tensor_tensor(cond.t[:], cond.t[:], cT.t[:], op=ALU.mult), reads=[cond, cT], writes=[cond])
                P.op("dve", lambda e: e.tensor_copy(condbc.t[:], bc(cond.t[:, :], 2, 128)), reads=[cond], writes=[condbc])
                P.dma("sp", adab.t[:], bcast_rows(adab_d[li:li + 1, :], 128), writes=[adab])
                P.dma("sp", ngrow.t[:], bcast_rows(ng_d[li:li + 1, :], 128), writes=[ngrow])
                for k in range(8):
                    aw = adaw[k % 2]
                    P.dma("sp", aw.t[:], adaw_d[li, k * 128:(k + 1) * 128, :], writes=[aw])
                    for n in range(6):
                        P.op("pe", lambda e, aw=aw, k=k, n=n: e.matmul(
                            pb[n].t[:], condbc.t[:, k, :], aw.t[:, n * 512:(n + 1) * 512],
                            start=(k == 0), stop=(k == 7)), reads=[aw, condbc], writes=[pb[n]])
                for n in range(6):
                    P.op("dve", lambda e, n=n: e.tensor_tensor(
                        modrow.t[:, n * 512:(n + 1) * 512], pb[n].t[:], adab.t[:, n * 512:(n + 1) * 512], op=ALU.add),
                        reads=[pb[n], adab], writes=[modrow])
                P.op("dve", lambda e: e.scalar_tensor_tensor(
                    arow.t[:], modrow.t[:, DM:2 * DM], 1.0, ngrow.t[:], op0=ALU.add, op1=ALU.mult),
                    reads=[modrow, ngrow], writes=[arow])
                P.op("act", lambda e: e.activation(Grow.t[:], modrow.t[:, 2 * DM:3 * DM], AF.Copy),
                     reads=[modrow], writes=[Grow])
                for j in range(16):
                    src = arow.t[:, j * 128:(j + 1) * 128] if j < 8 else modrow.t[:, (j - 8) * 128:(j - 7) * 128]
                    bank = pb[6 + (j % 2)]
                    P.op("pe", lambda e, src=src, bank=bank: e.transpose(bank.t[:, 0:128], src, identf.t[:]),
                         reads=[arow, modrow, identf], writes=[bank])
                    P.op("dve", lambda e, bank=bank, j=j: e.tensor_copy(AB.t[:, j:j + 1], bank.t[:, 0:1]),
                         reads=[bank], writes=[AB])

        if do_l0:
            prologue(0)
            for g in range(2):
                with P.scope():
                    emit_layer0(P, nc, NT, hc, g, pb, identb, AB, Grow, posT, mh,
                                dict(x=x_d, x1=(x1_d if x1_d is not None else out_d), rwin=rwin_d, rgn=rgn_d, rwout=rwout_d,
                                     invf=invf_d, maskT=maskT_d, qdecr=qdecr_d, kdecr=kdecr_d, kdecc=kdecc_d),
                                final=(not do_l1))
        if do_l1:
            prologue(1)
            for g in range(int(os.environ.get('NG', NGRP))):
                with P.scope():
                    emit_layer1(P, nc, NT, g, pb, identb, AB, Grow, mh,
                                dict(x1=x1_d, out=out_d, awin=awin_d, aqk=aqk_d, abias=abias_d, acfar=acfar_d,
                                     awout=awout_d))
    return nc


def emit_norm_stage(P, t, par, xt, src_ap, ss, xs, hT, AB, mh, pbT, identb, q="sp", part="all"):
    if part in ("all", "load"):
        P.dma(q, xt.t[:], src_ap, writes=[xt])
    if part == "load":
        return
    if part in ("all", "stat"):
        _norm_stat(P, xt, ss, xs, mh)
    if part == "stat":
        return
    _norm_tr(P, xs, hT, AB, pbT, identb)


def _norm_stat(P, xt, ss, xs, mh):
    P.op("act", lambda e: e.activation(xs.t[:], xt.t[:], AF.Square, accum_out=ss.t[:, 0:1]),
         reads=[xt], writes=[ss, xs])
    P.op("pool", lambda e: e.tensor_scalar(ss.t[:, 1:2], ss.t[:, 0:1], 1.0 / DM, EPS, op0=ALU.mult, op1=ALU.add),
         reads=[ss], writes=[ss])
    P.op("pool", lambda e: e.tensor_tensor(ss.t[:, 2:3], ss.t[:, 1:2], mh.t[:, 0:1], op=ALU.pow),
         reads=[ss, mh], writes=[ss])
    P.op("act", lambda e: e.activation(xs.t[:], xt.t[:], AF.Copy, scale=ss.t[:, 2:3]), reads=[xt, ss], writes=[xs])


def _norm_tr(P, xs, hT, AB, pbT, identb):
    pv = pbT.t[:, :].bitcast(BF16)
    for k in range(8):
        P.op("pe", lambda e, k=k: e.transpose(pv[:, k * 128:(k + 1) * 128], xs.t[:, k * 128:(k + 1) * 128], identb.t[:]),
             reads=[xs, identb], writes=[pbT])
    hv = hT.t[:, :, :]
    P.op("dve", lambda e: e.tensor_tensor(hv, pv.rearrange("p (k t) -> p k t", k=8), bc(AB.t[:, 0:8], 2, 128), op=ALU.mult),
         reads=[pbT, AB], writes=[hT])
    P.op("dve", lambda e: e.tensor_tensor(hv, hv, bc(AB.t[:, 8:16], 2, 128), op=ALU.add),
         reads=[hT, AB], writes=[hT])


def emit_layer0(P, nc, NT, hc, g, pb, identb, AB, Grow, posT, mh, D, final):
    HP = 2
    first = (g == 0)
    h0 = g * HP
    gam128 = hc["gam128"]
    Win = P.sb("Win0", [128, 8, 3072], BF16)
    Wout = P.sb("Wout0", [128, 8, DM], BF16)
    gnT = P.sb("gnT", [128, 16], F32)
    invr = P.sb("invr", [128, 128], F32)
    maskT = P.sb("maskT", [128, HP * 128], F32)
    qdecr = P.sb("qdecr", [128, HP * 128], F32)
    kdecr = P.sb("kdecr", [128, HP * 128], F32)
    kdecc = P.sb("kdecc", [128, RH], F32)
    S32 = [P.sb("S32_%d" % i, [128, 512], F32) for i in range(2 * HP)]
    Sbf = [P.sb("Sbf_%d" % i, [128, 512], BF16) for i in range(2 * HP)]
    NX = 4
    xt = [P.sb("xt%d" % i, [128, DM], F32) for i in range(NX)]
    ot = [P.sb("ot%d" % i, [128, DM], F32) for i in range(NX)] if not first else None
    ss = [P.sb("ss%d" % i, [128, 8], F32) for i in range(NX)]
    xs = P.sb("xs", [128, DM], BF16)
    hT2 = [P.sb("hT%d" % i, [128, 8, 128], BF16) for i in range(2)]
    sc2 = [P.sb("sc%d" % i, [128, 256], F32) for i in range(2)]
    stmp = P.sb("stmp", [128, 5, 128], F32)
    rtmp = P.sb("rtmp", [128, 4, 256], F32)
    qk2 = [P.sb("qk%d" % i, [128, 1024], BF16) for i in range(2)]
    v2 = [P.sb("v%d" % i, [128, 1024], BF16) for i in range(2)]
    sg2 = [P.sb("sg%d" % i, [128, 1024], BF16) for i in range(2)]
    kd = P.sb("kd", [128, 512], BF16)
    qdT = P.sb("qdT", [128, 4, 128], BF16)
    kdT = P.sb("kdT", [128, 4, 128], BF16)
    sT = P.sb("sT", [128, HP, 128], BF16)
    y = P.sb("y", [128, 1024], BF16)
    yT = P.sb("yT", [128, 8, 128], BF16)
    junko = P.sb("junko", [128, 512], BF16)
    osq = P.sb("osq", [128, 8], F32)

    class _V:
        def __init__(self, ap):
            self.t = ap
    tmpv = [_V(stmp.t[:, i, :]) for i in range(5)]

    P.dma("sp", gnT.t[:], D["rgn"][:, :], writes=[gnT])
    P.dma("sp", invr.t[:], bcast_rows(D["invf"], 128), writes=[invr])
    P.dma("sp", maskT.t[:], D["maskT"][:, h0 * 128:(h0 + HP) * 128], writes=[maskT])
    P.dma("sp", qdecr.t[:], bcast_rows(D["qdecr"][0:1, h0 * 128:(h0 + HP) * 128], 128), writes=[qdecr])
    P.dma("sp", kdecr.t[:], bcast_rows(D["kdecr"][0:1, h0 * 128:(h0 + HP) * 128], 128), writes=[kdecr])
    P.dma("sp", kdecc.t[:], D["kdecc"][:, :], writes=[kdecc])
    slabs = [(0, h0 * 256, 512), (512, 1024 + h0 * 256, 512), (1024, 2048 + h0 * 512, 1024), (2048, 4096 + h0 * 512, 1024)]
    for k in range(8):
        for (dst0, src0, wdt) in slabs:
            P.dma("pool", Win.t[:, k, dst0:dst0 + wdt], D["rwin"][k * 128:(k + 1) * 128, src0:src0 + wdt],
                  writes=[Win], max_dma_last_dim=4096)
    for ch in range(8):
        w = xt[ch % 2]
        cg = g * 8 + ch
        P.dma("sp", w.t[:], D["rwout"][cg * 128:(cg + 1) * 128, :], writes=[w])
        P.op("dve", lambda e, w=w, ch=ch, cg=cg: e.scalar_tensor_tensor(
            Wout.t[:, ch, :], w.t[:], gnT.t[:, cg:cg + 1], Grow.t[:], op0=ALU.mult, op1=ALU.mult),
            reads=[w, gnT, Grow], writes=[Wout])
    for i in range(2 * HP):
        P.op("pool", lambda e, i=i: e.memset(S32[i].t[:], 0.0), writes=[S32[i]])
        P.op("pool", lambda e, i=i: e.memset(Sbf[i].t[:], 0.0), writes=[Sbf[i]])

    pbP = [pb[0], pb[1]]
    pbT = [pb[2], pb[3]]
    pbS = pb[4]
    pbO = [pb[5], pb[6]]
    pbU = pb[7]
    x1buf = TBuf("x1dram")

    def s1_chunks(t):
        par = t % 2
        X = xt[t % NX]
        hT = hT2[par]
        sc = sc2[par]
        qk, v, sg = qk2[par], v2[par], sg2[par]
        sin_b = bc(sc.t[:, 0:128], 1, 2)
        cos_b = bc(sc.t[:, 128:256], 1, 2)

        def c0a():
            emit_norm_stage(P, t, par, X, D["x"][t * 128:(t + 1) * 128, :], ss[t % NX], xs, hT, AB, mh, pbT[0], identb, part="load")
            if not first:
                P.dma("sp", ot[t % NX].t[:], D["x1"][t * 128:(t + 1) * 128, :], reads=[x1buf], writes=[ot[t % NX]])

        def c0b():
            emit_norm_stage(P, t, par, X, None, ss[t % NX], xs, hT, AB, mh, pbT[0], identb, part="stat")

        def c0():
            emit_norm_stage(P, t, par, X, None, ss[t % NX], xs, hT, AB, mh, pbT[0], identb, part="tr")
            P.op("act", lambda e: e.activation(sc.t[:, 0:128], tmpv[3].t, AF.Sin), reads=[stmp], writes=[sc])
            P.op("act", lambda e: e.activation(sc.t[:, 128:256], tmpv[4].t, AF.Sin), reads=[stmp], writes=[sc])

        def csc():
            angv = tmpv[0]
            P.op("dve", lambda e: e.tensor_scalar(angv.t, invr.t[:], posT.t[:, t:t + 1], None, op0=ALU.mult),
                 reads=[invr, posT], writes=[stmp])
            emit_sincos_v(P, stmp, tmpv, sc, sin=False)

        def blk(n):
            bank = pbP[n % 2]
            for k in range(8):
                P.op("pe", lambda e, k=k: e.matmul(
                    bank.t[:], hT.t[:, k, :], Win.t[:, k, n * 512:(n + 1) * 512], start=(k == 0), stop=(k == 7)),
                    reads=[hT, Win], writes=[bank])
            if n < 2:
                pv4 = bank.t[:, :].rearrange("p (h t d) -> p h t d", h=2, t=2)
                t1, t2 = pv4[:, :, 0, :], pv4[:, :, 1, :]
                r = [rtmp.t[:, i, :].rearrange("p (h d) -> p h d", h=2) for i in range(4)]
                P.op("dve", lambda e: e.tensor_tensor(r[0], t1, cos_b, op=ALU.mult), reads=[bank, sc], writes=[rtmp])
                P.op("dve", lambda e: e.tensor_tensor(r[1], t2, sin_b, op=ALU.mult), reads=[bank, sc], writes=[rtmp])
                P.op("dve", lambda e: e.tensor_tensor(r[2], t1, sin_b, op=ALU.mult), reads=[bank, sc], writes=[rtmp])
                P.op("dve", lambda e: e.tensor_tensor(r[3], t2, cos_b, op=ALU.mult), reads=[bank, sc], writes=[rtmp])
                ov = qk.t[:, n * 512:(n + 1) * 512].rearrange("p (h t d) -> p h t d", h=2, t=2)
                P.op("pool", lambda e: e.tensor_tensor(ov[:, :, 0, :], r[0], r[1], op=ALU.subtract),
                     reads=[rtmp], writes=[qk])
                P.op("pool", lambda e: e.tensor_tensor(ov[:, :, 1, :], r[2], r[3], op=ALU.add),
                     reads=[rtmp], writes=[qk])
            elif n < 4:
                j = n - 2
                P.op("act", lambda e: e.activation(v.t[:, j * 512:(j + 1) * 512], bank.t[:], AF.Copy),
                     reads=[bank], writes=[v])
            else:
                j = n - 4
                P.op("act", lambda e: e.activation(sg.t[:, j * 512:(j + 1) * 512], bank.t[:], AF.Silu),
                     reads=[bank], writes=[sg])
        return [c0] + [(lambda n=n: blk(n)) for n in range(6)] + [c0a, c0b, csc]

    def s2_chunks(t):
        par = t % 2
        X = xt[t % NX]
        qk, v, sg = qk2[par], v2[par], sg2[par]

        def d0():
            for hh in range(HP):
                P.op("dve", lambda e, hh=hh: e.tensor_scalar(
                    kd.t[:, hh * 256:(hh + 1) * 256], qk.t[:, 512 + hh * 256:512 + (hh + 1) * 256],
                    kdecc.t[:, h0 + hh:h0 + hh + 1], None, op0=ALU.mult), reads=[qk, kdecc], writes=[kd])
            bank = pbT[1]
            pv = bank.t[:, :].bitcast(BF16)
            for j in range(8):
                P.op("pe", lambda e, j=j: e.transpose(
                    pv[:, j * 128:(j + 1) * 128], qk.t[:, j * 128:(j + 1) * 128], identb.t[:]),
                    reads=[qk, identb], writes=[bank])
            for which, (dst, dec) in enumerate(((qdT, qdecr), (kdT, kdecr))):
                decb = bc(dec.t[:, :].rearrange("p (h t) -> p h t", h=HP), 2, 2)
                src = pv[:, which * 512:(which + 1) * 512].rearrange("p (h c t) -> p h c t", h=HP, c=2)
                P.op("dve", lambda e, dst=dst, decb=decb, src=src: e.tensor_tensor(
                    dst.t[:, :, :].rearrange("p (h c) t -> p h c t", h=HP), src, decb, op=ALU.mult),
                    reads=[bank, dec], writes=[dst])

        def d1():
            for hh in range(HP):
                for ch in range(2):
                    P.op("pe", lambda e, hh=hh, ch=ch: e.matmul(
                        pbS.t[:, hh * 128:(hh + 1) * 128], kdT.t[:, 2 * hh + ch, :], qdT.t[:, 2 * hh + ch, :],
                        start=(ch == 0), stop=(ch == 1)), reads=[kdT, qdT], writes=[pbS])
            P.op("dve", lambda e: e.tensor_tensor(sT.t[:, :, :].rearrange("p a b -> p (a b)"), pbS.t[:, 0:HP * 128], maskT.t[:], op=ALU.mult),
                 reads=[pbS, maskT], writes=[sT])

        def head(hh):
            ob = pbO[hh % 2]
            vh = v.t[:, hh * 512:(hh + 1) * 512]
            P.op("pe", lambda e: e.matmul(ob.t[:], sT.t[:, hh, :], vh, start=True, stop=False),
                 reads=[sT, v], writes=[ob])
            for ch in range(2):
                i = 2 * hh + ch
                P.op("pe", lambda e, i=i, ch=ch: e.matmul(ob.t[:], qdT.t[:, i, :], Sbf[i].t[:], start=False, stop=(ch == 1)),
                     reads=[qdT, Sbf[i]], writes=[ob])
            P.op("act", lambda e: e.activation(junko.t[:], ob.t[:], AF.Square, accum_out=osq.t[:, hh:hh + 1]),
                 reads=[ob], writes=[osq])
            P.op("pool", lambda e: e.tensor_scalar(osq.t[:, 4 + hh:5 + hh], osq.t[:, hh:hh + 1], 1.0 / 512, EPS, op0=ALU.mult, op1=ALU.add),
                 reads=[osq], writes=[osq])
            P.op("pool", lambda e: e.tensor_tensor(osq.t[:, 4 + hh:5 + hh], osq.t[:, 4 + hh:5 + hh], mh.t[:, 0:1], op=ALU.pow),
                 reads=[osq, mh], writes=[osq])
            for ch in range(2):
                i = 2 * hh + ch
                ub = pbU if ch == 0 else pbT[0]
                P.op("pe", lambda e, ch=ch, ub=ub: e.matmul(
                    ub.t[:], kd.t[:, hh * 256 + ch * 128: hh * 256 + (ch + 1) * 128], vh, start=True, stop=True),
                    reads=[kd, v], writes=[ub])
                P.op("dve", lambda e, i=i, ub=ub: e.scalar_tensor_tensor(
                    S32[i].t[:], S32[i].t[:], gam128[h0 + hh], ub.t[:], op0=ALU.mult, op1=ALU.add),
                    reads=[S32[i], ub], writes=[S32[i]])
                P.op("act", lambda e, i=i: e.activation(Sbf[i].t[:], S32[i].t[:], AF.Copy), reads=[S32[i]], writes=[Sbf[i]])

        def ygate(hh):
            ob = pbO[hh % 2]
            P.op("dve", lambda e: e.scalar_tensor_tensor(
                y.t[:, hh * 512:(hh + 1) * 512], ob.t[:], osq.t[:, 4 + hh:5 + hh], sg.t[:, hh * 512:(hh + 1) * 512],
                op0=ALU.mult, op1=ALU.mult), reads=[ob, osq, sg], writes=[y])

        def ytr():
            bank = pbT[1]
            pv = bank.t[:, :].bitcast(BF16)
            for j in range(8):
                P.op("pe", lambda e, j=j: e.transpose(
                    pv[:, j * 128:(j + 1) * 128], y.t[:, j * 128:(j + 1) * 128], identb.t[:]),
                    reads=[y, identb], writes=[bank])
            P.op("act", lambda e: e.activation(yT.t[:, :, :].rearrange("p a b -> p (a b)"), pv, AF.Copy), reads=[bank], writes=[yT])

        def oproj(n):
            acc = X if first else ot[t % NX]
            bank = pbP[n]
            for ch in range(8):
                P.op("pe", lambda e, ch=ch: e.matmul(
                    bank.t[:], yT.t[:, ch, :], Wout.t[:, ch, n * 512:(n + 1) * 512], start=(ch == 0), stop=(ch == 7)),
                    reads=[yT, Wout], writes=[bank])
            P.op("dve", lambda e: e.tensor_tensor(
                acc.t[:, n * 512:(n + 1) * 512], bank.t[:], acc.t[:, n * 512:(n + 1) * 512], op=ALU.add),
                reads=[bank, acc], writes=[acc])
            if n == 1:
                P.dma("sp", D["x1"][t * 128:(t + 1) * 128, :], acc.t[:], reads=[acc], writes=[x1buf],
                      final=(final and g == 1))

        return [d0, d1, lambda: head(0), lambda: head(1), lambda: ygate(0), lambda: ygate(1),
                ytr, lambda: oproj(0), lambda: oproj(1)]

    order = [("d", 0), ("c", 1), ("p", 6), ("d", 1), ("c", 2), ("p", 7), ("b", 0), ("d", 2), ("p", 8), ("a", 0), ("c", 3), ("d", 3),
             ("d", 4), ("c", 4), ("s", 0), ("d", 5), ("c", 5), ("e", 0), ("c", 6)]
    s1 = {}
    for tt in range(min(3, NT)):
        s1[tt] = s1_chunks(tt)
        s1[tt][7]()
    for tt in range(min(2, NT)):
        s1[tt][8]()
        s1[tt][9]()
        s1[tt][0]()
    for f in s1[0][1:7]:
        f()
    prev = None
    for t in range(NT):
        dch = s2_chunks(t)
        cch = s1.get(t + 1)
        if t + 3 < NT:
            s1[t + 3] = s1_chunks(t + 3)
        for kind, i in order:
            if kind == "d":
                dch[i]()
            elif kind == "p":
                if prev is not None:
                    prev[i]()
            elif kind == "c":
                if cch is not None:
                    cch[i]()
            elif kind == "a":
                if t + 3 < NT:
                    s1[t + 3][7]()
            elif kind == "b":
                if t + 2 < NT:
                    s1[t + 2][8]()
            elif kind == "s":
                if t + 2 < NT:
                    s1[t + 2][9]()
            elif t + 2 < NT:
                s1[t + 2][0]()
        s1.pop(t, None)
        prev = dch
    for i in (6, 7, 8):
        prev[i]()


def emit_layer1(P, nc, NT, g, pb, identb, AB, Grow, mh, D):
    first = (g == 0)
    last = (g == NGRP - 1)
    W = HG * 128
    NR = 6
    Win = P.sb("Win1", [128, 8, 4 * W], BF16)
    Wout = P.sb("Wout1", [128, HG, DM], BF16)
    qkg = P.sb("qkg", [128, 2], F32)
    GG = P.sb("GG", [128, 1], F32)
    Bn = P.sb("Bn", [128, HG, 2, 128], F32)
    cfar = P.sb("cfar", [128, HG], F32)
    kTc = [P.sb("kTc%d" % i, [128, HG, 128], BF16) for i in range(NR)]
    vxc = [P.sb("vxc%d" % i, [128, HG, 130], BF16) for i in range(NR)]
    pT = [P.sb("pT%d" % i, [128, 5, 128], BF16) for i in range(2)]
    NX = 4
    xt = [P.sb("xt%d" % i, [128, DM], F32) for i in range(NX)]
    ot = [P.sb("ot%d" % i, [128, DM], F32) for i in range(NX)] if not first else None
    ss = [P.sb("ss%d" % i, [128, 8], F32) for i in range(NX)]
    xs2 = [P.sb("xs%d" % i, [128, DM], BF16) for i in range(2)]
    hT2 = [P.sb("hT%d" % i, [128, 8, 128], BF16) for i in range(2)]
    qkr = P.sb("qkr", [128, 2 * W], BF16)
    qkn = P.sb("qkn", [128, 2 * W], BF16)
    sq = [P.sb("sq%d" % i, [128, 2 * HG], F32) for i in range(2)]
    rq = [P.sb("rq%d" % i, [128, 2 * HG], F32) for i in range(2)]
    th = P.sb("th", [128, 512], F32)
    u = [P.sb("u%d" % i, [128, W], BF16) for i in range(2)]
    qT = [P.sb("qT%d" % i, [128, HG, 128], BF16) for i in range(2)]
    stmp = [P.sb("stmp%d" % i, [128, 256], F32) for i in range(2)]
    rinv = P.sb("rinv", [128, HG], F32)
    y = P.sb("y", [128, W], BF16)
    yT = P.sb("yT", [128, HG, 128], BF16)
    sqv = [P.sb("sqv%d" % i, [128, 512], F32) for i in range(2)]
    qkrb = [TBuf("qkr%d" % i, qkr.t) for i in range(4)]

    for k in range(8):
        for j in range(4):
            P.dma("pool", Win.t[:, k, j * W:(j + 1) * W],
                  D["awin"][k * 128:(k + 1) * 128, j * 2048 + g * W: j * 2048 + (g + 1) * W],
                  writes=[Win], max_dma_last_dim=4096)
    for h in range(HG):
        w = xt[h % 2]
        r0 = (g * HG + h) * 128
        P.dma("sp", w.t[:], D["awout"][r0:r0 + 128, :], writes=[w])
        P.op("dve", lambda e, w=w, h=h: e.scalar_tensor_tensor(
            Wout.t[:, h, :], w.t[:], 0.5, Grow.t[:], op0=ALU.mult, op1=ALU.mult), reads=[w, Grow], writes=[Wout])
    P.dma("sp", qkg.t[:], D["aqk"][:, :], writes=[qkg])
    P.op("dve", lambda e: e.scalar_tensor_tensor(GG.t[:], qkg.t[:, 0:1], 128.0 ** -0.5, qkg.t[:, 1:2], op0=ALU.mult, op1=ALU.mult),
         reads=[qkg], writes=[GG])
    P.dma("sp", Bn.t[:], D["abias"][g * HG:(g + 1) * HG].rearrange("h b p c -> p h b c"), writes=[Bn])
    P.dma("sp", cfar.t[:], bcast_rows(D["acfar"][0:1, g * HG:(g + 1) * HG], 128), writes=[cfar])
    for i in range(NR):
        P.op("pool", lambda e, i=i: e.memset(vxc[i].t[:, :, 128:130], 1.0), writes=[vxc[i]])
    for i in range(2):
        P.op("pool", lambda e, i=i: e.memset(pT[i].t[:], 0.0), writes=[pT[i]])

    pbP = [pb[0], pb[1]]
    pbT = pb[2]
    pbF = [pb[3], pb[4]]
    pbN = [TBuf("pbN0", pb[5].t), TBuf("pbN1", pb[5].t)]
    pbV = [pb[6], pb[7]]
    outbuf = TBuf("outdram")
    src_d = D["x1"]
    pv = pbT.t[:, :].bitcast(BF16)

    def s1_chunks(t):
        par = t % 2
        X = xt[t % NX]
        hT = hT2[par]
        slot = t % NR
        SQ, RQ, U, QT = sq[par], rq[par], u[par], qT[par]

        def c0a():
            emit_norm_stage(P, t, par, X, src_d[t * 128:(t + 1) * 128, :], ss[t % NX], xs2[par], hT, AB, mh, pbT, identb, part="load")
            if not first:
                P.dma("sp", ot[t % NX].t[:], D["out"][t * 128:(t + 1) * 128, :], reads=[outbuf], writes=[ot[t % NX]])

        def c0b():
            emit_norm_stage(P, t, par, X, None, ss[t % NX], xs2[par], hT, AB, mh, pbT, identb, part="stat")

        def c0():
            emit_norm_stage(P, t, par, X, None, ss[t % NX], xs2[par], hT, AB, mh, pbT, identb, part="tr")

        def blk(n):
            bank = pbP[n % 2]
            for k in range(8):
                P.op("pe", lambda e, k=k: e.matmul(
                    bank.t[:], hT.t[:, k, :], Win.t[:, k, n * 512:(n + 1) * 512], start=(k == 0), stop=(k == 7)),
                    reads=[hT, Win], writes=[bank])
            if 1 <= n <= 4:
                m = n - 1
                svm = sqv[m % 2]
                P.op("dve", lambda e: e.tensor_reduce(SQ.t[:, m * 4:(m + 1) * 4], svm.t[:, :].rearrange("p (h d) -> p h d", h=4),
                                                      op=ALU.add, axis=mybir.AxisListType.X),
                     reads=[svm], writes=[SQ])
            if n < 4:
                qb = qkrb[n]
                P.op("dve", lambda e: e.tensor_copy(qkr.t[:, n * 512:(n + 1) * 512], bank.t[:]),
                     reads=[bank], writes=[qb])
                sv = sqv[n % 2]
                P.op("pool", lambda e: e.tensor_tensor(sv.t[:], qkr.t[:, n * 512:(n + 1) * 512], qkr.t[:, n * 512:(n + 1) * 512], op=ALU.mult),
                     reads=[qb], writes=[sv])
            elif n < 6:
                j = n - 4
                P.op("act", lambda e: e.activation(
                    vxc[slot].t[:, j * 4:(j + 1) * 4, 0:128], bank.t[:, :].rearrange("p (h d) -> p h d", h=4), AF.Copy),
                    reads=[bank], writes=[vxc[slot]])
            else:
                j = n - 6
                P.op("act", lambda e: e.activation(th.t[:], bank.t[:], AF.Tanh, scale=0.5), reads=[bank], writes=[th])
                P.op("dve", lambda e: e.scalar_tensor_tensor(
                    U.t[:, j * 512:(j + 1) * 512], th.t[:], 1.0, bank.t[:], op0=ALU.add, op1=ALU.mult),
                    reads=[th, bank], writes=[U])

        def c9a():
            P.op("pool", lambda e: e.tensor_scalar(RQ.t[:], SQ.t[:], 1.0 / 128, EPS, op0=ALU.mult, op1=ALU.add), reads=[SQ], writes=[RQ])
            P.op("pool", lambda e: e.tensor_tensor(RQ.t[:], RQ.t[:], mh.t[:, 0:16], op=ALU.pow), reads=[RQ, mh], writes=[RQ])

        def c9d():
            for hf in range(2):
                P.op("dve", lambda e, hf=hf: e.tensor_tensor(
                    qkn.t[:, hf * W:(hf + 1) * W].rearrange("p (h d) -> p h d", h=HG),
                    qkr.t[:, hf * W:(hf + 1) * W].rearrange("p (h d) -> p h d", h=HG),
                    bc(RQ.t[:, hf * HG:(hf + 1) * HG], 2, 128), op=ALU.mult), reads=[qkrb[2 * hf], qkrb[2 * hf + 1], RQ], writes=[qkn])

        def c9b():
            for j in range(HG):
                P.op("pe", lambda e, j=j: e.transpose(pv[:, j * 128:(j + 1) * 128], qkn.t[:, j * 128:(j + 1) * 128], identb.t[:]),
                     reads=[qkn, identb], writes=[pbT])
            P.op("dve", lambda e: e.tensor_scalar(QT.t[:, :, :].rearrange("p a b -> p (a b)"), pv, GG.t[:, 0:1], None, op0=ALU.mult),
                 reads=[pbT, GG], writes=[QT])

        def c10k():
            for j in range(HG):
                P.op("pe", lambda e, j=j: e.transpose(pv[:, j * 128:(j + 1) * 128], qkn.t[:, W + j * 128: W + (j + 1) * 128], identb.t[:]),
                     reads=[qkn, identb], writes=[pbT])
            P.op("dve", lambda e: e.tensor_copy(kTc[slot].t[:, :, :].rearrange("p a b -> p (a b)"), pv),
                 reads=[pbT], writes=[kTc[slot]])

        return [c0] + [(lambda n=n: blk(n)) for n in range(8)] + [c9a, c9b, c0a, c0b, c9d, c10k]

    def s2_chunks(t):
        par = t % 2
        X = xt[t % NX]
        U, QT = u[par], qT[par]
        blocks = [i for i in range(5) if t - 4 + i >= 0]
        far = [i for i in blocks if i < 3]

        def scores(h):
            hp = h % 2
            fb, nb = pbF[hp], pbN[hp]
            for i in far:
                sl = (t - 4 + i) % NR
                P.op("pe", lambda e, i=i, sl=sl: e.matmul(
                    fb.t[:, i * 128:(i + 1) * 128], kTc[sl].t[:, h, :], QT.t[:, h, :], start=True, stop=True),
                    reads=[kTc[sl], QT], writes=[fb])
            for i in blocks:
                if i < 3:
                    continue
                sl = (t - 4 + i) % NR
                nc0 = hp * 256 + (i - 3) * 128
                P.op("pe", lambda e, i=i, sl=sl, nc0=nc0: e.matmul(
                    nb.t[:, nc0:nc0 + 128], kTc[sl].t[:, h, :], QT.t[:, h, :], start=True, stop=True),
                    reads=[kTc[sl], QT], writes=[nb])

        def expo(h):
            hp = h % 2
            fb, nb, pt = pbF[hp], pbN[hp], pT[hp]
            if far:
                f0 = far[0]
                cf_ = f0 * 128
                P.op("act", lambda e: e.activation(
                    pt.t[:, f0:3, :].rearrange("p a b -> p (a b)"), fb.t[:, cf_:384], AF.Exp, bias=cfar.t[:, h:h + 1]),
                    reads=[fb, cfar], writes=[pt])
                if f0 == 0:
                    P.op("pool", lambda e: e.memset(pt.t[0:64, 0, 64:128], 0.0), writes=[pt])
            nn = [i for i in blocks if i >= 3]
            n0 = nn[0] - 3
            st = stmp[hp]
            P.op("dve", lambda e: e.tensor_tensor(
                st.t[:, n0 * 128:256], nb.t[:, hp * 256 + n0 * 128:hp * 256 + 256], Bn.t[:, h, n0:2, :].rearrange("p a b -> p (a b)"), op=ALU.add),
                reads=[nb, Bn], writes=[st])
            P.op("act", lambda e: e.activation(
                pt.t[:, 3 + n0:5, :].rearrange("p a b -> p (a b)"), st.t[:, n0 * 128:256], AF.Exp), reads=[st], writes=[pt])
        def pvh(h):
            hp = h % 2
            pt = pT[hp]
            vs = pbV[hp]
            for bi, i in enumerate(blocks):
                sl = (t - 4 + i) % NR
                P.op("pe", lambda e, i=i, sl=sl, bi=bi: e.matmul(
                    vs.t[:, 0:129], pt.t[:, i, :], vxc[sl].t[:, h, 0:129], start=(bi == 0), stop=(bi == len(blocks) - 1)),
                    reads=[pt, vxc[sl]], writes=[vs])
            P.op("dve", lambda e: e.reciprocal(rinv.t[:, h:h + 1], vs.t[:, 128:129]), reads=[vs], writes=[rinv])
            P.op("dve", lambda e: e.scalar_tensor_tensor(
                y.t[:, h * 128:(h + 1) * 128], vs.t[:, 0:128], rinv.t[:, h:h + 1], U.t[:, h * 128:(h + 1) * 128],
                op0=ALU.mult, op1=ALU.mult), reads=[vs, rinv, U], writes=[y])

        def ytr():
            for j in range(HG):
                P.op("pe", lambda e, j=j: e.transpose(pv[:, j * 128:(j + 1) * 128], y.t[:, j * 128:(j + 1) * 128], identb.t[:]),
                     reads=[y, identb], writes=[pbT])
            P.op("act", lambda e: e.activation(yT.t[:, :, :].rearrange("p a b -> p (a b)"), pv, AF.Copy), reads=[pbT], writes=[yT])

        def oproj(n):
            acc = X if first else ot[t % NX]
            bank = pbP[n]
            for ch in range(HG):
                P.op("pe", lambda e, ch=ch: e.matmul(
                    bank.t[:], yT.t[:, ch, :], Wout.t[:, ch, n * 512:(n + 1) * 512], start=(ch == 0), stop=(ch == HG - 1)),
                    reads=[yT, Wout], writes=[bank])
            P.op("dve", lambda e: e.tensor_tensor(
                acc.t[:, n * 512:(n + 1) * 512], bank.t[:], acc.t[:, n * 512:(n + 1) * 512], op=ALU.add),
                reads=[bank, acc], writes=[acc])
            if n == 1:
                P.dma("sp", D["out"][t * 128:(t + 1) * 128, :], acc.t[:], reads=[acc], writes=[outbuf], final=last)

        return [(lambda h=h: scores(h)) for h in range(HG)] + [(lambda h=h: expo(h)) for h in range(HG)] + \
               [(lambda h=h: pvh(h)) for h in range(HG)] + [ytr, lambda: oproj(0), lambda: oproj(1)]

    order = [("d", 0), ("c", 1), ("p", 24), ("d", 1), ("p", 25), ("b", 0),
             ("d", 8), ("c", 2), ("d", 2), ("d", 16), ("p", 26), ("a", 0),
             ("d", 9), ("c", 3), ("d", 3), ("d", 17),
             ("d", 10), ("c", 4), ("d", 4), ("d", 18),
             ("d", 11), ("c", 5), ("c", 9), ("d", 5), ("d", 19),
             ("d", 12), ("c", 6), ("d", 6), ("d", 20),
             ("d", 13), ("c", 7), ("c", 13), ("d", 7), ("d", 21), ("e", 0),
             ("d", 14), ("c", 8), ("d", 22), ("c", 10),
             ("d", 15), ("c", 14), ("d", 23)]
    s1 = {}
    for tt in range(min(3, NT)):
        s1[tt] = s1_chunks(tt)
        s1[tt][11]()
    for tt in range(min(2, NT)):
        s1[tt][12]()
        s1[tt][0]()
    for i in (1, 2, 3, 4, 5, 6, 7, 8, 9, 13, 10, 14):
        s1[0][i]()
    prev = None
    for t in range(NT):
        dch = s2_chunks(t)
        cch = s1.get(t + 1)
        if t + 3 < NT:
            s1[t + 3] = s1_chunks(t + 3)
        for kind, i in order:
            if kind == "d":
                dch[i]()
            elif kind == "p":
                if prev is not None:
                    prev[i]()
            elif kind == "c":
                if cch is not None:
                    cch[i]()
            elif kind == "a":
                if t + 3 < NT:
                    s1[t + 3][11]()
            elif kind == "b":
                if t + 2 < NT:
                    s1[t + 2][12]()
            elif t + 2 < NT:
                s1[t + 2][0]()
        s1.pop(t, None)
        prev = dch
    for i in (24, 25, 26):
        prev[i]()


def _lay(inputs, b, NT):
    S = NT * 128
    hc = host_consts()
    f = lambda a: np.ascontiguousarray(a, dtype=np.float32)
    d = {
        "x": f(inputs["x"][b, :S]),
        "cT": f(inputs["c"][b].reshape(8, 128).T),
        "pos": np.ascontiguousarray(inputs["positions"][b, :S].reshape(NT, 128).astype(np.int32)),
        "norm_g": f(inputs["norm_g"]),
        "ada_w": f(inputs["ada_w"]),
        "ada_b": f(inputs["ada_b"]),
        "ret_w_in": f(inputs["ret_w_in"][0]),
        "ret_gnT": f(inputs["ret_gn_g"][0].reshape(16, 128).T),
        "ret_w_out": f(inputs["ret_w_out"][0]),
        "att_w_in": f(inputs["att_w_in"][0]),
        "att_qk_g": f(np.stack([inputs["att_q_g"][0], inputs["att_k_g"][0]], axis=1)),
        "att_bias_tiles": attn_bias_tiles(np.asarray(inputs["att_rel_bias"][0], np.float32)),
        "att_cfar": f(inputs["att_rel_bias"][0][:, 256].reshape(1, AH)),
        "att_w_out": f(inputs["att_w_out"][0]),
    }
    for k in ("identb", "identf", "invf", "maskT", "qdec_row", "kdec_row", "kdec_col"):
        d[k] = hc[k]
    return d


def run(inputs, NT=64, ncores=8, do_l0=True, do_l1=True):
    inputs = {k: np.asarray(v) for k, v in inputs.items()}
    nc = build_program(NT, do_l0, do_l1)
    in_maps = [_lay(inputs, b, NT) for b in range(ncores)]
    res = run_bass_kernel_spmd(nc, in_maps, core_ids=list(range(ncores)))
    return np.stack([np.asarray(r["out"]) for r in res.results], axis=0)


def kernel(**inputs):
    return run(inputs, NT=64, ncores=8)
```

```python
import math
import os
from contextlib import ExitStack, contextmanager

import numpy as np
import ml_dtypes
import concourse.bass as bass
import concourse.mybir as mybir
from concourse.bass_utils import run_bass_kernel_spmd

F32 = mybir.dt.float32
BF16 = mybir.dt.bfloat16
I32 = mybir.dt.int32
AF = mybir.ActivationFunctionType
ALU = mybir.AluOpType

ENGINES = ("pe", "act", "dve", "pool", "sp")
NDSEM = 6


class TBuf:
    __slots__ = ("name", "t", "w", "r", "wd")

    def __init__(self, name, t=None):
        self.name = name
        self.t = t
        self.w = None
        self.r = {}
        self.wd = {}


class _Op:
    __slots__ = ("eng", "fn", "idx", "deps", "kind", "dsem")


class Prog:
    def __init__(self, nc):
        self.nc = nc
        self.ops = {e: [] for e in ENGINES}
        self.ndma = {e: 0 for e in ENGINES}
        self.finals = []
        self.stack = None
        self.phase = 0
        self.base = {e: 0 for e in ENGINES}

    @contextmanager
    def ctx(self):
        with ExitStack() as st:
            self.stack = st
            self.csem = {e: st.enter_context(self.nc.semaphore("cs_" + e)) for e in ENGINES}
            self.dsem = {
                e: [st.enter_context(self.nc.semaphore("ds_%s%d" % (e, i))) for i in range(NDSEM)]
                for e in ("sp", "pool", "act")
            }
            yield self
            self.emit()

    @contextmanager
    def scope(self):
        self.emit()
        outer = self.stack
        with ExitStack() as st:
            self.stack = st
            yield self
            self.emit()
        self.stack = outer

    def sb(self, name, shape, dt):
        self.uid = getattr(self, "uid", 0) + 1
        t = self.stack.enter_context(self.nc.sbuf_tensor("sb%d_%s" % (self.uid, name), list(shape), dt))
        return TBuf(name, t)

    def ps(self, name, shape, dt):
        t = self.stack.enter_context(self.nc.psum_tensor("ps_" + name, list(shape), dt))
        return TBuf(name, t)

    def view(self, buf, name):
        return TBuf(name, buf.t)

    def _collect(self, eng, reads, writes):
        deps = []
        for b in reads:
            if b.w is not None:
                deps.append(b.w)
            deps.extend(b.wd.values())
        for b in writes:
            if b.w is not None:
                deps.append(b.w)
            deps.extend(b.wd.values())
            deps.extend(b.r.values())
        ph = self.phase
        deps = [d for d in deps if not (d[0] == "c" and d[3] != ph)]
        if eng == "pe":
            deps = [d for d in deps if not (d[0] == "c" and d[1] == "pe")]
        return deps

    def _commit(self, reads, writes, tk):
        key = tk[:2] if tk[0] == "c" else tk[:3]
        for b in reads:
            b.r[key] = tk
        for b in writes:
            b.w = tk
            b.r = {}
            if tk[0] == "d":
                b.wd[key] = tk
            else:
                b.wd = {}

    def op(self, eng, fn, reads=(), writes=(), kind="c"):
        o = _Op()
        o.eng, o.fn, o.idx, o.kind = eng, fn, len(self.ops[eng]), kind
        o.deps = self._collect(eng, reads, writes)
        tk = ("c", eng, o.idx, self.phase)
        self._commit(reads, writes, tk)
        self.ops[eng].append(o)
        return tk

    def dma(self, q, out_ap, in_ap, reads=(), writes=(), final=False, **kw):
        j = self.ndma[q]
        self.ndma[q] += 1
        si, val = j % NDSEM, 16 * (j // NDSEM + 1)
        o = _Op()
        o.eng, o.idx, o.kind, o.dsem = q, len(self.ops[q]), "d", si
        o.fn = lambda e: e.dma_start(out=out_ap, in_=in_ap, **kw)
        o.deps = self._collect(q, reads, writes)
        if val > 16:
            o.deps.append(("d", q, si, val - 16))
        tk = ("d", q, si, val)
        self._commit(reads, writes, tk)
        self.ops[q].append(o)
        if final:
            self.finals.append(tk)
        return tk

    def emit(self):
        nc = self.nc
        if not any(self.ops[e] for e in ENGINES):
            return
        ph = self.phase
        deps = list(self.finals)
        self.finals = []
        for e in ENGINES:
            if e != "sp" and self.ops[e]:
                deps.append(("c", e, len(self.ops[e]) - 1, ph))
        for q in self.dsem:
            for si in range(NDSEM):
                n = (self.ndma[q] - si + NDSEM - 1) // NDSEM
                if n > 0:
                    deps.append(("d", q, si, 16 * n))
        o = _Op()
        o.eng, o.idx, o.kind, o.fn, o.deps = "sp", len(self.ops["sp"]), "w", None, deps
        self.ops["sp"].append(o)
        o = _Op()
        o.eng, o.idx, o.kind, o.fn, o.deps = "sp", len(self.ops["sp"]), "s", None, []
        self.ops["sp"].append(o)
        rel = ("c", "sp", o.idx, ph)
        for e in ENGINES:
            if e != "sp":
                o = _Op()
                o.eng, o.idx, o.kind, o.fn, o.deps = e, len(self.ops[e]), "w", None, [rel]
                self.ops[e].append(o)

        sig = {e: set() for e in ENGINES}
        for e in ENGINES:
            for o in self.ops[e]:
                for d in o.deps:
                    if d[0] == "c":
                        sig[d[1]].add(d[2])
        rank = {e: {idx: self.base[e] + r + 1 for r, idx in enumerate(sorted(sig[e]))} for e in ENGINES}
        ops = self.ops

        def run(e, eng):
            waited = {}
            for o in ops[e]:
                need = {}
                for d in o.deps:
                    if d[0] == "c":
                        k, sem, val = ("c", d[1]), self.csem[d[1]], rank[d[1]][d[2]]
                    else:
                        k, sem, val = ("d", d[1], d[2]), self.dsem[d[1]][d[2]], d[3]
                    if need.get(k, (None, 0))[1] < val:
                        need[k] = (sem, val)
                for k, (sem, val) in need.items():
                    if waited.get(k, 0) >= val:
                        continue
                    eng.wait_ge(sem, val)
                    waited[k] = val
                if os.environ.get("PDBG"):
                    print("PDBG", self.phase, e, o.idx, o.kind, sorted((str(k), v) for k, (_, v) in need.items()),
                          "SIG=%d" % rank[e][o.idx] if o.idx in sig[e] else "")
                if o.kind == "w":
                    continue
                if o.kind == "s":
                    eng.sem_inc(self.csem[e], 1)
                    continue
                ins = o.fn(eng)
                if o.kind == "d":
                    ins.then_inc(self.dsem[e][o.dsem], 16)
                elif o.idx in sig[e]:
                    ins.then_inc(self.csem[e], 1)

        with nc.Block() as block:
            @block.tensor
            def _(eng):
                run("pe", eng)

            @block.scalar
            def _(eng):
                run("act", eng)

            @block.vector
            def _(eng):
                run("dve", eng)

            @block.gpsimd
            def _(eng):
                run("pool", eng)

            @block.sync
            def _(eng):
                run("sp", eng)

        for e in ENGINES:
            self.base[e] += len(sig[e])
        self.ops = {e: [] for e in ENGINES}
        self.phase += 1


def bcast_rows(ap, n):
    dims = [list(d) for d in ap.ap]
    if len(dims) >= 2 and dims[0][1] == 1:
        dims = dims[1:]
    return bass.AP(ap.tensor, ap.offset, [[0, n]] + dims)


TWO_PI = 2.0 * math.pi
CW1 = 6.28125
CW2 = float(np.float32(TWO_PI - CW1))
CW3 = float(TWO_PI - CW1 - float(np.float32(TWO_PI - CW1)))
PI_LO = 3.1415925


def emit_sincos(P, ang, sc, ki, kf, red, redc, eng="dve"):
    Fd = ang.t.shape[1]
    P.op(eng, lambda e: e.tensor_scalar(ki.t[:], ang.t[:], 1.0 / TWO_PI, None, op0=ALU.mult),
         reads=[ang], writes=[ki])
    P.op(eng, lambda e: e.tensor_copy(kf.t[:], ki.t[:]), reads=[ki], writes=[kf])
    P.op(eng, lambda e: e.scalar_tensor_tensor(red.t[:], kf.t[:], -CW1, ang.t[:], op0=ALU.mult, op1=ALU.add),
         reads=[kf, ang], writes=[red])
    P.op(eng, lambda e: e.scalar_tensor_tensor(red.t[:], kf.t[:], -CW2, red.t[:], op0=ALU.mult, op1=ALU.add),
         reads=[kf, red], writes=[red])
    P.op(eng, lambda e: e.scalar_tensor_tensor(red.t[:], kf.t[:], -CW3, red.t[:], op0=ALU.mult, op1=ALU.add),
         reads=[kf, red], writes=[red])
    P.op(eng, lambda e: e.tensor_scalar(kf.t[:], red.t[:], math.pi, -TWO_PI, op0=ALU.is_gt, op1=ALU.mult),
         reads=[red], writes=[kf])
    P.op(eng, lambda e: e.tensor_tensor(red.t[:], red.t[:], kf.t[:], op=ALU.add), reads=[red, kf], writes=[red])
    P.op(eng, lambda e: e.tensor_scalar(kf.t[:], red.t[:], math.pi / 2, -TWO_PI, op0=ALU.is_gt, op1=ALU.mult),
         reads=[red], writes=[kf])
    P.op(eng, lambda e: e.scalar_tensor_tensor(redc.t[:], red.t[:], math.pi / 2, kf.t[:], op0=ALU.add, op1=ALU.add),
         reads=[red, kf], writes=[redc])
    for src in (red, redc):
        P.op(eng, lambda e, s=src: e.tensor_scalar(s.t[:], s.t[:], PI_LO, -PI_LO, op0=ALU.min, op1=ALU.max),
             reads=[src], writes=[src])
    P.op("act", lambda e: e.activation(sc.t[:, 0:Fd], red.t[:], AF.Sin), reads=[red], writes=[sc])
    P.op("act", lambda e: e.activation(sc.t[:, Fd:2 * Fd], redc.t[:], AF.Sin), reads=[redc], writes=[sc])


def emit_sincos_v(P, buf, tv, sc, eng="dve", sin=True):
    ang, ki, kf, red, redc = [v.t for v in tv]
    kii = ki.bitcast(I32)
    ops = [
        lambda e: e.tensor_scalar(kii, ang, 1.0 / TWO_PI, None, op0=ALU.mult),
        lambda e: e.tensor_copy(kf, kii),
        lambda e: e.scalar_tensor_tensor(red, kf, -CW1, ang, op0=ALU.mult, op1=ALU.add),
        lambda e: e.scalar_tensor_tensor(red, kf, -(CW2 + CW3), red, op0=ALU.mult, op1=ALU.add),
        lambda e: e.tensor_scalar(kf, red, math.pi, -TWO_PI, op0=ALU.is_gt, op1=ALU.mult),
        lambda e: e.tensor_tensor(red, red, kf, op=ALU.add),
        lambda e: e.tensor_scalar(kf, red, math.pi / 2, -TWO_PI, op0=ALU.is_gt, op1=ALU.mult),
        lambda e: e.scalar_tensor_tensor(redc, red, math.pi / 2, kf, op0=ALU.add, op1=ALU.add),
        lambda e: e.tensor_scalar(red, red, PI_LO, -PI_LO, op0=ALU.min, op1=ALU.max),
        lambda e: e.tensor_scalar(redc, redc, PI_LO, -PI_LO, op0=ALU.min, op1=ALU.max),
    ]
    for f in ops:
        P.op(eng, f, reads=[buf], writes=[buf])
    if not sin:
        return
    P.op("act", lambda e: e.activation(sc.t[:, 0:128], red, AF.Sin), reads=[buf], writes=[sc])
    P.op("act", lambda e: e.activation(sc.t[:, 128:256], redc, AF.Sin), reads=[buf], writes=[sc])


DM = 1024
EPS = 1e-6
RH = 4
AH = 16
HG = 8
NGRP = AH // HG
MASKNEG = -30000.0


def bc(ap, axis, n):
    dims = [list(d) for d in ap.ap]
    dims.insert(axis, [0, n])
    return bass.AP(ap.tensor, ap.offset, dims)


def host_consts():
    c = {}
    c["identb"] = np.eye(128, dtype=np.float32).astype(ml_dtypes.bfloat16)
    c["identf"] = np.eye(128, dtype=np.float32)
    c["invf"] = (1.0 / (10000.0 ** (np.arange(0, 256, 2, dtype=np.float32) / np.float32(256)))).astype(np.float32).reshape(1, 128)
    idx = np.arange(128, dtype=np.float64)
    maskT = np.zeros((128, RH, 128), np.float64)
    qdec = np.zeros((RH, 128), np.float64)
    kdec = np.zeros((RH, 128), np.float64)
    for h in range(RH):
        g = 1.0 - 2.0 ** (-5.0 - h)
        lg = math.log(g)
        k = idx[:, None]
        cc = idx[None, :]
        same = (k // 64) == (cc // 64)
        cross = (k < 64) & (cc >= 64)
        dm = np.where(same, np.exp(lg * np.abs(cc - k)), np.where(cross, np.exp(lg * (cc - k)), 0.0))
        maskT[:, h, :] = dm * np.exp(lg * (k - cc - 128.0))
        qdec[h] = np.exp(lg * (idx + 1.0))
        kdec[h] = np.exp(lg * (127.0 - idx)) / 16.0
    c["maskT"] = maskT.astype(np.float32).reshape(128, RH * 128)
    c["qdec_row"] = qdec.astype(np.float32).reshape(1, RH * 128)
    c["kdec_row"] = kdec.astype(np.float32).reshape(1, RH * 128)
    c["kdec_col"] = kdec.T.astype(np.float32).copy()
    c["gam128"] = [float((1.0 - 2.0 ** (-5.0 - h)) ** 128) for h in range(RH)]
    return c


def attn_bias_tiles(rel_table):
    H = rel_table.shape[0]
    k = np.arange(128)[:, None]
    q = np.arange(128)[None, :]
    out = np.empty((H, 2, 128, 128), np.float32)
    for bi, koff in enumerate((-128, 0)):
        rel = q - (k + koff)
        idx = np.clip(rel, -128, 128) + 128
        t = rel_table[:, idx]
        if koff == 0:
            invalid = (k // 64) > (q // 64)
            t = np.where(invalid[None], np.float32(MASKNEG), t)
        out[:, bi] = t
    return out


def build_program(NT, do_l0=True, do_l1=True):
    S = NT * 128
    nc = bass.Bass("TRN2", target_bir_lowering=False)
    hc = host_consts()

    def din(name, shape, dtype=F32):
        return nc.dram_tensor(name, list(shape), dtype, kind="ExternalInput").ap()

    x_d = din("x", [S, DM])
    cT_d = din("cT", [128, 8])
    pos_d = din("pos", [NT, 128], I32)
    ng_d = din("norm_g", [2, DM])
    adaw_d = din("ada_w", [2, DM, 3 * DM])
    adab_d = din("ada_b", [2, 3 * DM])
    rwin_d = din("ret_w_in", [DM, 6144])
    rgn_d = din("ret_gnT", [128, 16])
    rwout_d = din("ret_w_out", [2048, DM])
    awin_d = din("att_w_in", [DM, 8192])
    aqk_d = din("att_qk_g", [128, 2])
    abias_d = din("att_bias_tiles", [AH, 2, 128, 128])
    acfar_d = din("att_cfar", [1, AH])
    awout_d = din("att_w_out", [2048, DM])
    identb_d = din("identb", [128, 128], BF16)
    identf_d = din("identf", [128, 128])
    invf_d = din("invf", [1, 128])
    maskT_d = din("maskT", [128, RH * 128])
    qdecr_d = din("qdec_row", [1, 512])
    kdecr_d = din("kdec_row", [1, 512])
    kdecc_d = din("kdec_col", [128, RH])
    if do_l0 and do_l1:
        x1_d = nc.dram_tensor("x1s", [S, DM], F32, kind="Internal").ap()
    elif do_l0:
        x1_d = None
    else:
        x1_d = x_d
    out_d = nc.dram_tensor("out", [S, DM], F32, kind="ExternalOutput").ap()

    P = Prog(nc)
    with P.ctx():
        pb = [P.ps("pb%d" % i, [128, 512], F32) for i in range(8)]
        identb = P.sb("identb", [128, 128], BF16)
        identf = P.sb("identf", [128, 128], F32)
        AB = P.sb("AB", [128, 16], F32)
        Grow = P.sb("Grow", [128, DM], F32)
        posT = P.sb("posT", [128, NT], F32)
        mh = P.sb("mh", [128, 16], F32)
        P.dma("sp", identb.t[:], identb_d[:, :], writes=[identb])
        P.dma("sp", identf.t[:], identf_d[:, :], writes=[identf])
        P.op("pool", lambda e: e.memset(mh.t[:], -0.5), writes=[mh])

        def prologue(li):
            with P.scope():
                cT = P.sb("cT", [128, 8], F32)
                cond = P.sb("cond", [128, 8], F32)
                condbc = P.sb("condbc", [128, 8, 128], F32)
                adaw = [P.sb("adaw%d" % i, [128, 3 * DM], F32) for i in range(2)]
                modrow = P.sb("modrow", [128, 3 * DM], F32)
                adab = P.sb("adab", [128, 3 * DM], F32)
                ngrow = P.sb("ngrow", [128, DM], F32)
                arow = P.sb("arow", [128, DM], F32)
                P.dma("sp", cT.t[:], cT_d[:, :], writes=[cT])
                if li == 0 or not do_l0:
                    posi = P.sb("posi", [NT, 128], I32)
                    posf = P.sb("posf", [NT, 128], F32)
                    P.dma("sp", posi.t[:], pos_d[:, :], writes=[posi])
                    P.op("dve", lambda e: e.tensor_copy(posf.t[:], posi.t[:]), reads=[posi], writes=[posf])
                    P.op("pe", lambda e: e.transpose(pb[7].t[:, 0:NT], posf.t[:, :], identf.t[0:NT, 0:NT]),
                         reads=[posf, identf], writes=[pb[7]])
                    P.op("dve", lambda e: e.tensor_copy(posT.t[:], pb[7].t[:, 0:NT]), reads=[pb[7]], writes=[posT])
                P.op("act", lambda e: e.activation(cond.t[:], cT.t[:], AF.Tanh, scale=0.5), reads=[cT], writes=[cond])
                P.op("dve", lambda e: e.tensor_scalar(cond.t[:], cond.t[:], 0.5, 0.5, op0=ALU.mult, op1=ALU.add),
                     reads=[cond], writes=[cond])
                P.op("dve", lambda e: e.tensor_tensor(cond.t[:], cond.t[:], cT.t[:], op=ALU.mult), reads=[cond, cT], writes=[cond])
                P.op("dve", lambda e: e.tensor_copy(condbc.t[:], bc(cond.t[:, :], 2, 128)), reads=[cond], writes=[condbc])
                P.dma("sp", adab.t[:], bcast_rows(adab_d[li:li + 1, :], 128), writes=[adab])
                P.dma("sp", ngrow.t[:], bcast_rows(ng_d[li:li + 1, :], 128), writes=[ngrow])
                for k in range(8):
                    aw = adaw[k % 2]
                    P.dma("sp", aw.t[:], adaw_d[li, k * 128:(k + 1) * 128, :], writes=[aw])
                    for n in range(6):
                        P.op("pe", lambda e, aw=aw, k=k, n=n: e.matmul(
                            pb[n].t[:], condbc.t[:, k, :], aw.t[:, n * 512:(n + 1) * 512],
                            start=(k == 0), stop=(k == 7)), reads=[aw, condbc], writes=[pb[n]])
                for n in range(6):
                    P.op("dve", lambda e, n=n: e.tensor_tensor(
                        modrow.t[:, n * 512:(n + 1) * 512], pb[n].t[:], adab.t[:, n * 512:(n + 1) * 512], op=ALU.add),
                        reads=[pb[n], adab], writes=[modrow])
                P.op("dve", lambda e: e.scalar_tensor_tensor(
                    arow.t[:], modrow.t[:, DM:2 * DM], 1.0, ngrow.t[:], op0=ALU.add, op1=ALU.mult),
                    reads=[modrow, ngrow], writes=[arow])
                P.op("act", lambda e: e.activation(Grow.t[:], modrow.t[:, 2 * DM:3 * DM], AF.Copy),
                     reads=[modrow], writes=[Grow])
                for j in range(16):
                    src = arow.t[:, j * 128:(j + 1) * 128] if j < 8 else modrow.t[:, (j - 8) * 128:(j - 7) * 128]
                    bank = pb[6 + (j % 2)]
                    P.op("pe", lambda e, src=src, bank=bank: e.transpose(bank.t[:, 0:128], src, identf.t[:]),
                         reads=[arow, modrow, identf], writes=[bank])
                    P.op("dve", lambda e, bank=bank, j=j: e.tensor_copy(AB.t[:, j:j + 1], bank.t[:, 0:1]),
                         reads=[bank], writes=[AB])

        if do_l0:
            prologue(0)
            for g in range(2):
                with P.scope():
                    emit_layer0(P, nc, NT, hc, g, pb, identb, AB, Grow, posT, mh,
                                dict(x=x_d, x1=(x1_d if x1_d is not None else out_d), rwin=rwin_d, rgn=rgn_d, rwout=rwout_d,
                                     invf=invf_d, maskT=maskT_d, qdecr=qdecr_d, kdecr=kdecr_d, kdecc=kdecc_d),
                                final=(not do_l1))
        if do_l1:
            prologue(1)
            for g in range(int(os.environ.get('NG', NGRP))):
                with P.scope():
                    emit_layer1(P, nc, NT, g, pb, identb, AB, Grow, mh,
                                dict(x1=x1_d, out=out_d, awin=awin_d, aqk=aqk_d, abias=abias_d, acfar=acfar_d,
                                     awout=awout_d))
    return nc


def emit_norm_stage(P, t, par, xt, src_ap, ss, xs, hT, AB, mh, pbT, identb, q="sp", part="all"):
    if part in ("all", "load"):
        P.dma(q, xt.t[:], src_ap, writes=[xt])
    if part == "load":
        return
    if part in ("all", "stat"):
        _norm_stat(P, xt, ss, xs, mh)
    if part == "stat":
        return
    _norm_tr(P, xs, hT, AB, pbT, identb)


def _norm_stat(P, xt, ss, xs, mh):
    P.op("act", lambda e: e.activation(xs.t[:], xt.t[:], AF.Square, accum_out=ss.t[:, 0:1]),
         reads=[xt], writes=[ss, xs])
    P.op("pool", lambda e: e.tensor_scalar(ss.t[:, 1:2], ss.t[:, 0:1], 1.0 / DM, EPS, op0=ALU.mult, op1=ALU.add),
         reads=[ss], writes=[ss])
    P.op("pool", lambda e: e.tensor_tensor(ss.t[:, 2:3], ss.t[:, 1:2], mh.t[:, 0:1], op=ALU.pow),
         reads=[ss, mh], writes=[ss])
    P.op("act", lambda e: e.activation(xs.t[:], xt.t[:], AF.Copy, scale=ss.t[:, 2:3]), reads=[xt, ss], writes=[xs])


def _norm_tr(P, xs, hT, AB, pbT, identb):
    pv = pbT.t[:, :].bitcast(BF16)
    for k in range(8):
        P.op("pe", lambda e, k=k: e.transpose(pv[:, k * 128:(k + 1) * 128], xs.t[:, k * 128:(k + 1) * 128], identb.t[:]),
             reads=[xs, identb], writes=[pbT])
    hv = hT.t[:, :, :]
    P.op("dve", lambda e: e.tensor_tensor(hv, pv.rearrange("p (k t) -> p k t", k=8), bc(AB.t[:, 0:8], 2, 128), op=ALU.mult),
         reads=[pbT, AB], writes=[hT])
    P.op("pool", lambda e: e.tensor_tensor(hv, hv, bc(AB.t[:, 8:16], 2, 128), op=ALU.add),
         reads=[hT, AB], writes=[hT])


def emit_layer0(P, nc, NT, hc, g, pb, identb, AB, Grow, posT, mh, D, final):
    HP = 2
    first = (g == 0)
    h0 = g * HP
    gam128 = hc["gam128"]
    Win = P.sb("Win0", [128, 8, 3072], BF16)
    Wout = P.sb("Wout0", [128, 8, DM], BF16)
    gnT = P.sb("gnT", [128, 16], F32)
    invr = P.sb("invr", [128, 128], F32)
    maskT = P.sb("maskT", [128, HP * 128], F32)
    qdecr = P.sb("qdecr", [128, HP * 128], F32)
    kdecr = P.sb("kdecr", [128, HP * 128], F32)
    kdecc = P.sb("kdecc", [128, RH], F32)
    S32 = [P.sb("S32_%d" % i, [128, 512], F32) for i in range(2 * HP)]
    Sbf = [P.sb("Sbf_%d" % i, [128, 512], BF16) for i in range(2 * HP)]
    NX = 4
    xt = [P.sb("xt%d" % i, [128, DM], F32) for i in range(NX)]
    ot = [P.sb("ot%d" % i, [128, DM], F32) for i in range(NX)] if not first else None
    ss = [P.sb("ss%d" % i, [128, 8], F32) for i in range(NX)]
    xs = P.sb("xs", [128, DM], BF16)
    hT2 = [P.sb("hT%d" % i, [128, 8, 128], BF16) for i in range(2)]
    sc2 = [P.sb("sc%d" % i, [128, 256], F32) for i in range(2)]
    stmp = P.sb("stmp", [128, 5, 128], F32)
    rtmp = P.sb("rtmp", [128, 4, 256], F32)
    qk2 = [P.sb("qk%d" % i, [128, 1024], BF16) for i in range(2)]
    v2 = [P.sb("v%d" % i, [128, 1024], BF16) for i in range(2)]
    sg2 = [P.sb("sg%d" % i, [128, 1024], BF16) for i in range(2)]
    kd = P.sb("kd", [128, 512], BF16)
    qdT = P.sb("qdT", [128, 4, 128], BF16)
    kdT = P.sb("kdT", [128, 4, 128], BF16)
    sT = P.sb("sT", [128, HP, 128], BF16)
    y = P.sb("y", [128, 1024], BF16)
    yT = P.sb("yT", [128, 8, 128], BF16)
    junko = P.sb("junko", [128, 512], BF16)
    osq = P.sb("osq", [128, 8], F32)

    class _V:
        def __init__(self, ap):
            self.t = ap
    tmpv = [_V(stmp.t[:, i, :]) for i in range(5)]

    P.dma("sp", gnT.t[:], D["rgn"][:, :], writes=[gnT])
    P.dma("sp", invr.t[:], bcast_rows(D["invf"], 128), writes=[invr])
    P.dma("sp", maskT.t[:], D["maskT"][:, h0 * 128:(h0 + HP) * 128], writes=[maskT])
    P.dma("sp", qdecr.t[:], bcast_rows(D["qdecr"][0:1, h0 * 128:(h0 + HP) * 128], 128), writes=[qdecr])
    P.dma("sp", kdecr.t[:], bcast_rows(D["kdecr"][0:1, h0 * 128:(h0 + HP) * 128], 128), writes=[kdecr])
    P.dma("sp", kdecc.t[:], D["kdecc"][:, :], writes=[kdecc])
    slabs = [(0, h0 * 256, 512), (512, 1024 + h0 * 256, 512), (1024, 2048 + h0 * 512, 1024), (2048, 4096 + h0 * 512, 1024)]
    for k in range(8):
        for (dst0, src0, wdt) in slabs:
            P.dma("pool", Win.t[:, k, dst0:dst0 + wdt], D["rwin"][k * 128:(k + 1) * 128, src0:src0 + wdt],
                  writes=[Win], max_dma_last_dim=4096)
    for ch in range(8):
        w = xt[ch % 2]
        cg = g * 8 + ch
        P.dma("sp", w.t[:], D["rwout"][cg * 128:(cg + 1) * 128, :], writes=[w])
        P.op("dve", lambda e, w=w, ch=ch, cg=cg: e.scalar_tensor_tensor(
            Wout.t[:, ch, :], w.t[:], gnT.t[:, cg:cg + 1], Grow.t[:], op0=ALU.mult, op1=ALU.mult),
            reads=[w, gnT, Grow], writes=[Wout])
    for i in range(2 * HP):
        P.op("pool", lambda e, i=i: e.memset(S32[i].t[:], 0.0), writes=[S32[i]])
        P.op("pool", lambda e, i=i: e.memset(Sbf[i].t[:], 0.0), writes=[Sbf[i]])

    pbP = [pb[0], pb[1]]
    pbT = [pb[2], pb[3]]
    pbS = pb[4]
    pbO = [pb[5], pb[6]]
    pbU = pb[7]
    x1buf = TBuf("x1dram")

    def s1_chunks(t):
        par = t % 2
        X = xt[t % NX]
        hT = hT2[par]
        sc = sc2[par]
        qk, v, sg = qk2[par], v2[par], sg2[par]
        sin_b = bc(sc.t[:, 0:128], 1, 2)
        cos_b = bc(sc.t[:, 128:256], 1, 2)

        def c0a():
            emit_norm_stage(P, t, par, X, D["x"][t * 128:(t + 1) * 128, :], ss[t % NX], xs, hT, AB, mh, pbT[0], identb, part="load")
            if not first:
                P.dma("sp", ot[t % NX].t[:], D["x1"][t * 128:(t + 1) * 128, :], reads=[x1buf], writes=[ot[t % NX]])

        def c0b():
            emit_norm_stage(P, t, par, X, None, ss[t % NX], xs, hT, AB, mh, pbT[0], identb, part="stat")

        def c0():
            emit_norm_stage(P, t, par, X, None, ss[t % NX], xs, hT, AB, mh, pbT[0], identb, part="tr")
            P.op("act", lambda e: e.activation(sc.t[:, 0:128], tmpv[3].t, AF.Sin), reads=[stmp], writes=[sc])
            P.op("act", lambda e: e.activation(sc.t[:, 128:256], tmpv[4].t, AF.Sin), reads=[stmp], writes=[sc])

        def csc():
            angv = tmpv[0]
            P.op("dve", lambda e: e.tensor_scalar(angv.t, invr.t[:], posT.t[:, t:t + 1], None, op0=ALU.mult),
                 reads=[invr, posT], writes=[stmp])
            emit_sincos_v(P, stmp, tmpv, sc, sin=False)

        def blk(n):
            bank = pbP[n % 2]
            for k in range(8):
                P.op("pe", lambda e, k=k: e.matmul(
                    bank.t[:], hT.t[:, k, :], Win.t[:, k, n * 512:(n + 1) * 512], start=(k == 0), stop=(k == 7)),
                    reads=[hT, Win], writes=[bank])
            if n < 2:
                pv4 = bank.t[:, :].rearrange("p (h t d) -> p h t d", h=2, t=2)
                t1, t2 = pv4[:, :, 0, :], pv4[:, :, 1, :]
                r = [rtmp.t[:, i, :].rearrange("p (h d) -> p h d", h=2) for i in range(4)]
                P.op("dve", lambda e: e.tensor_tensor(r[0], t1, cos_b, op=ALU.mult), reads=[bank, sc], writes=[rtmp])
                P.op("dve", lambda e: e.tensor_tensor(r[1], t2, sin_b, op=ALU.mult), reads=[bank, sc], writes=[rtmp])
                P.op("dve", lambda e: e.tensor_tensor(r[2], t1, sin_b, op=ALU.mult), reads=[bank, sc], writes=[rtmp])
                P.op("dve", lambda e: e.tensor_tensor(r[3], t2, cos_b, op=ALU.mult), reads=[bank, sc], writes=[rtmp])
                ov = qk.t[:, n * 512:(n + 1) * 512].rearrange("p (h t d) -> p h t d", h=2, t=2)
                P.op("pool", lambda e: e.tensor_tensor(ov[:, :, 0, :], r[0], r[1], op=ALU.subtract),
                     reads=[rtmp], writes=[qk])
                P.op("pool", lambda e: e.tensor_tensor(ov[:, :, 1, :], r[2], r[3], op=ALU.add),
                     reads=[rtmp], writes=[qk])
            elif n < 4:
                j = n - 2
                P.op("act", lambda e: e.activation(v.t[:, j * 512:(j + 1) * 512], bank.t[:], AF.Copy),
                     reads=[bank], writes=[v])
            else:
                j = n - 4
                P.op("act", lambda e: e.activation(sg.t[:, j * 512:(j + 1) * 512], bank.t[:], AF.Silu),
                     reads=[bank], writes=[sg])
        return [c0] + [(lambda n=n: blk(n)) for n in range(6)] + [c0a, c0b, csc]

    def s2_chunks(t):
        par = t % 2
        X = xt[t % NX]
        qk, v, sg = qk2[par], v2[par], sg2[par]

        def d0():
            for hh in range(HP):
                P.op("dve", lambda e, hh=hh: e.tensor_scalar(
                    kd.t[:, hh * 256:(hh + 1) * 256], qk.t[:, 512 + hh * 256:512 + (hh + 1) * 256],
                    kdecc.t[:, h0 + hh:h0 + hh + 1], None, op0=ALU.mult), reads=[qk, kdecc], writes=[kd])
            bank = pbT[1]
            pv = bank.t[:, :].bitcast(BF16)
            for j in range(8):
                P.op("pe", lambda e, j=j: e.transpose(
                    pv[:, j * 128:(j + 1) * 128], qk.t[:, j * 128:(j + 1) * 128], identb.t[:]),
                    reads=[qk, identb], writes=[bank])
            for which, (dst, dec) in enumerate(((qdT, qdecr), (kdT, kdecr))):
                decb = bc(dec.t[:, :].rearrange("p (h t) -> p h t", h=HP), 2, 2)
                src = pv[:, which * 512:(which + 1) * 512].rearrange("p (h c t) -> p h c t", h=HP, c=2)
                P.op("dve", lambda e, dst=dst, decb=decb, src=src: e.tensor_tensor(
                    dst.t[:, :, :].rearrange("p (h c) t -> p h c t", h=HP), src, decb, op=ALU.mult),
                    reads=[bank, dec], writes=[dst])

        def d1():
            for hh in range(HP):
                for ch in range(2):
                    P.op("pe", lambda e, hh=hh, ch=ch: e.matmul(
                        pbS.t[:, hh * 128:(hh + 1) * 128], kdT.t[:, 2 * hh + ch, :], qdT.t[:, 2 * hh + ch, :],
                        start=(ch == 0), stop=(ch == 1)), reads=[kdT, qdT], writes=[pbS])
            P.op("dve", lambda e: e.tensor_tensor(sT.t[:, :, :].rearrange("p a b -> p (a b)"), pbS.t[:, 0:HP * 128], maskT.t[:], op=ALU.mult),
                 reads=[pbS, maskT], writes=[sT])

        def head(hh):
            ob = pbO[hh % 2]
            vh = v.t[:, hh * 512:(hh + 1) * 512]
            P.op("pe", lambda e: e.matmul(ob.t[:], sT.t[:, hh, :], vh, start=True, stop=False),
                 reads=[sT, v], writes=[ob])
            for ch in range(2):
                i = 2 * hh + ch
                P.op("pe", lambda e, i=i, ch=ch: e.matmul(ob.t[:], qdT.t[:, i, :], Sbf[i].t[:], start=False, stop=(ch == 1)),
                     reads=[qdT, Sbf[i]], writes=[ob])
            P.op("act", lambda e: e.activation(junko.t[:], ob.t[:], AF.Square, accum_out=osq.t[:, hh:hh + 1]),
                 reads=[ob], writes=[osq])
            P.op("pool", lambda e: e.tensor_scalar(osq.t[:, 4 + hh:5 + hh], osq.t[:, hh:hh + 1], 1.0 / 512, EPS, op0=ALU.mult, op1=ALU.add),
                 reads=[osq], writes=[osq])
            P.op("pool", lambda e: e.tensor_tensor(osq.t[:, 4 + hh:5 + hh], osq.t[:, 4 + hh:5 + hh], mh.t[:, 0:1], op=ALU.pow),
                 reads=[osq, mh], writes=[osq])
            for ch in range(2):
                i = 2 * hh + ch
                ub = pbU if ch == 0 else pbT[0]
                P.op("pe", lambda e, ch=ch, ub=ub: e.matmul(
                    ub.t[:], kd.t[:, hh * 256 + ch * 128: hh * 256 + (ch + 1) * 128], vh, start=True, stop=True),
                    reads=[kd, v], writes=[ub])
                P.op("dve", lambda e, i=i, ub=ub: e.scalar_tensor_tensor(
                    S32[i].t[:], S32[i].t[:], gam128[h0 + hh], ub.t[:], op0=ALU.mult, op1=ALU.add),
                    reads=[S32[i], ub], writes=[S32[i]])
                P.op("act", lambda e, i=i: e.activation(Sbf[i].t[:], S32[i].t[:], AF.Copy), reads=[S32[i]], writes=[Sbf[i]])

        def ygate(hh):
            ob = pbO[hh % 2]
            P.op("dve", lambda e: e.scalar_tensor_tensor(
                y.t[:, hh * 512:(hh + 1) * 512], ob.t[:], osq.t[:, 4 + hh:5 + hh], sg.t[:, hh * 512:(hh + 1) * 512],
                op0=ALU.mult, op1=ALU.mult), reads=[ob, osq, sg], writes=[y])

        def ytr():
            bank = pbT[1]
            pv = bank.t[:, :].bitcast(BF16)
            for j in range(8):
                P.op("pe", lambda e, j=j: e.transpose(
                    pv[:, j * 128:(j + 1) * 128], y.t[:, j * 128:(j + 1) * 128], identb.t[:]),
                    reads=[y, identb], writes=[bank])
            P.op("act", lambda e: e.activation(yT.t[:, :, :].rearrange("p a b -> p (a b)"), pv, AF.Copy), reads=[bank], writes=[yT])

        def oproj(n):
            acc = X if first else ot[t % NX]
            bank = pbP[n]
            for ch in range(8):
                P.op("pe", lambda e, ch=ch: e.matmul(
                    bank.t[:], yT.t[:, ch, :], Wout.t[:, ch, n * 512:(n + 1) * 512], start=(ch == 0), stop=(ch == 7)),
                    reads=[yT, Wout], writes=[bank])
            P.op("dve", lambda e: e.tensor_tensor(
                acc.t[:, n * 512:(n + 1) * 512], bank.t[:], acc.t[:, n * 512:(n + 1) * 512], op=ALU.add),
                reads=[bank, acc], writes=[acc])
            if n == 1:
                P.dma("sp", D["x1"][t * 128:(t + 1) * 128, :], acc.t[:], reads=[acc], writes=[x1buf],
                      final=(final and g == 1))

        return [d0, d1, lambda: head(0), lambda: head(1), lambda: ygate(0), lambda: ygate(1),
                ytr, lambda: oproj(0), lambda: oproj(1)]

    order = [("d", 0), ("c", 1), ("p", 6), ("d", 1), ("c", 2), ("p", 7), ("b", 0), ("d", 2), ("p", 8), ("a", 0), ("s", 0), ("c", 3), ("d", 3),
             ("d", 4), ("c", 4), ("d", 5), ("e", 0), ("c", 5), ("c", 6)]
    s1 = {}
    for tt in range(min(3, NT)):
        s1[tt] = s1_chunks(tt)
        s1[tt][7]()
    for tt in range(min(2, NT)):
        s1[tt][8]()
        s1[tt][9]()
        s1[tt][0]()
    for f in s1[0][1:7]:
        f()
    prev = None
    for t in range(NT):
        dch = s2_chunks(t)
        cch = s1.get(t + 1)
        if t + 3 < NT:
            s1[t + 3] = s1_chunks(t + 3)
        for kind, i in order:
            if kind == "d":
                dch[i]()
            elif kind == "p":
                if prev is not None:
                    prev[i]()
            elif kind == "c":
                if cch is not None:
                    cch[i]()
            elif kind == "a":
                if t + 3 < NT:
                    s1[t + 3][7]()
            elif kind == "b":
                if t + 2 < NT:
                    s1[t + 2][8]()
            elif kind == "s":
                if t + 2 < NT:
                    s1[t + 2][9]()
            elif t + 2 < NT:
                s1[t + 2][0]()
        s1.pop(t, None)
        prev = dch
    for i in (6, 7, 8):
        prev[i]()


def emit_layer1(P, nc, NT, g, pb, identb, AB, Grow, mh, D):
    first = (g == 0)
    last = (g == NGRP - 1)
    W = HG * 128
    NR = 6
    Win = P.sb("Win1", [128, 8, 4 * W], BF16)
    Wout = P.sb("Wout1", [128, HG, DM], BF16)
    qkg = P.sb("qkg", [128, 2], F32)
    GG = P.sb("GG", [128, 1], F32)
    Bn = P.sb("Bn", [128, HG, 2, 128], F32)
    cfar = P.sb("cfar", [128, HG], F32)
    kTc = [P.sb("kTc%d" % i, [128, HG, 128], BF16) for i in range(NR)]
    vxc = [P.sb("vxc%d" % i, [128, HG, 130], BF16) for i in range(NR)]
    pT = [P.sb("pT%d" % i, [128, 5, 128], BF16) for i in range(2)]
    NX = 4
    xt = [P.sb("xt%d" % i, [128, DM], F32) for i in range(NX)]
    ot = [P.sb("ot%d" % i, [128, DM], F32) for i in range(NX)] if not first else None
    ss = [P.sb("ss%d" % i, [128, 8], F32) for i in range(NX)]
    xs2 = [P.sb("xs%d" % i, [128, DM], BF16) for i in range(2)]
    hT2 = [P.sb("hT%d" % i, [128, 8, 128], BF16) for i in range(2)]
    qkr = P.sb("qkr", [128, 2 * W], BF16)
    qkn = P.sb("qkn", [128, 2 * W], BF16)
    sq = [P.sb("sq%d" % i, [128, 2 * HG], F32) for i in range(2)]
    rq = [P.sb("rq%d" % i, [128, 2 * HG], F32) for i in range(2)]
    th = P.sb("th", [128, 512], F32)
    u = [P.sb("u%d" % i, [128, W], BF16) for i in range(2)]
    qT = [P.sb("qT%d" % i, [128, HG, 128], BF16) for i in range(2)]
    stmp = [P.sb("stmp%d" % i, [128, 256], F32) for i in range(2)]
    rinv = P.sb("rinv", [128, HG], F32)
    y = P.sb("y", [128, W], BF16)
    yT = P.sb("yT", [128, HG, 128], BF16)
    sqv = [P.sb("sqv%d" % i, [128, 512], F32) for i in range(2)]
    qkrb = [TBuf("qkr%d" % i, qkr.t) for i in range(4)]

    for k in range(8):
        for j in range(4):
            P.dma("pool", Win.t[:, k, j * W:(j + 1) * W],
                  D["awin"][k * 128:(k + 1) * 128, j * 2048 + g * W: j * 2048 + (g + 1) * W],
                  writes=[Win], max_dma_last_dim=4096)
    for h in range(HG):
        w = xt[h % 2]
        r0 = (g * HG + h) * 128
        P.dma("sp", w.t[:], D["awout"][r0:r0 + 128, :], writes=[w])
        P.op("dve", lambda e, w=w, h=h: e.scalar_tensor_tensor(
            Wout.t[:, h, :], w.t[:], 0.5, Grow.t[:], op0=ALU.mult, op1=ALU.mult), reads=[w, Grow], writes=[Wout])
    P.dma("sp", qkg.t[:], D["aqk"][:, :], writes=[qkg])
    P.op("dve", lambda e: e.scalar_tensor_tensor(GG.t[:], qkg.t[:, 0:1], 128.0 ** -0.5, qkg.t[:, 1:2], op0=ALU.mult, op1=ALU.mult),
         reads=[qkg], writes=[GG])
    P.dma("sp", Bn.t[:], D["abias"][g * HG:(g + 1) * HG].rearrange("h b p c -> p h b c"), writes=[Bn])
    P.dma("sp", cfar.t[:], bcast_rows(D["acfar"][0:1, g * HG:(g + 1) * HG], 128), writes=[cfar])
    for i in range(NR):
        P.op("pool", lambda e, i=i: e.memset(vxc[i].t[:, :, 128:130], 1.0), writes=[vxc[i]])
    for i in range(2):
        P.op("pool", lambda e, i=i: e.memset(pT[i].t[:], 0.0), writes=[pT[i]])

    pbP = [pb[0], pb[1]]
    pbT = pb[2]
    pbF = [pb[3], pb[4]]
    pbN = [TBuf("pbN0", pb[5].t), TBuf("pbN1", pb[5].t)]
    pbV = [pb[6], pb[7]]
    outbuf = TBuf("outdram")
    src_d = D["x1"]
    pv = pbT.t[:, :].bitcast(BF16)

    def s1_chunks(t):
        par = t % 2
        X = xt[t % NX]
        hT = hT2[par]
        slot = t % NR
        SQ, RQ, U, QT = sq[par], rq[par], u[par], qT[par]

        def c0a():
            emit_norm_stage(P, t, par, X, src_d[t * 128:(t + 1) * 128, :], ss[t % NX], xs2[par], hT, AB, mh, pbT, identb, part="load")
            if not first:
                P.dma("sp", ot[t % NX].t[:], D["out"][t * 128:(t + 1) * 128, :], reads=[outbuf], writes=[ot[t % NX]])

        def c0b():
            emit_norm_stage(P, t, par, X, None, ss[t % NX], xs2[par], hT, AB, mh, pbT, identb, part="stat")

        def c0():
            emit_norm_stage(P, t, par, X, None, ss[t % NX], xs2[par], hT, AB, mh, pbT, identb, part="tr")

        def blk(n):
            bank = pbP[n % 2]
            for k in range(8):
                P.op("pe", lambda e, k=k: e.matmul(
                    bank.t[:], hT.t[:, k, :], Win.t[:, k, n * 512:(n + 1) * 512], start=(k == 0), stop=(k == 7)),
                    reads=[hT, Win], writes=[bank])
            if 1 <= n <= 4:
                m = n - 1
                svm = sqv[m % 2]
                P.op("dve", lambda e: e.tensor_reduce(SQ.t[:, m * 4:(m + 1) * 4], svm.t[:, :].rearrange("p (h d) -> p h d", h=4),
                                                      op=ALU.add, axis=mybir.AxisListType.X),
                     reads=[svm], writes=[SQ])
            if n < 4:
                qb = qkrb[n]
                P.op("dve", lambda e: e.tensor_copy(qkr.t[:, n * 512:(n + 1) * 512], bank.t[:]),
                     reads=[bank], writes=[qb])
                sv = sqv[n % 2]
                P.op("pool", lambda e: e.tensor_tensor(sv.t[:], qkr.t[:, n * 512:(n + 1) * 512], qkr.t[:, n * 512:(n + 1) * 512], op=ALU.mult),
                     reads=[qb], writes=[sv])
            elif n < 6:
                j = n - 4
                P.op("act", lambda e: e.activation(
                    vxc[slot].t[:, j * 4:(j + 1) * 4, 0:128], bank.t[:, :].rearrange("p (h d) -> p h d", h=4), AF.Copy),
                    reads=[bank], writes=[vxc[slot]])
            else:
                j = n - 6
                P.op("act", lambda e: e.activation(th.t[:], bank.t[:], AF.Tanh, scale=0.5), reads=[bank], writes=[th])
                P.op("dve", lambda e: e.scalar_tensor_tensor(
                    U.t[:, j * 512:(j + 1) * 512], th.t[:], 1.0, bank.t[:], op0=ALU.add, op1=ALU.mult),
                    reads=[th, bank], writes=[U])

        def c9a():
            P.op("pool", lambda e: e.tensor_scalar(RQ.t[:], SQ.t[:], 1.0 / 128, EPS, op0=ALU.mult, op1=ALU.add), reads=[SQ], writes=[RQ])
            P.op("pool", lambda e: e.tensor_tensor(RQ.t[:], RQ.t[:], mh.t[:, 0:16], op=ALU.pow), reads=[RQ, mh], writes=[RQ])

        def c9d():
            for hf in range(2):
                P.op("pool", lambda e, hf=hf: e.tensor_tensor(
                    qkn.t[:, hf * W:(hf + 1) * W].rearrange("p (h d) -> p h d", h=HG),
                    qkr.t[:, hf * W:(hf + 1) * W].rearrange("p (h d) -> p h d", h=HG),
                    bc(RQ.t[:, hf * HG:(hf + 1) * HG], 2, 128), op=ALU.mult), reads=[qkrb[2 * hf], qkrb[2 * hf + 1], RQ], writes=[qkn])

        def c9b():
            for j in range(HG):
                P.op("pe", lambda e, j=j: e.transpose(pv[:, j * 128:(j + 1) * 128], qkn.t[:, j * 128:(j + 1) * 128], identb.t[:]),
                     reads=[qkn, identb], writes=[pbT])
            P.op("dve", lambda e: e.tensor_scalar(QT.t[:, :, :].rearrange("p a b -> p (a b)"), pv, GG.t[:, 0:1], None, op0=ALU.mult),
                 reads=[pbT, GG], writes=[QT])

        def c10k():
            for j in range(HG):
                P.op("pe", lambda e, j=j: e.transpose(pv[:, j * 128:(j + 1) * 128], qkn.t[:, W + j * 128: W + (j + 1) * 128], identb.t[:]),
                     reads=[qkn, identb], writes=[pbT])
            P.op("dve", lambda e: e.tensor_copy(kTc[slot].t[:, :, :].rearrange("p a b -> p (a b)"), pv),
                 reads=[pbT], writes=[kTc[slot]])

        return [c0] + [(lambda n=n: blk(n)) for n in range(8)] + [c9a, c9b, c0a, c0b, c9d, c10k]

    def s2_chunks(t):
        par = t % 2
        X = xt[t % NX]
        U, QT = u[par], qT[par]
        blocks = [i for i in range(5) if t - 4 + i >= 0]
        far = [i for i in blocks if i < 3]

        def scores(h):
            hp = h % 2
            fb, nb = pbF[hp], pbN[hp]
            for i in far:
                sl = (t - 4 + i) % NR
                P.op("pe", lambda e, i=i, sl=sl: e.matmul(
                    fb.t[:, i * 128:(i + 1) * 128], kTc[sl].t[:, h, :], QT.t[:, h, :], start=True, stop=True),
                    reads=[kTc[sl], QT], writes=[fb])
            for i in blocks:
                if i < 3:
                    continue
                sl = (t - 4 + i) % NR
                nc0 = hp * 256 + (i - 3) * 128
                P.op("pe", lambda e, i=i, sl=sl, nc0=nc0: e.matmul(
                    nb.t[:, nc0:nc0 + 128], kTc[sl].t[:, h, :], QT.t[:, h, :], start=True, stop=True),
                    reads=[kTc[sl], QT], writes=[nb])

        def expo(h):
            hp = h % 2
            fb, nb, pt = pbF[hp], pbN[hp], pT[hp]
            if far:
                f0 = far[0]
                cf_ = f0 * 128
                P.op("act", lambda e: e.activation(
                    pt.t[:, f0:3, :].rearrange("p a b -> p (a b)"), fb.t[:, cf_:384], AF.Exp, bias=cfar.t[:, h:h + 1]),
                    reads=[fb, cfar], writes=[pt])
                if f0 == 0:
                    P.op("pool", lambda e: e.memset(pt.t[0:64, 0, 64:128], 0.0), writes=[pt])
            nn = [i for i in blocks if i >= 3]
            n0 = nn[0] - 3
            st = stmp[hp]
            P.op("dve", lambda e: e.tensor_tensor(
                st.t[:, n0 * 128:256], nb.t[:, hp * 256 + n0 * 128:hp * 256 + 256], Bn.t[:, h, n0:2, :].rearrange("p a b -> p (a b)"), op=ALU.add),
                reads=[nb, Bn], writes=[st])
            P.op("act", lambda e: e.activation(
                pt.t[:, 3 + n0:5, :].rearrange("p a b -> p (a b)"), st.t[:, n0 * 128:256], AF.Exp), reads=[st], writes=[pt])
        def pvh(h):
            hp = h % 2
            pt = pT[hp]
            vs = pbV[hp]
            for bi, i in enumerate(blocks):
                sl = (t - 4 + i) % NR
                P.op("pe", lambda e, i=i, sl=sl, bi=bi: e.matmul(
                    vs.t[:, 0:129], pt.t[:, i, :], vxc[sl].t[:, h, 0:129], start=(bi == 0), stop=(bi == len(blocks) - 1)),
                    reads=[pt, vxc[sl]], writes=[vs])
            P.op("dve", lambda e: e.reciprocal(rinv.t[:, h:h + 1], vs.t[:, 128:129]), reads=[vs], writes=[rinv])
            P.op("dve", lambda e: e.scalar_tensor_tensor(
                y.t[:, h * 128:(h + 1) * 128], vs.t[:, 0:128], rinv.t[:, h:h + 1], U.t[:, h * 128:(h + 1) * 128],
                op0=ALU.mult, op1=ALU.mult), reads=[vs, rinv, U], writes=[y])

        def ytr():
            for j in range(HG):
                P.op("pe", lambda e, j=j: e.transpose(pv[:, j * 128:(j + 1) * 128], y.t[:, j * 128:(j + 1) * 128], identb.t[:]),
                     reads=[y, identb], writes=[pbT])
            P.op("act", lambda e: e.activation(yT.t[:, :, :].rearrange("p a b -> p (a b)"), pv, AF.Copy), reads=[pbT], writes=[yT])

        def oproj(n):
            acc = X if first else ot[t % NX]
            bank = pbP[n]
            for ch in range(HG):
                P.op("pe", lambda e, ch=ch: e.matmul(
                    bank.t[:], yT.t[:, ch, :], Wout.t[:, ch, n * 512:(n + 1) * 512], start=(ch == 0), stop=(ch == HG - 1)),
                    reads=[yT, Wout], writes=[bank])
            P.op("dve", lambda e: e.tensor_tensor(
                acc.t[:, n * 512:(n + 1) * 512], bank.t[:], acc.t[:, n * 512:(n + 1) * 512], op=ALU.add),
                reads=[bank, acc], writes=[acc])
            if n == 1:
                P.dma("sp", D["out"][t * 128:(t + 1) * 128, :], acc.t[:], reads=[acc], writes=[outbuf], final=last)

        return [(lambda h=h: scores(h)) for h in range(HG)] + [(lambda h=h: expo(h)) for h in range(HG)] + \
               [(lambda h=h: pvh(h)) for h in range(HG)] + [ytr, lambda: oproj(0), lambda: oproj(1)]

    order = [("d", 0), ("c", 1), ("p", 24), ("d", 1), ("p", 25), ("b", 0),
             ("d", 8), ("c", 2), ("d", 2), ("d", 16), ("p", 26), ("a", 0),
             ("d", 9), ("c", 3), ("d", 3), ("d", 17),
             ("d", 10), ("c", 4), ("d", 4), ("d", 18),
             ("d", 11), ("c", 5), ("c", 9), ("d", 5), ("d", 19),
             ("d", 12), ("c", 6), ("d", 6), ("d", 20),
             ("d", 13), ("c", 7), ("c", 13), ("d", 7), ("d", 21), ("e", 0),
             ("d", 14), ("c", 8), ("d", 22), ("c", 10),
             ("d", 15), ("c", 14), ("d", 23)]
    s1 = {}
    for tt in range(min(3, NT)):
        s1[tt] = s1_chunks(tt)
        s1[tt][11]()
    for tt in range(min(2, NT)):
        s1[tt][12]()
        s1[tt][0]()
    for i in (1, 2, 3, 4, 5, 6, 7, 8, 9, 13, 10, 14):
        s1[0][i]()
    prev = None
    for t in range(NT):
        dch = s2_chunks(t)
        cch = s1.get(t + 1)
        if t + 3 < NT:
            s1[t + 3] = s1_chunks(t + 3)
        for kind, i in order:
            if kind == "d":
                dch[i]()
            elif kind == "p":
                if prev is not None:
                    prev[i]()
            elif kind == "c":
                if cch is not None:
                    cch[i]()
            elif kind == "a":
                if t + 3 < NT:
                    s1[t + 3][11]()
            elif kind == "b":
                if t + 2 < NT:
                    s1[t + 2][12]()
            elif t + 2 < NT:
                s1[t + 2][0]()
        s1.pop(t, None)
        prev = dch
    for i in (24, 25, 26):
        prev[i]()


def _lay(inputs, b, NT):
    S = NT * 128
    hc = host_consts()
    f = lambda a: np.ascontiguousarray(a, dtype=np.float32)
    d = {
        "x": f(inputs["x"][b, :S]),
        "cT": f(inputs["c"][b].reshape(8, 128).T),
        "pos": np.ascontiguousarray(inputs["positions"][b, :S].reshape(NT, 128).astype(np.int32)),
        "norm_g": f(inputs["norm_g"]),
        "ada_w": f(inputs["ada_w"]),
        "ada_b": f(inputs["ada_b"]),
        "ret_w_in": f(inputs["ret_w_in"][0]),
        "ret_gnT": f(inputs["ret_gn_g"][0].reshape(16, 128).T),
        "ret_w_out": f(inputs["ret_w_out"][0]),
        "att_w_in": f(inputs["att_w_in"][0]),
        "att_qk_g": f(np.stack([inputs["att_q_g"][0], inputs["att_k_g"][0]], axis=1)),
        "att_bias_tiles": attn_bias_tiles(np.asarray(inputs["att_rel_bias"][0], np.float32)),
        "att_cfar": f(inputs["att_rel_bias"][0][:, 256].reshape(1, AH)),
        "att_w_out": f(inputs["att_w_out"][0]),
    }
    for k in ("identb", "identf", "invf", "maskT", "qdec_row", "kdec_row", "kdec_col"):
        d[k] = hc[k]
    return d


def run(inputs, NT=64, ncores=8, do_l0=True, do_l1=True):
    inputs = {k: np.asarray(v) for k, v in inputs.items()}
    nc = build_program(NT, do_l0, do_l1)
    in_maps = [_lay(inputs, b, NT) for b in range(ncores)]
    res = run_bass_kernel_spmd(nc, in_maps, core_ids=list(range(ncores)))
    return np.stack([np.asarray(r["out"]) for r in res.results], axis=0)


def kernel(**inputs):
    return run(inputs, NT=64, ncores=8)
```

```python
import math
import os
from contextlib import ExitStack, contextmanager

import numpy as np
import ml_dtypes
import concourse.bass as bass
import concourse.mybir as mybir
from concourse.bass_utils import run_bass_kernel_spmd

F32 = mybir.dt.float32
BF16 = mybir.dt.bfloat16
I32 = mybir.dt.int32
AF = mybir.ActivationFunctionType
ALU = mybir.AluOpType

ENGINES = ("pe", "act", "dve", "pool", "sp")
NDSEM = 6


class TBuf:
    __slots__ = ("name", "t", "w", "r", "wd")

    def __init__(self, name, t=None):
        self.name = name
        self.t = t
        self.w = None
        self.r = {}
        self.wd = {}


class _Op:
    __slots__ = ("eng", "fn", "idx", "deps", "kind", "dsem")


class Prog:
    def __init__(self, nc):
        self.nc = nc
        self.ops = {e: [] for e in ENGINES}
        self.ndma = {e: 0 for e in ENGINES}
        self.finals = []
        self.stack = None
        self.phase = 0
        self.base = {e: 0 for e in ENGINES}

    @contextmanager
    def ctx(self):
        with ExitStack() as st:
            self.stack = st
            self.csem = {e: st.enter_context(self.nc.semaphore("cs_" + e)) for e in ENGINES}
            self.dsem = {
                e: [st.enter_context(self.nc.semaphore("ds_%s%d" % (e, i))) for i in range(NDSEM)]
                for e in ("sp", "pool", "act")
            }
            yield self
            self.emit()

    @contextmanager
    def scope(self):
        self.emit()
        outer = self.stack
        with ExitStack() as st:
            self.stack = st
            yield self
            self.emit()
        self.stack = outer

    def sb(self, name, shape, dt):
        self.uid = getattr(self, "uid", 0) + 1
        t = self.stack.enter_context(self.nc.sbuf_tensor("sb%d_%s" % (self.uid, name), list(shape), dt))
        return TBuf(name, t)

    def ps(self, name, shape, dt):
        t = self.stack.enter_context(self.nc.psum_tensor("ps_" + name, list(shape), dt))
        return TBuf(name, t)

    def view(self, buf, name):
        return TBuf(name, buf.t)

    def _collect(self, eng, reads, writes):
        deps = []
        for b in reads:
            if b.w is not None:
                deps.append(b.w)
            deps.extend(b.wd.values())
        for b in writes:
            if b.w is not None:
                deps.append(b.w)
            deps.extend(b.wd.values())
            deps.extend(b.r.values())
        ph = self.phase
        deps = [d for d in deps if not (d[0] == "c" and d[3] != ph)]
        if eng == "pe":
            deps = [d for d in deps if not (d[0] == "c" and d[1] == "pe")]
        return deps

    def _commit(self, reads, writes, tk):
        key = tk[:2] if tk[0] == "c" else tk[:3]
        for b in reads:
            b.r[key] = tk
        for b in writes:
            b.w = tk
            b.r = {}
            if tk[0] == "d":
                b.wd[key] = tk
            else:
                b.wd = {}

    def op(self, eng, fn, reads=(), writes=(), kind="c"):
        o = _Op()
        o.eng, o.fn, o.idx, o.kind = eng, fn, len(self.ops[eng]), kind
        o.deps = self._collect(eng, reads, writes)
        tk = ("c", eng, o.idx, self.phase)
        self._commit(reads, writes, tk)
        self.ops[eng].append(o)
        return tk

    def dma(self, q, out_ap, in_ap, reads=(), writes=(), final=False, **kw):
        j = self.ndma[q]
        self.ndma[q] += 1
        si, val = j % NDSEM, 16 * (j // NDSEM + 1)
        o = _Op()
        o.eng, o.idx, o.kind, o.dsem = q, len(self.ops[q]), "d", si
        o.fn = lambda e: e.dma_start(out=out_ap, in_=in_ap, **kw)
        o.deps = self._collect(q, reads, writes)
        if val > 16:
            o.deps.append(("d", q, si, val - 16))
        tk = ("d", q, si, val)
        self._commit(reads, writes, tk)
        self.ops[q].append(o)
        if final:
            self.finals.append(tk)
        return tk

    def emit(self):
        nc = self.nc
        if not any(self.ops[e] for e in ENGINES):
            return
        ph = self.phase
        deps = list(self.finals)
        self.finals = []
        for e in ENGINES:
            if e != "sp" and self.ops[e]:
                deps.append(("c", e, len(self.ops[e]) - 1, ph))
        for q in self.dsem:
            for si in range(NDSEM):
                n = (self.ndma[q] - si + NDSEM - 1) // NDSEM
                if n > 0:
                    deps.append(("d", q, si, 16 * n))
        o = _Op()
        o.eng, o.idx, o.kind, o.fn, o.deps = "sp", len(self.ops["sp"]), "w", None, deps
        self.ops["sp"].append(o)
        o = _Op()
        o.eng, o.idx, o.kind, o.fn, o.deps = "sp", len(self.ops["sp"]), "s", None, []
        self.ops["sp"].append(o)
        rel = ("c", "sp", o.idx, ph)
        for e in ENGINES:
            if e != "sp":
                o = _Op()
                o.eng, o.idx, o.kind, o.fn, o.deps = e, len(self.ops[e]), "w", None, [rel]
                self.ops[e].append(o)

        sig = {e: set() for e in ENGINES}
        for e in ENGINES:
            for o in self.ops[e]:
                for d in o.deps:
                    if d[0] == "c":
                        sig[d[1]].add(d[2])
        rank = {e: {idx: self.base[e] + r + 1 for r, idx in enumerate(sorted(sig[e]))} for e in ENGINES}
        ops = self.ops

        def run(e, eng):
            waited = {}
            for o in ops[e]:
                need = {}
                for d in o.deps:
                    if d[0] == "c":
                        k, sem, val = ("c", d[1]), self.csem[d[1]], rank[d[1]][d[2]]
                    else:
                        k, sem, val = ("d", d[1], d[2]), self.dsem[d[1]][d[2]], d[3]
                    if need.get(k, (None, 0))[1] < val:
                        need[k] = (sem, val)
                for k, (sem, val) in need.items():
                    if waited.get(k, 0) >= val:
                        continue
                    eng.wait_ge(sem, val)
                    waited[k] = val
                if os.environ.get("PDBG"):
                    print("PDBG", self.phase, e, o.idx, o.kind, sorted((str(k), v) for k, (_, v) in need.items()),
                          "SIG=%d" % rank[e][o.idx] if o.idx in sig[e] else "")
                if o.kind == "w":
                    continue
                if o.kind == "s":
                    eng.sem_inc(self.csem[e], 1)
                    continue
                ins = o.fn(eng)
                if o.kind == "d":
                    ins.then_inc(self.dsem[e][o.dsem], 16)
                elif o.idx in sig[e]:
                    ins.then_inc(self.csem[e], 1)

        with nc.Block() as block:
            @block.tensor
            def _(eng):
                run("pe", eng)

            @block.scalar
            def _(eng):
                run("act", eng)

            @block.vector
            def _(eng):
                run("dve", eng)

            @block.gpsimd
            def _(eng):
                run("pool", eng)

            @block.sync
            def _(eng):
                run("sp", eng)

        for e in ENGINES:
            self.base[e] += len(sig[e])
        self.ops = {e: [] for e in ENGINES}
        self.phase += 1


def bcast_rows(ap, n):
    dims = [list(d) for d in ap.ap]
    if len(dims) >= 2 and dims[0][1] == 1:
        dims = dims[1:]
    return bass.AP(ap.tensor, ap.offset, [[0, n]] + dims)


TWO_PI = 2.0 * math.pi
CW1 = 6.28125
CW2 = float(np.float32(TWO_PI - CW1))
CW3 = float(TWO_PI - CW1 - float(np.float32(TWO_PI - CW1)))
PI_LO = 3.1415925


def emit_sincos(P, ang, sc, ki, kf, red, redc, eng="dve"):
    Fd = ang.t.shape[1]
    P.op(eng, lambda e: e.tensor_scalar(ki.t[:], ang.t[:], 1.0 / TWO_PI, None, op0=ALU.mult),
         reads=[ang], writes=[ki])
    P.op(eng, lambda e: e.tensor_copy(kf.t[:], ki.t[:]), reads=[ki], writes=[kf])
    P.op(eng, lambda e: e.scalar_tensor_tensor(red.t[:], kf.t[:], -CW1, ang.t[:], op0=ALU.mult, op1=ALU.add),
         reads=[kf, ang], writes=[red])
    P.op(eng, lambda e: e.scalar_tensor_tensor(red.t[:], kf.t[:], -CW2, red.t[:], op0=ALU.mult, op1=ALU.add),
         reads=[kf, red], writes=[red])
    P.op(eng, lambda e: e.scalar_tensor_tensor(red.t[:], kf.t[:], -CW3, red.t[:], op0=ALU.mult, op1=ALU.add),
         reads=[kf, red], writes=[red])
    P.op(eng, lambda e: e.tensor_scalar(kf.t[:], red.t[:], math.pi, -TWO_PI, op0=ALU.is_gt, op1=ALU.mult),
         reads=[red], writes=[kf])
    P.op(eng, lambda e: e.tensor_tensor(red.t[:], red.t[:], kf.t[:], op=ALU.add), reads=[red, kf], writes=[red])
    P.op(eng, lambda e: e.tensor_scalar(kf.t[:], red.t[:], math.pi / 2, -TWO_PI, op0=ALU.is_gt, op1=ALU.mult),
         reads=[red], writes=[kf])
    P.op(eng, lambda e: e.scalar_tensor_tensor(redc.t[:], red.t[:], math.pi / 2, kf.t[:], op0=ALU.add, op1=ALU.add),
         reads=[red, kf], writes=[redc])
    for src in (red, redc):
        P.op(eng, lambda e, s=src: e.tensor_scalar(s.t[:], s.t[:], PI_LO, -PI_LO, op0=ALU.min, op1=ALU.max),
             reads=[src], writes=[src])
    P.op("act", lambda e: e.activation(sc.t[:, 0:Fd], red.t[:], AF.Sin), reads=[red], writes=[sc])
    P.op("act", lambda e: e.activation(sc.t[:, Fd:2 * Fd], redc.t[:], AF.Sin), reads=[redc], writes=[sc])


def emit_sincos_v(P, buf, tv, sc, eng="dve", sin=True):
    ang, ki, kf, red, redc = [v.t for v in tv]
    kii = ki.bitcast(I32)
    ops = [
        lambda e: e.tensor_scalar(kii, ang, 1.0 / TWO_PI, None, op0=ALU.mult),
        lambda e: e.tensor_copy(kf, kii),
        lambda e: e.scalar_tensor_tensor(red, kf, -CW1, ang, op0=ALU.mult, op1=ALU.add),
        lambda e: e.scalar_tensor_tensor(red, kf, -(CW2 + CW3), red, op0=ALU.mult, op1=ALU.add),
        lambda e: e.tensor_scalar(kf, red, math.pi, -TWO_PI, op0=ALU.is_gt, op1=ALU.mult),
        lambda e: e.tensor_tensor(red, red, kf, op=ALU.add),
        lambda e: e.tensor_scalar(kf, red, math.pi / 2, -TWO_PI, op0=ALU.is_gt, op1=ALU.mult),
        lambda e: e.scalar_tensor_tensor(redc, red, math.pi / 2, kf, op0=ALU.add, op1=ALU.add),
        lambda e: e.tensor_scalar(red, red, PI_LO, -PI_LO, op0=ALU.min, op1=ALU.max),
        lambda e: e.tensor_scalar(redc, redc, PI_LO, -PI_LO, op0=ALU.min, op1=ALU.max),
    ]
    for f in ops:
        P.op(eng, f, reads=[buf], writes=[buf])
    if not sin:
        return
    P.op("act", lambda e: e.activation(sc.t[:, 0:128], red, AF.Sin), reads=[buf], writes=[sc])
    P.op("act", lambda e: e.activation(sc.t[:, 128:256], redc, AF.Sin), reads=[buf], writes=[sc])


DM = 1024
EPS = 1e-6
RH = 4
AH = 16
HG = 8
NGRP = AH // HG
MASKNEG = -30000.0


def bc(ap, axis, n):
    dims = [list(d) for d in ap.ap]
    dims.insert(axis, [0, n])
    return bass.AP(ap.tensor, ap.offset, dims)


def host_consts():
    c = {}
    c["identb"] = np.eye(128, dtype=np.float32).astype(ml_dtypes.bfloat16)
    c["identf"] = np.eye(128, dtype=np.float32)
    c["invf"] = (1.0 / (10000.0 ** (np.arange(0, 256, 2, dtype=np.float32) / np.float32(256)))).astype(np.float32).reshape(1, 128)
    idx = np.arange(128, dtype=np.float64)
    maskT = np.zeros((128, RH, 128), np.float64)
    qdec = np.zeros((RH, 128), np.float64)
    kdec = np.zeros((RH, 128), np.float64)
    for h in range(RH):
        g = 1.0 - 2.0 ** (-5.0 - h)
        lg = math.log(g)
        k = idx[:, None]
        cc = idx[None, :]
        same = (k // 64) == (cc // 64)
        cross = (k < 64) & (cc >= 64)
        dm = np.where(same, np.exp(lg * np.abs(cc - k)), np.where(cross, np.exp(lg * (cc - k)), 0.0))
        maskT[:, h, :] = dm * np.exp(lg * (k - cc - 128.0))
        qdec[h] = np.exp(lg * (idx + 1.0))
        kdec[h] = np.exp(lg * (127.0 - idx)) / 16.0
    c["maskT"] = maskT.astype(np.float32).reshape(128, RH * 128)
    c["qdec_row"] = qdec.astype(np.float32).reshape(1, RH * 128)
    c["kdec_row"] = kdec.astype(np.float32).reshape(1, RH * 128)
    c["kdec_col"] = kdec.T.astype(np.float32).copy()
    c["gam128"] = [float((1.0 - 2.0 ** (-5.0 - h)) ** 128) for h in range(RH)]
    return c


def attn_bias_tiles(rel_table):
    H = rel_table.shape[0]
    k = np.arange(128)[:, None]
    q = np.arange(128)[None, :]
    out = np.empty((H, 2, 128, 128), np.float32)
    for bi, koff in enumerate((-128, 0)):
        rel = q - (k + koff)
        idx = np.clip(rel, -128, 128) + 128
        t = rel_table[:, idx]
        if koff == 0:
            invalid = (k // 64) > (q // 64)
            t = np.where(invalid[None], np.float32(MASKNEG), t)
        out[:, bi] = t
    return out


def build_program(NT, do_l0=True, do_l1=True):
    S = NT * 128
    nc = bass.Bass("TRN2", target_bir_lowering=False)
    hc = host_consts()

    def din(name, shape, dtype=F32):
        return nc.dram_tensor(name, list(shape), dtype, kind="ExternalInput").ap()

    x_d = din("x", [S, DM])
    cT_d = din("cT", [128, 8])
    pos_d = din("pos", [NT, 128], I32)
    ng_d = din("norm_g", [2, DM])
    adaw_d = din("ada_w", [2, DM, 3 * DM])
    adab_d = din("ada_b", [2, 3 * DM])
    rwin_d = din("ret_w_in", [DM, 6144])
    rgn_d = din("ret_gnT", [128, 16])
    rwout_d = din("ret_w_out", [2048, DM])
    awin_d = din("att_w_in", [DM, 8192])
    aqk_d = din("att_qk_g", [128, 2])
    abias_d = din("att_bias_tiles", [AH, 2, 128, 128])
    acfar_d = din("att_cfar", [1, AH])
    awout_d = din("att_w_out", [2048, DM])
    identb_d = din("identb", [128, 128], BF16)
    identf_d = din("identf", [128, 128])
    invf_d = din("invf", [1, 128])
    maskT_d = din("maskT", [128, RH * 128])
    qdecr_d = din("qdec_row", [1, 512])
    kdecr_d = din("kdec_row", [1, 512])
    kdecc_d = din("kdec_col", [128, RH])
    if do_l0 and do_l1:
        x1_d = nc.dram_tensor("x1s", [S, DM], F32, kind="Internal").ap()
    elif do_l0:
        x1_d = None
    else:
        x1_d = x_d
    out_d = nc.dram_tensor("out", [S, DM], F32, kind="ExternalOutput").ap()

    P = Prog(nc)
    with P.ctx():
        pb = [P.ps("pb%d" % i, [128, 512], F32) for i in range(8)]
        identb = P.sb("identb", [128, 128], BF16)
        identf = P.sb("identf", [128, 128], F32)
        AB = P.sb("AB", [128, 16], F32)
        Grow = P.sb("Grow", [128, DM], F32)
        posT = P.sb("posT", [128, NT], F32)
        mh = P.sb("mh", [128, 16], F32)
        P.dma("sp", identb.t[:], identb_d[:, :], writes=[identb])
        P.dma("sp", identf.t[:], identf_d[:, :], writes=[identf])
        P.op("pool", lambda e: e.memset(mh.t[:], -0.5), writes=[mh])

        def prologue(li):
            with P.scope():
                cT = P.sb("cT", [128, 8], F32)
                cond = P.sb("cond", [128, 8], F32)
                condbc = P.sb("condbc", [128, 8, 128], F32)
                adaw = [P.sb("adaw%d" % i, [128, 3 * DM], F32) for i in range(2)]
                modrow = P.sb("modrow", [128, 3 * DM], F32)
                adab = P.sb("adab", [128, 3 * DM], F32)
                ngrow = P.sb("ngrow", [128, DM], F32)
                arow = P.sb("arow", [128, DM], F32)
                P.dma("sp", cT.t[:], cT_d[:, :], writes=[cT])
                if li == 0 or not do_l0:
                    posi = P.sb("posi", [NT, 128], I32)
                    posf = P.sb("posf", [NT, 128], F32)
                    P.dma("sp", posi.t[:], pos_d[:, :], writes=[posi])
                    P.op("dve", lambda e: e.tensor_copy(posf.t[:], posi.t[:]), reads=[posi], writes=[posf])
                    P.op("pe", lambda e: e.transpose(pb[7].t[:, 0:NT], posf.t[:, :], identf.t[0:NT, 0:NT]),
                         reads=[posf, identf], writes=[pb[7]])
                    P.op("dve", lambda e: e.tensor_copy(posT.t[:], pb[7].t[:, 0:NT]), reads=[pb[7]], writes=[posT])
                P.op("act", lambda e: e.activation(cond.t[:], cT.t[:], AF.Tanh, scale=0.5), reads=[cT], writes=[cond])
                P.op("dve", lambda e: e.tensor_scalar(cond.t[:], cond.t[:], 0.5, 0.5, op0=ALU.mult, op1=ALU.add),
                     reads=[cond], writes=[cond])
                P.op("dve", lambda e: e.tensor_tensor(cond.t[:], cond.t[:], cT.t[:], op=ALU.mult), reads=[cond, cT], writes=[cond])
                P.op("dve", lambda e: e.tensor_copy(condbc.t[:], bc(cond.t[:, :], 2, 128)), reads=[cond], writes=[condbc])
                P.dma("sp", adab.t[:], bcast_rows(adab_d[li:li + 1, :], 128), writes=[adab])
                P.dma("sp", ngrow.t[:], bcast_rows(ng_d[li:li + 1, :], 128), writes=[ngrow])
                for k in range(8):
                    aw = adaw[k % 2]
                    P.dma("sp", aw.t[:], adaw_d[li, k * 128:(k + 1) * 128, :], writes=[aw])
                    for n in range(6):
                        P.op("pe", lambda e, aw=aw, k=k, n=n: e.matmul(
                            pb[n].t[:], condbc.t[:, k, :], aw.t[:, n * 512:(n + 1) * 512],
                            start=(k == 0), stop=(k == 7)), reads=[aw, condbc], writes=[pb[n]])
                for n in range(6):
                    P.op("dve", lambda e, n=n: e.tensor_tensor(
                        modrow.t[:, n * 512:(n + 1) * 512], pb[n].t[:], adab.t[:, n * 512:(n + 1) * 512], op=ALU.add),
                        reads=[pb[n], adab], writes=[modrow])
                P.op("dve", lambda e: e.scalar_tensor_tensor(
                    arow.t[:], modrow.t[:, DM:2 * DM], 1.0, ngrow.t[:], op0=ALU.add, op1=ALU.mult),
                    reads=[modrow, ngrow], writes=[arow])
                P.op("act", lambda e: e.activation(Grow.t[:], modrow.t[:, 2 * DM:3 * DM], AF.Copy),
                     reads=[modrow], writes=[Grow])
                for j in range(16):
                    src = arow.t[:, j * 128:(j + 1) * 128] if j < 8 else modrow.t[:, (j - 8) * 128:(j - 7) * 128]
                    bank = pb[6 + (j % 2)]
                    P.op("pe", lambda e, src=src, bank=bank: e.transpose(bank.t[:, 0:128], src, identf.t[:]),
                         reads=[arow, modrow, identf], writes=[bank])
                    P.op("dve", lambda e, bank=bank, j=j: e.tensor_copy(AB.t[:, j:j + 1], bank.t[:, 0:1]),
                         reads=[bank], writes=[AB])

        if do_l0:
            prologue(0)
            for g in range(2):
                with P.scope():
                    emit_layer0(P, nc, NT, hc, g, pb, identb, AB, Grow, posT, mh,
                                dict(x=x_d, x1=(x1_d if x1_d is not None else out_d), rwin=rwin_d, rgn=rgn_d, rwout=rwout_d,
                                     invf=invf_d, maskT=maskT_d, qdecr=qdecr_d, kdecr=kdecr_d, kdecc=kdecc_d),
                                final=(not do_l1))
        if do_l1:
            prologue(1)
            for g in range(int(os.environ.get('NG', NGRP))):
                with P.scope():
                    emit_layer1(P, nc, NT, g, pb, identb, AB, Grow, mh,
                                dict(x1=x1_d, out=out_d, awin=awin_d, aqk=aqk_d, abias=abias_d, acfar=acfar_d,
                                     awout=awout_d))
    return nc


def emit_norm_stage(P, t, par, xt, src_ap, ss, xs, hT, AB, mh, pbT, identb, q="sp", part="all"):
    if part in ("all", "load"):
        P.dma(q, xt.t[:], src_ap, writes=[xt])
    if part == "load":
        return
    if part in ("all", "stat"):
        _norm_stat(P, xt, ss, xs, mh)
    if part == "stat":
        return
    _norm_tr(P, xs, hT, AB, pbT, identb)


def _norm_stat(P, xt, ss, xs, mh):
    P.op("act", lambda e: e.activation(xs.t[:], xt.t[:], AF.Square, accum_out=ss.t[:, 0:1]),
         reads=[xt], writes=[ss, xs])
    P.op("pool", lambda e: e.tensor_scalar(ss.t[:, 1:2], ss.t[:, 0:1], 1.0 / DM, EPS, op0=ALU.mult, op1=ALU.add),
         reads=[ss], writes=[ss])
    P.op("pool", lambda e: e.tensor_tensor(ss.t[:, 2:3], ss.t[:, 1:2], mh.t[:, 0:1], op=ALU.pow),
         reads=[ss, mh], writes=[ss])
    P.op("act", lambda e: e.activation(xs.t[:], xt.t[:], AF.Copy, scale=ss.t[:, 2:3]), reads=[xt, ss], writes=[xs])


def _norm_tr(P, xs, hT, AB, pbT, identb):
    pv = pbT.t[:, :].bitcast(BF16)
    for k in range(8):
        P.op("pe", lambda e, k=k: e.transpose(pv[:, k * 128:(k + 1) * 128], xs.t[:, k * 128:(k + 1) * 128], identb.t[:]),
             reads=[xs, identb], writes=[pbT])
    hv = hT.t[:, :, :]
    P.op("dve", lambda e: e.tensor_tensor(hv, pv.rearrange("p (k t) -> p k t", k=8), bc(AB.t[:, 0:8], 2, 128), op=ALU.mult),
         reads=[pbT, AB], writes=[hT])
    P.op("pool", lambda e: e.tensor_tensor(hv, hv, bc(AB.t[:, 8:16], 2, 128), op=ALU.add),
         reads=[hT, AB], writes=[hT])


def emit_layer0(P, nc, NT, hc, g, pb, identb, AB, Grow, posT, mh, D, final):
    HP = 2
    first = (g == 0)
    h0 = g * HP
    gam128 = hc["gam128"]
    Win = P.sb("Win0", [128, 8, 3072], BF16)
    Wout = P.sb("Wout0", [128, 8, DM], BF16)
    gnT = P.sb("gnT", [128, 16], F32)
    invr = P.sb("invr", [128, 128], F32)
    maskT = P.sb("maskT", [128, HP * 128], F32)
    qdecr = P.sb("qdecr", [128, HP * 128], F32)
    kdecr = P.sb("kdecr", [128, HP * 128], F32)
    kdecc = P.sb("kdecc", [128, RH], F32)
    S32 = [P.sb("S32_%d" % i, [128, 512], F32) for i in range(2 * HP)]
    Sbf = [P.sb("Sbf_%d" % i, [128, 512], BF16) for i in range(2 * HP)]
    NX = 4
    xt = [P.sb("xt%d" % i, [128, DM], F32) for i in range(NX)]
    ot = [P.sb("ot%d" % i, [128, DM], F32) for i in range(NX)] if not first else None
    ss = [P.sb("ss%d" % i, [128, 8], F32) for i in range(NX)]
    xs = P.sb("xs", [128, DM], BF16)
    hT2 = [P.sb("hT%d" % i, [128, 8, 128], BF16) for i in range(2)]
    sc2 = [P.sb("sc%d" % i, [128, 256], F32) for i in range(2)]
    stmp = P.sb("stmp", [128, 5, 128], F32)
    rtmp = P.sb("rtmp", [128, 4, 256], F32)
    qk2 = [P.sb("qk%d" % i, [128, 1024], BF16) for i in range(2)]
    v2 = [P.sb("v%d" % i, [128, 1024], BF16) for i in range(2)]
    sg2 = [P.sb("sg%d" % i, [128, 1024], BF16) for i in range(2)]
    kd = P.sb("kd", [128, 512], BF16)
    qdT = P.sb("qdT", [128, 4, 128], BF16)
    kdT = P.sb("kdT", [128, 4, 128], BF16)
    sT = P.sb("sT", [128, HP, 128], BF16)
    y = P.sb("y", [128, 1024], BF16)
    yT = P.sb("yT", [128, 8, 128], BF16)
    junko = P.sb("junko", [128, 512], BF16)
    osq = P.sb("osq", [128, 8], F32)

    class _V:
        def __init__(self, ap):
            self.t = ap
    tmpv = [_V(stmp.t[:, i, :]) for i in range(5)]

    P.dma("sp", gnT.t[:], D["rgn"][:, :], writes=[gnT])
    P.dma("sp", invr.t[:], bcast_rows(D["invf"], 128), writes=[invr])
    P.dma("sp", maskT.t[:], D["maskT"][:, h0 * 128:(h0 + HP) * 128], writes=[maskT])
    P.dma("sp", qdecr.t[:], bcast_rows(D["qdecr"][0:1, h0 * 128:(h0 + HP) * 128], 128), writes=[qdecr])
    P.dma("sp", kdecr.t[:], bcast_rows(D["kdecr"][0:1, h0 * 128:(h0 + HP) * 128], 128), writes=[kdecr])
    P.dma("sp", kdecc.t[:], D["kdecc"][:, :], writes=[kdecc])
    slabs = [(0, h0 * 256, 512), (512, 1024 + h0 * 256, 512), (1024, 2048 + h0 * 512, 1024), (2048, 4096 + h0 * 512, 1024)]
    for k in range(8):
        for (dst0, src0, wdt) in slabs:
            P.dma("pool", Win.t[:, k, dst0:dst0 + wdt], D["rwin"][k * 128:(k + 1) * 128, src0:src0 + wdt],
                  writes=[Win], max_dma_last_dim=4096)
    for ch in range(8):
        w = xt[ch % 2]
        cg = g * 8 + ch
        P.dma("sp", w.t[:], D["rwout"][cg * 128:(cg + 1) * 128, :], writes=[w])
        P.op("dve", lambda e, w=w, ch=ch, cg=cg: e.scalar_tensor_tensor(
            Wout.t[:, ch, :], w.t[:], gnT.t[:, cg:cg + 1], Grow.t[:], op0=ALU.mult, op1=ALU.mult),
            reads=[w, gnT, Grow], writes=[Wout])
    for i in range(2 * HP):
        P.op("pool", lambda e, i=i: e.memset(S32[i].t[:], 0.0), writes=[S32[i]])
        P.op("pool", lambda e, i=i: e.memset(Sbf[i].t[:], 0.0), writes=[Sbf[i]])

    pbP = [pb[0], pb[1]]
    pbT = [pb[2], pb[3]]
    pbS = pb[4]
    pbO = [pb[5], pb[6]]
    pbU = pb[7]
    x1buf = TBuf("x1dram")

    def s1_chunks(t):
        par = t % 2
        X = xt[t % NX]
        hT = hT2[par]
        sc = sc2[par]
        qk, v, sg = qk2[par], v2[par], sg2[par]
        sin_b = bc(sc.t[:, 0:128], 1, 2)
        cos_b = bc(sc.t[:, 128:256], 1, 2)

        def c0a():
            emit_norm_stage(P, t, par, X, D["x"][t * 128:(t + 1) * 128, :], ss[t % NX], xs, hT, AB, mh, pbT[0], identb, part="load")
            if not first:
                P.dma("sp", ot[t % NX].t[:], D["x1"][t * 128:(t + 1) * 128, :], reads=[x1buf], writes=[ot[t % NX]])

        def c0b():
            emit_norm_stage(P, t, par, X, None, ss[t % NX], xs, hT, AB, mh, pbT[0], identb, part="stat")

        def c0():
            emit_norm_stage(P, t, par, X, None, ss[t % NX], xs, hT, AB, mh, pbT[0], identb, part="tr")
            P.op("act", lambda e: e.activation(sc.t[:, 0:128], tmpv[3].t, AF.Sin), reads=[stmp], writes=[sc])
            P.op("act", lambda e: e.activation(sc.t[:, 128:256], tmpv[4].t, AF.Sin), reads=[stmp], writes=[sc])

        def csc():
            angv = tmpv[0]
            P.op("dve", lambda e: e.tensor_scalar(angv.t, invr.t[:], posT.t[:, t:t + 1], None, op0=ALU.mult),
                 reads=[invr, posT], writes=[stmp])
            emit_sincos_v(P, stmp, tmpv, sc, sin=False)

        def blk(n):
            bank = pbP[n % 2]
            for k in range(8):
                P.op("pe", lambda e, k=k: e.matmul(
                    bank.t[:], hT.t[:, k, :], Win.t[:, k, n * 512:(n + 1) * 512], start=(k == 0), stop=(k == 7)),
                    reads=[hT, Win], writes=[bank])
            if n < 2:
                pv4 = bank.t[:, :].rearrange("p (h t d) -> p h t d", h=2, t=2)
                t1, t2 = pv4[:, :, 0, :], pv4[:, :, 1, :]
                r = [rtmp.t[:, i, :].rearrange("p (h d) -> p h d", h=2) for i in range(4)]
                P.op("dve", lambda e: e.tensor_tensor(r[0], t1, cos_b, op=ALU.mult), reads=[bank, sc], writes=[rtmp])
                P.op("dve", lambda e: e.tensor_tensor(r[1], t2, sin_b, op=ALU.mult), reads=[bank, sc], writes=[rtmp])
                P.op("dve", lambda e: e.tensor_tensor(r[2], t1, sin_b, op=ALU.mult), reads=[bank, sc], writes=[rtmp])
                P.op("dve", lambda e: e.tensor_tensor(r[3], t2, cos_b, op=ALU.mult), reads=[bank, sc], writes=[rtmp])
                ov = qk.t[:, n * 512:(n + 1) * 512].rearrange("p (h t d) -> p h t d", h=2, t=2)
                P.op("pool", lambda e: e.tensor_tensor(ov[:, :, 0, :], r[0], r[1], op=ALU.subtract),
                     reads=[rtmp], writes=[qk])
                P.op("pool", lambda e: e.tensor_tensor(ov[:, :, 1, :], r[2], r[3], op=ALU.add),
                     reads=[rtmp], writes=[qk])
            elif n < 4:
                j = n - 2
                P.op("act", lambda e: e.activation(v.t[:, j * 512:(j + 1) * 512], bank.t[:], AF.Copy),
                     reads=[bank], writes=[v])
            else:
                j = n - 4
                P.op("act", lambda e: e.activation(sg.t[:, j * 512:(j + 1) * 512], bank.t[:], AF.Silu),
                     reads=[bank], writes=[sg])
        return [c0] + [(lambda n=n: blk(n)) for n in range(6)] + [c0a, c0b, csc]

    def s2_chunks(t):
        par = t % 2
        X = xt[t % NX]
        qk, v, sg = qk2[par], v2[par], sg2[par]

        def d0():
            for hh in range(HP):
                P.op("dve", lambda e, hh=hh: e.tensor_scalar(
                    kd.t[:, hh * 256:(hh + 1) * 256], qk.t[:, 512 + hh * 256:512 + (hh + 1) * 256],
                    kdecc.t[:, h0 + hh:h0 + hh + 1], None, op0=ALU.mult), reads=[qk, kdecc], writes=[kd])
            bank = pbT[1]
            pv = bank.t[:, :].bitcast(BF16)
            for j in range(8):
                P.op("pe", lambda e, j=j: e.transpose(
                    pv[:, j * 128:(j + 1) * 128], qk.t[:, j * 128:(j + 1) * 128], identb.t[:]),
                    reads=[qk, identb], writes=[bank])
            for which, (dst, dec) in enumerate(((qdT, qdecr), (kdT, kdecr))):
                decb = bc(dec.t[:, :].rearrange("p (h t) -> p h t", h=HP), 2, 2)
                src = pv[:, which * 512:(which + 1) * 512].rearrange("p (h c t) -> p h c t", h=HP, c=2)
                P.op("dve", lambda e, dst=dst, decb=decb, src=src: e.tensor_tensor(
                    dst.t[:, :, :].rearrange("p (h c) t -> p h c t", h=HP), src, decb, op=ALU.mult),
                    reads=[bank, dec], writes=[dst])

        def d1():
            for hh in range(HP):
                for ch in range(2):
                    P.op("pe", lambda e, hh=hh, ch=ch: e.matmul(
                        pbS.t[:, hh * 128:(hh + 1) * 128], kdT.t[:, 2 * hh + ch, :], qdT.t[:, 2 * hh + ch, :],
                        start=(ch == 0), stop=(ch == 1)), reads=[kdT, qdT], writes=[pbS])
            P.op("dve", lambda e: e.tensor_tensor(sT.t[:, :, :].rearrange("p a b -> p (a b)"), pbS.t[:, 0:HP * 128], maskT.t[:], op=ALU.mult),
                 reads=[pbS, maskT], writes=[sT])

        def head(hh):
            ob = pbO[hh % 2]
            vh = v.t[:, hh * 512:(hh + 1) * 512]
            P.op("pe", lambda e: e.matmul(ob.t[:], sT.t[:, hh, :], vh, start=True, stop=False),
                 reads=[sT, v], writes=[ob])
            for ch in range(2):
                i = 2 * hh + ch
                P.op("pe", lambda e, i=i, ch=ch: e.matmul(ob.t[:], qdT.t[:, i, :], Sbf[i].t[:], start=False, stop=(ch == 1)),
                     reads=[qdT, Sbf[i]], writes=[ob])
            P.op("act", lambda e: e.activation(junko.t[:], ob.t[:], AF.Square, accum_out=osq.t[:, hh:hh + 1]),
                 reads=[ob], writes=[osq])
            P.op("pool", lambda e: e.tensor_scalar(osq.t[:, 4 + hh:5 + hh], osq.t[:, hh:hh + 1], 1.0 / 512, EPS, op0=ALU.mult, op1=ALU.add),
                 reads=[osq], writes=[osq])
            P.op("pool", lambda e: e.tensor_tensor(osq.t[:, 4 + hh:5 + hh], osq.t[:, 4 + hh:5 + hh], mh.t[:, 0:1], op=ALU.pow),
                 reads=[osq, mh], writes=[osq])
            for ch in range(2):
                i = 2 * hh + ch
                ub = pbU if ch == 0 else pbT[0]
                P.op("pe", lambda e, ch=ch, ub=ub: e.matmul(
                    ub.t[:], kd.t[:, hh * 256 + ch * 128: hh * 256 + (ch + 1) * 128], vh, start=True, stop=True),
                    reads=[kd, v], writes=[ub])
                P.op("dve", lambda e, i=i, ub=ub: e.scalar_tensor_tensor(
                    S32[i].t[:], S32[i].t[:], gam128[h0 + hh], ub.t[:], op0=ALU.mult, op1=ALU.add),
                    reads=[S32[i], ub], writes=[S32[i]])
                P.op("act", lambda e, i=i: e.activation(Sbf[i].t[:], S32[i].t[:], AF.Copy), reads=[S32[i]], writes=[Sbf[i]])

        def ygate(hh):
            ob = pbO[hh % 2]
            P.op("dve", lambda e: e.scalar_tensor_tensor(
                y.t[:, hh * 512:(hh + 1) * 512], ob.t[:], osq.t[:, 4 + hh:5 + hh], sg.t[:, hh * 512:(hh + 1) * 512],
                op0=ALU.mult, op1=ALU.mult), reads=[ob, osq, sg], writes=[y])

        def ytr():
            bank = pbT[1]
            pv = bank.t[:, :].bitcast(BF16)
            for j in range(8):
                P.op("pe", lambda e, j=j: e.transpose(
                    pv[:, j * 128:(j + 1) * 128], y.t[:, j * 128:(j + 1) * 128], identb.t[:]),
                    reads=[y, identb], writes=[bank])
            P.op("act", lambda e: e.activation(yT.t[:, :, :].rearrange("p a b -> p (a b)"), pv, AF.Copy), reads=[bank], writes=[yT])

        def oproj(n):
            acc = X if first else ot[t % NX]
            bank = pbP[n]
            for ch in range(8):
                P.op("pe", lambda e, ch=ch: e.matmul(
                    bank.t[:], yT.t[:, ch, :], Wout.t[:, ch, n * 512:(n + 1) * 512], start=(ch == 0), stop=(ch == 7)),
                    reads=[yT, Wout], writes=[bank])
            P.op("dve", lambda e: e.tensor_tensor(
                acc.t[:, n * 512:(n + 1) * 512], bank.t[:], acc.t[:, n * 512:(n + 1) * 512], op=ALU.add),
                reads=[bank, acc], writes=[acc])
            if n == 1:
                P.dma("sp", D["x1"][t * 128:(t + 1) * 128, :], acc.t[:], reads=[acc], writes=[x1buf],
                      final=(final and g == 1))

        return [d0, d1, lambda: head(0), lambda: head(1), lambda: ygate(0), lambda: ygate(1),
                ytr, lambda: oproj(0), lambda: oproj(1)]

    order = [("d", 0), ("c", 1), ("p", 6), ("d", 1), ("c", 2), ("p", 7), ("b", 0), ("d", 2), ("p", 8), ("a", 0), ("s", 0), ("c", 3), ("d", 3),
             ("d", 4), ("c", 4), ("d", 5), ("e", 0), ("c", 5), ("c", 6)]
    s1 = {}
    for tt in range(min(3, NT)):
        s1[tt] = s1_chunks(tt)
        s1[tt][7]()
    for tt in range(min(2, NT)):
        s1[tt][8]()
        s1[tt][9]()
        s1[tt][0]()
    for f in s1[0][1:7]:
        f()
    prev = None
    for t in range(NT):
        dch = s2_chunks(t)
        cch = s1.get(t + 1)
        if t + 3 < NT:
            s1[t + 3] = s1_chunks(t + 3)
        for kind, i in order:
            if kind == "d":
                dch[i]()
            elif kind == "p":
                if prev is not None:
                    prev[i]()
            elif kind == "c":
                if cch is not None:
                    cch[i]()
            elif kind == "a":
                if t + 3 < NT:
                    s1[t + 3][7]()
            elif kind == "b":
                if t + 2 < NT:
                    s1[t + 2][8]()
            elif kind == "s":
                if t + 2 < NT:
                    s1[t + 2][9]()
            elif t + 2 < NT:
                s1[t + 2][0]()
        s1.pop(t, None)
        prev = dch
    for i in (6, 7, 8):
        prev[i]()


def emit_layer1(P, nc, NT, g, pb, identb, AB, Grow, mh, D):
    first = (g == 0)
    last = (g == NGRP - 1)
    W = HG * 128
    NR = 6
    Win = P.sb("Win1", [128, 8, 4 * W], BF16)
    Wout = P.sb("Wout1", [128, HG, DM], BF16)
    qkg = P.sb("qkg", [128, 2], F32)
    GG = P.sb("GG", [128, 1], F32)
    Bn = P.sb("Bn", [128, HG, 2, 128], F32)
    cfar = P.sb("cfar", [128, HG], F32)
    kTc = [P.sb("kTc%d" % i, [128, HG, 128], BF16) for i in range(NR)]
    vxc = [P.sb("vxc%d" % i, [128, HG, 130], BF16) for i in range(NR)]
    pT = [P.sb("pT%d" % i, [128, 5, 128], BF16) for i in range(2)]
    pTF = [TBuf("pTF%d" % i, pT[i].t) for i in range(2)]
    pTN = [TBuf("pTN%d" % i, pT[i].t) for i in range(2)]
    NX = 4
    xt = [P.sb("xt%d" % i, [128, DM], F32) for i in range(NX)]
    ot = [P.sb("ot%d" % i, [128, DM], F32) for i in range(NX)] if not first else None
    ss = [P.sb("ss%d" % i, [128, 8], F32) for i in range(NX)]
    xs2 = [P.sb("xs%d" % i, [128, DM], BF16) for i in range(2)]
    hT2 = [P.sb("hT%d" % i, [128, 8, 128], BF16) for i in range(2)]
    qkr = P.sb("qkr", [128, 2 * W], BF16)
    qkn = P.sb("qkn", [128, 2 * W], BF16)
    sq = [P.sb("sq%d" % i, [128, 2 * HG], F32) for i in range(2)]
    rq = [P.sb("rq%d" % i, [128, 2 * HG], F32) for i in range(2)]
    th = P.sb("th", [128, 512], F32)
    u = [P.sb("u%d" % i, [128, W], BF16) for i in range(2)]
    qT = [P.sb("qT%d" % i, [128, HG, 128], BF16) for i in range(2)]
    stmp = [P.sb("stmp%d" % i, [128, 256], F32) for i in range(2)]
    rinv = P.sb("rinv", [128, HG], F32)
    y = P.sb("y", [128, W], BF16)
    yT = P.sb("yT", [128, HG, 128], BF16)
    sqv = [P.sb("sqv%d" % i, [128, 512], F32) for i in range(2)]
    qkrb = [TBuf("qkr%d" % i, qkr.t) for i in range(4)]

    for k in range(8):
        for j in range(4):
            P.dma("pool", Win.t[:, k, j * W:(j + 1) * W],
                  D["awin"][k * 128:(k + 1) * 128, j * 2048 + g * W: j * 2048 + (g + 1) * W],
                  writes=[Win], max_dma_last_dim=4096)
    for h in range(HG):
        w = xt[h % 2]
        r0 = (g * HG + h) * 128
        P.dma("sp", w.t[:], D["awout"][r0:r0 + 128, :], writes=[w])
        P.op("dve", lambda e, w=w, h=h: e.scalar_tensor_tensor(
            Wout.t[:, h, :], w.t[:], 0.5, Grow.t[:], op0=ALU.mult, op1=ALU.mult), reads=[w, Grow], writes=[Wout])
    P.dma("sp", qkg.t[:], D["aqk"][:, :], writes=[qkg])
    P.op("dve", lambda e: e.scalar_tensor_tensor(GG.t[:], qkg.t[:, 0:1], 128.0 ** -0.5, qkg.t[:, 1:2], op0=ALU.mult, op1=ALU.mult),
         reads=[qkg], writes=[GG])
    P.dma("sp", Bn.t[:], D["abias"][g * HG:(g + 1) * HG].rearrange("h b p c -> p h b c"), writes=[Bn])
    P.dma("sp", cfar.t[:], bcast_rows(D["acfar"][0:1, g * HG:(g + 1) * HG], 128), writes=[cfar])
    for i in range(NR):
        P.op("pool", lambda e, i=i: e.memset(vxc[i].t[:, :, 128:130], 1.0), writes=[vxc[i]])
    for i in range(2):
        P.op("pool", lambda e, i=i: e.memset(pT[i].t[:], 0.0), writes=[pTF[i], pTN[i]])

    pbP = [pb[0], pb[1]]
    pbT = pb[2]
    pbF = [pb[3], pb[4]]
    pbN = [TBuf("pbN0", pb[5].t), TBuf("pbN1", pb[5].t)]
    pbV = [pb[6], pb[7]]
    outbuf = TBuf("outdram")
    src_d = D["x1"]
    pv = pbT.t[:, :].bitcast(BF16)

    def s1_chunks(t):
        par = t % 2
        X = xt[t % NX]
        hT = hT2[par]
        slot = t % NR
        SQ, RQ, U, QT = sq[par], rq[par], u[par], qT[par]

        def c0a():
            emit_norm_stage(P, t, par, X, src_d[t * 128:(t + 1) * 128, :], ss[t % NX], xs2[par], hT, AB, mh, pbT, identb, part="load")
            if not first:
                P.dma("sp", ot[t % NX].t[:], D["out"][t * 128:(t + 1) * 128, :], reads=[outbuf], writes=[ot[t % NX]])

        def c0b():
            emit_norm_stage(P, t, par, X, None, ss[t % NX], xs2[par], hT, AB, mh, pbT, identb, part="stat")

        def c0():
            emit_norm_stage(P, t, par, X, None, ss[t % NX], xs2[par], hT, AB, mh, pbT, identb, part="tr")

        def blk(n):
            bank = pbP[n % 2]
            for k in range(8):
                P.op("pe", lambda e, k=k: e.matmul(
                    bank.t[:], hT.t[:, k, :], Win.t[:, k, n * 512:(n + 1) * 512], start=(k == 0), stop=(k == 7)),
                    reads=[hT, Win], writes=[bank])
            if 1 <= n <= 4:
                m = n - 1
                svm = sqv[m % 2]
                P.op("dve", lambda e: e.tensor_reduce(SQ.t[:, m * 4:(m + 1) * 4], svm.t[:, :].rearrange("p (h d) -> p h d", h=4),
                                                      op=ALU.add, axis=mybir.AxisListType.X),
                     reads=[svm], writes=[SQ])
            if n < 4:
                qb = qkrb[n]
                P.op("dve", lambda e: e.tensor_copy(qkr.t[:, n * 512:(n + 1) * 512], bank.t[:]),
                     reads=[bank], writes=[qb])
                sv = sqv[n % 2]
                P.op("pool", lambda e: e.tensor_tensor(sv.t[:], qkr.t[:, n * 512:(n + 1) * 512], qkr.t[:, n * 512:(n + 1) * 512], op=ALU.mult),
                     reads=[qb], writes=[sv])
            elif n < 6:
                j = n - 4
                P.op("act", lambda e: e.activation(
                    vxc[slot].t[:, j * 4:(j + 1) * 4, 0:128], bank.t[:, :].rearrange("p (h d) -> p h d", h=4), AF.Copy),
                    reads=[bank], writes=[vxc[slot]])
            else:
                j = n - 6
                P.op("act", lambda e: e.activation(th.t[:], bank.t[:], AF.Tanh, scale=0.5), reads=[bank], writes=[th])
                P.op("dve", lambda e: e.scalar_tensor_tensor(
                    U.t[:, j * 512:(j + 1) * 512], th.t[:], 1.0, bank.t[:], op0=ALU.add, op1=ALU.mult),
                    reads=[th, bank], writes=[U])

        def c9a():
            P.op("pool", lambda e: e.tensor_scalar(RQ.t[:], SQ.t[:], 1.0 / 128, EPS, op0=ALU.mult, op1=ALU.add), reads=[SQ], writes=[RQ])
            P.op("pool", lambda e: e.tensor_tensor(RQ.t[:], RQ.t[:], mh.t[:, 0:16], op=ALU.pow), reads=[RQ, mh], writes=[RQ])

        def c9d():
            for hf in range(2):
                P.op("pool", lambda e, hf=hf: e.tensor_tensor(
                    qkn.t[:, hf * W:(hf + 1) * W].rearrange("p (h d) -> p h d", h=HG),
                    qkr.t[:, hf * W:(hf + 1) * W].rearrange("p (h d) -> p h d", h=HG),
                    bc(RQ.t[:, hf * HG:(hf + 1) * HG], 2, 128), op=ALU.mult), reads=[qkrb[2 * hf], qkrb[2 * hf + 1], RQ], writes=[qkn])

        def c9b():
            for j in range(HG):
                P.op("pe", lambda e, j=j: e.transpose(pv[:, j * 128:(j + 1) * 128], qkn.t[:, j * 128:(j + 1) * 128], identb.t[:]),
                     reads=[qkn, identb], writes=[pbT])
            P.op("dve", lambda e: e.tensor_scalar(QT.t[:, :, :].rearrange("p a b -> p (a b)"), pv, GG.t[:, 0:1], None, op0=ALU.mult),
                 reads=[pbT, GG], writes=[QT])

        def c10k():
            for j in range(HG):
                P.op("pe", lambda e, j=j: e.transpose(pv[:, j * 128:(j + 1) * 128], qkn.t[:, W + j * 128: W + (j + 1) * 128], identb.t[:]),
                     reads=[qkn, identb], writes=[pbT])
            P.op("dve", lambda e: e.tensor_copy(kTc[slot].t[:, :, :].rearrange("p a b -> p (a b)"), pv),
                 reads=[pbT], writes=[kTc[slot]])

        return [c0] + [(lambda n=n: blk(n)) for n in range(8)] + [c9a, c9b, c0a, c0b, c9d, c10k]

    def s2_chunks(t):
        par = t % 2
        X = xt[t % NX]
        U, QT = u[par], qT[par]
        blocks = [i for i in range(5) if t - 4 + i >= 0]
        far = [i for i in blocks if i < 3]

        def scores(h):
            hp = h % 2
            fb, nb = pbF[hp], pbN[hp]
            for i in far:
                sl = (t - 4 + i) % NR
                P.op("pe", lambda e, i=i, sl=sl: e.matmul(
                    fb.t[:, i * 128:(i + 1) * 128], kTc[sl].t[:, h, :], QT.t[:, h, :], start=True, stop=True),
                    reads=[kTc[sl], QT], writes=[fb])
            for i in blocks:
                if i < 3:
                    continue
                sl = (t - 4 + i) % NR
                nc0 = hp * 256 + (i - 3) * 128
                P.op("pe", lambda e, i=i, sl=sl, nc0=nc0: e.matmul(
                    nb.t[:, nc0:nc0 + 128], kTc[sl].t[:, h, :], QT.t[:, h, :], start=True, stop=True),
                    reads=[kTc[sl], QT], writes=[nb])

        def expo(h):
            hp = h % 2
            fb, nb, pt = pbF[hp], pbN[hp], pT[hp]
            if far:
                f0 = far[0]
                cf_ = f0 * 128
                P.op("act", lambda e: e.activation(
                    pt.t[:, f0:3, :].rearrange("p a b -> p (a b)"), fb.t[:, cf_:384], AF.Exp, bias=cfar.t[:, h:h + 1]),
                    reads=[fb, cfar], writes=[pTF[hp]])
                if f0 == 0:
                    P.op("dve", lambda e: e.memset(pt.t[0:64, 0, 64:128], 0.0), writes=[pTF[hp]])
            nn = [i for i in blocks if i >= 3]
            n0 = nn[0] - 3
            st = stmp[hp]
            P.op("dve", lambda e: e.tensor_tensor(
                st.t[:, n0 * 128:256], nb.t[:, hp * 256 + n0 * 128:hp * 256 + 256], Bn.t[:, h, n0:2, :].rearrange("p a b -> p (a b)"), op=ALU.add),
                reads=[nb, Bn], writes=[st])
            P.op("act", lambda e: e.activation(
                pt.t[:, 3 + n0:5, :].rearrange("p a b -> p (a b)"), st.t[:, n0 * 128:256], AF.Exp), reads=[st], writes=[pTN[hp]])
        def pvh(h):
            hp = h % 2
            pt = pT[hp]
            vs = pbV[hp]
            for bi, i in enumerate(blocks):
                sl = (t - 4 + i) % NR
                P.op("pe", lambda e, i=i, sl=sl, bi=bi: e.matmul(
                    vs.t[:, 0:129], pt.t[:, i, :], vxc[sl].t[:, h, 0:129], start=(bi == 0), stop=(bi == len(blocks) - 1)),
                    reads=[pTF[hp], pTN[hp], vxc[sl]], writes=[vs])
            P.op("dve", lambda e: e.reciprocal(rinv.t[:, h:h + 1], vs.t[:, 128:129]), reads=[vs], writes=[rinv])
            P.op("dve", lambda e: e.scalar_tensor_tensor(
                y.t[:, h * 128:(h + 1) * 128], vs.t[:, 0:128], rinv.t[:, h:h + 1], U.t[:, h * 128:(h + 1) * 128],
                op0=ALU.mult, op1=ALU.mult), reads=[vs, rinv, U], writes=[y])

        def ytr():
            for j in range(HG):
                P.op("pe", lambda e, j=j: e.transpose(pv[:, j * 128:(j + 1) * 128], y.t[:, j * 128:(j + 1) * 128], identb.t[:]),
                     reads=[y, identb], writes=[pbT])
            P.op("act", lambda e: e.activation(yT.t[:, :, :].rearrange("p a b -> p (a b)"), pv, AF.Copy), reads=[pbT], writes=[yT])

        def oproj(n):
            acc = X if first else ot[t % NX]
            bank = pbP[n]
            for ch in range(HG):
                P.op("pe", lambda e, ch=ch: e.matmul(
                    bank.t[:], yT.t[:, ch, :], Wout.t[:, ch, n * 512:(n + 1) * 512], start=(ch == 0), stop=(ch == HG - 1)),
                    reads=[yT, Wout], writes=[bank])
            P.op("dve", lambda e: e.tensor_tensor(
                acc.t[:, n * 512:(n + 1) * 512], bank.t[:], acc.t[:, n * 512:(n + 1) * 512], op=ALU.add),
                reads=[bank, acc], writes=[acc])
            if n == 1:
                P.dma("sp", D["out"][t * 128:(t + 1) * 128, :], acc.t[:], reads=[acc], writes=[outbuf], final=last)

        return [(lambda h=h: scores(h)) for h in range(HG)] + [(lambda h=h: expo(h)) for h in range(HG)] + \
               [(lambda h=h: pvh(h)) for h in range(HG)] + [ytr, lambda: oproj(0), lambda: oproj(1)]

    order = [("d", 0), ("c", 1), ("p", 24), ("d", 1), ("p", 25), ("b", 0),
             ("d", 8), ("c", 2), ("d", 2), ("d", 16), ("p", 26), ("a", 0),
             ("d", 9), ("c", 3), ("d", 3), ("d", 17),
             ("d", 10), ("c", 4), ("d", 4), ("d", 18),
             ("d", 11), ("c", 5), ("c", 9), ("d", 5), ("d", 19),
             ("d", 12), ("c", 6), ("d", 6), ("d", 20),
             ("d", 13), ("c", 7), ("c", 13), ("d", 7), ("d", 21), ("e", 0),
             ("d", 14), ("c", 8), ("d", 22), ("c", 10),
             ("d", 15), ("c", 14), ("d", 23)]
    s1 = {}
    for tt in range(min(3, NT)):
        s1[tt] = s1_chunks(tt)
        s1[tt][11]()
    for tt in range(min(2, NT)):
        s1[tt][12]()
        s1[tt][0]()
    for i in (1, 2, 3, 4, 5, 6, 7, 8, 9, 13, 10, 14):
        s1[0][i]()
    prev = None
    for t in range(NT):
        dch = s2_chunks(t)
        cch = s1.get(t + 1)
        if t + 3 < NT:
            s1[t + 3] = s1_chunks(t + 3)
        for kind, i in order:
            if kind == "d":
                dch[i]()
            elif kind == "p":
                if prev is not None:
                    prev[i]()
            elif kind == "c":
                if cch is not None:
                    cch[i]()
            elif kind == "a":
                if t + 3 < NT:
                    s1[t + 3][11]()
            elif kind == "b":
                if t + 2 < NT:
                    s1[t + 2][12]()
            elif t + 2 < NT:
                s1[t + 2][0]()
        s1.pop(t, None)
        prev = dch
    for i in (24, 25, 26):
        prev[i]()


def _lay(inputs, b, NT):
    S = NT * 128
    hc = host_consts()
    f = lambda a: np.ascontiguousarray(a, dtype=np.float32)
    d = {
        "x": f(inputs["x"][b, :S]),
        "cT": f(inputs["c"][b].reshape(8, 128).T),
        "pos": np.ascontiguousarray(inputs["positions"][b, :S].reshape(NT, 128).astype(np.int32)),
        "norm_g": f(inputs["norm_g"]),
        "ada_w": f(inputs["ada_w"]),
        "ada_b": f(inputs["ada_b"]),
        "ret_w_in": f(inputs["ret_w_in"][0]),
        "ret_gnT": f(inputs["ret_gn_g"][0].reshape(16, 128).T),
        "ret_w_out": f(inputs["ret_w_out"][0]),
        "att_w_in": f(inputs["att_w_in"][0]),
        "att_qk_g": f(np.stack([inputs["att_q_g"][0], inputs["att_k_g"][0]], axis=1)),
        "att_bias_tiles": attn_bias_tiles(np.asarray(inputs["att_rel_bias"][0], np.float32)),
        "att_cfar": f(inputs["att_rel_bias"][0][:, 256].reshape(1, AH)),
        "att_w_out": f(inputs["att_w_out"][0]),
    }
    for k in ("identb", "identf", "invf", "maskT", "qdec_row", "kdec_row", "kdec_col"):
        d[k] = hc[k]
    return d


def run(inputs, NT=64, ncores=8, do_l0=True, do_l1=True):
    inputs = {k: np.asarray(v) for k, v in inputs.items()}
    nc = build_program(NT, do_l0, do_l1)
    in_maps = [_lay(inputs, b, NT) for b in range(ncores)]
    res = run_bass_kernel_spmd(nc, in_maps, core_ids=list(range(ncores)))
    return np.stack([np.asarray(r["out"]) for r in res.results], axis=0)


def kernel(**inputs):
    return run(inputs, NT=64, ncores=8)
```
